# Optimizing a Trainium2 kernel written in Bass

```python
import math
import jax, jax.numpy as jnp
from jax import lax
import numpy as np

D_MODEL = 1024
BATCH = 8
SEQ = 4096
DEPTH = 2

GRID_W = 64
CTX_LEN = 256
N_BRANCH = 4
BRANCH_W = D_MODEL // 4
HEAD_DIM = 64
NA_HEADS = BRANCH_W // HEAD_DIM
WIN_H = 8
WIN_W = 16
MLA_HEADS = 4
MLA_NOPE = 64
MLA_ROPE = 32
MLA_V = BRANCH_W // MLA_HEADS
MLA_Q_LORA = 256
MLA_KV_LORA = 128
LRU_W = BRANCH_W
LRU_BLOCKS = 4
LRU_BW = LRU_W // LRU_BLOCKS
CONV_W = 4
LRU_C = 8.0
GQA_HEADS = BRANCH_W // HEAD_DIM
GQA_KV_HEADS = 2
ROPE_THETA = 10000.0
Q_BLOCK = 128
EPS = 1e-6
NA_SCALE = HEAD_DIM ** -0.5
MLA_SCALE = (MLA_NOPE + MLA_ROPE) ** -0.5
GQA_SCALE = HEAD_DIM ** -0.5
DEEPNORM_ALPHA = (2 * DEPTH) ** 0.25
DEEPNORM_BETA = (8 * DEPTH) ** -0.25

MIX_SPLITS = (BRANCH_W, BRANCH_W, BRANCH_W,
              MLA_Q_LORA, MLA_KV_LORA, MLA_ROPE,
              LRU_W,
              GQA_HEADS * HEAD_DIM, GQA_KV_HEADS * HEAD_DIM, GQA_KV_HEADS * HEAD_DIM)
MIX_COLS = sum(MIX_SPLITS)
SILU_COLS = N_BRANCH * BRANCH_W
MERGE_COLS = N_BRANCH * D_MODEL
N_IN = MIX_COLS + SILU_COLS + MERGE_COLS

kernel_name = "hybrid_na_mla_rglru_gqa_prefix_block"


def layer_norm(x, eps=EPS):
    xf = x.astype(jnp.float32)
    mu = jnp.mean(xf, -1, keepdims=True)
    var = jnp.mean(jnp.square(xf - mu), -1, keepdims=True)
    return ((xf - mu) * lax.rsqrt(var + eps)).astype(x.dtype)


def rms_norm(x, g, eps=EPS):
    xf = x.astype(jnp.float32)
    y = xf * lax.rsqrt(jnp.mean(jnp.square(xf), -1, keepdims=True) + eps)
    return (y * g.astype(jnp.float32)).astype(x.dtype)


def rope_1d(x, pos):
    d = x.shape[-1]
    inv = ROPE_THETA ** (-jnp.arange(0, d, 2, dtype=jnp.float32) / d)
    ang = pos[:, None] * inv[None, :]
    cos, sin = jnp.cos(ang), jnp.sin(ang)
    xf = x.astype(jnp.float32)
    x1, x2 = xf[..., : d // 2], xf[..., d // 2:]
    return jnp.concatenate([x1 * cos - x2 * sin, x2 * cos + x1 * sin], -1).astype(x.dtype)


def axial_rope(x, rows, cols):
    half = x.shape[-1] // 2
    return jnp.concatenate([rope_1d(x[..., :half], rows), rope_1d(x[..., half:], cols)], -1)


def split_cols(p, sizes):
    out, start = [], 0
    for size in sizes:
        out.append(p[..., start:start + size])
        start += size
    return out


def split_heads(t, n_heads):
    b, n, _ = t.shape
    return t.reshape(b, n, n_heads, -1).transpose(0, 2, 1, 3)


def merge_heads(t):
    b, h, n, d = t.shape
    return t.transpose(0, 2, 1, 3).reshape(b, n, h * d)


def attend(q, keys, vals, scale):
    b, hq, nq, dk = q.shape
    hk = keys.shape[1]
    qg = q.reshape(b, hk, hq // hk, nq, dk)
    s = jnp.einsum('bkgqd,bknd->bkgqn', qg, keys, preferred_element_type=jnp.float32) * scale
    p = jax.nn.softmax(s, axis=-1).astype(vals.dtype)
    o = jnp.einsum('bkgqn,bknd->bkgqd', p, vals)
    return o.reshape(b, hq, nq, vals.shape[-1])


def prefix_attention(q, k, v, k_ctx, v_ctx, scale):
    b, hq, n, dk = q.shape
    keys = jnp.concatenate([k_ctx, k], axis=2)
    vals = jnp.concatenate([v_ctx, v], axis=2)
    nb = n // Q_BLOCK
    qb = q.reshape(b, hq, nb, Q_BLOCK, dk).transpose(2, 0, 1, 3, 4)
    out = lax.map(lambda qi: attend(qi, keys, vals, scale), qb)
    return out.transpose(1, 2, 0, 3, 4).reshape(b, hq, n, vals.shape[-1])


def neighbourhood_attention(q, k, v, k_ctx, v_ctx, rel_bias):
    b, h, n, hd = q.shape
    n_rows = n // GRID_W
    wh, ww = min(WIN_H, n_rows), WIN_W
    qg = q.reshape(b, h, n_rows, GRID_W, hd)
    kg = k.reshape(b, h, n_rows, GRID_W, hd)
    vg = v.reshape(b, h, n_rows, GRID_W, hd)
    col_start = np.clip(np.arange(GRID_W) - ww // 2, 0, GRID_W - ww)
    col_idx = col_start[:, None] + np.arange(ww)[None, :]
    dc = col_idx - np.arange(GRID_W)[:, None]
    row_start = np.clip(np.arange(n_rows) - wh // 2, 0, n_rows - wh)
    dr = row_start[:, None] + np.arange(wh)[None, :] - np.arange(n_rows)[:, None]
    bias = rel_bias[:, (dr + WIN_H - 1)[:, :, None, None], (dc + WIN_W - 1)[None, None, :, :]]
    bias = bias.transpose(1, 0, 3, 2, 4).reshape(n_rows, h, GRID_W, wh * ww)

    def row_fn(args):
        qr, rs, bias_r = args
        k_win = jnp.take(lax.dynamic_slice_in_dim(kg, rs, wh, axis=2), col_idx, axis=3)
        v_win = jnp.take(lax.dynamic_slice_in_dim(vg, rs, wh, axis=2), col_idx, axis=3)
        s_win = jnp.einsum('bhqd,bhiqjd->bhqij', qr, k_win, preferred_element_type=jnp.float32)
        s_win = s_win.reshape(b, h, GRID_W, wh * ww) * NA_SCALE + bias_r.astype(jnp.float32)
        s_ctx = jnp.einsum('bhqd,bhnd->bhqn', qr, k_ctx, preferred_element_type=jnp.float32) * NA_SCALE
        p = jax.nn.softmax(jnp.concatenate([s_win, s_ctx], -1), axis=-1).astype(v.dtype)
        p_win = p[..., :wh * ww].reshape(b, h, GRID_W, wh, ww)
        return (jnp.einsum('bhqij,bhiqjd->bhqd', p_win, v_win)
                + jnp.einsum('bhqn,bhnd->bhqd', p[..., wh * ww:], v_ctx))

    out = lax.map(row_fn, (qg.transpose(2, 0, 1, 3, 4), jnp.asarray(row_start, jnp.int32), bias))
    return out.transpose(1, 2, 0, 3, 4).reshape(b, h, n, hd)


def mla_project(cq, ckv, krope, q_norm, w_uq, kv_norm, w_ukv, rows, cols):
    b, n, _ = cq.shape
    q = (rms_norm(cq, q_norm) @ w_uq).reshape(b, n, MLA_HEADS, MLA_NOPE + MLA_ROPE).transpose(0, 2, 1, 3)
    kv = (rms_norm(ckv, kv_norm) @ w_ukv).reshape(b, n, MLA_HEADS, MLA_NOPE + MLA_V).transpose(0, 2, 1, 3)
    q_nope, q_rope = q[..., :MLA_NOPE], q[..., MLA_NOPE:]
    k_nope, v = kv[..., :MLA_NOPE], kv[..., MLA_NOPE:]
    if rows is not None:
        q_rope = axial_rope(q_rope, rows, cols)
        krope = axial_rope(krope, rows, cols)
    k_rope = jnp.broadcast_to(krope[:, None], (b, MLA_HEADS, n, MLA_ROPE))
    return (jnp.concatenate([q_nope, q_rope], -1), jnp.concatenate([k_nope, k_rope], -1), v)


def gqa_project(q, k, v, q_norm, k_norm, rows, cols):
    q = rms_norm(split_heads(q, GQA_HEADS), q_norm)
    k = rms_norm(split_heads(k, GQA_KV_HEADS), k_norm)
    v = split_heads(v, GQA_KV_HEADS)
    if rows is not None:
        q = axial_rope(q, rows, cols)
        k = axial_rope(k, rows, cols)
    return q, k, v


def centred_depthwise_conv(x, w, bias):
    pad_lo = CONV_W // 2
    y = lax.conv_general_dilated(x, w[:, None, :], window_strides=(1,),
                                 padding=[(pad_lo, CONV_W - 1 - pad_lo)],
                                 dimension_numbers=('NWC', 'WIO', 'NWC'),
                                 feature_group_count=x.shape[-1])
    return y + bias


def rglru_coeffs(x, w_a, b_a, w_x, b_x, lam):
    b, n, _ = x.shape
    xf = x.astype(jnp.float32)
    xb = xf.reshape(b, n, LRU_BLOCKS, LRU_BW)
    r = jax.nn.sigmoid(jnp.einsum('bnkc,kcd->bnkd', xb, w_a.astype(jnp.float32)).reshape(b, n, LRU_W) + b_a)
    i = jax.nn.sigmoid(jnp.einsum('bnkc,kcd->bnkd', xb, w_x.astype(jnp.float32)).reshape(b, n, LRU_W) + b_x)
    log_a = -LRU_C * r * jax.nn.softplus(-lam.astype(jnp.float32))
    a = jnp.exp(log_a)
    u = jnp.sqrt(-jnp.expm1(2.0 * log_a)) * (i * xf)
    return a, u


def linear_scan(a, u, h0, reverse):
    idx = -1 if reverse else 0
    u = u.at[:, idx].add(a[:, idx] * h0)

    def combine(lhs, rhs):
        a1, u1 = lhs
        a2, u2 = rhs
        return a1 * a2, a2 * u1 + u2

    _, h = lax.associative_scan(combine, (a, u), reverse=reverse, axis=1)
    return h


def rglru_mixer(x_lat, x_ctx, conv_w, conv_b, w_a, b_a, w_x, b_x, lam, need_ctx):
    xc = centred_depthwise_conv(x_ctx, conv_w, conv_b)
    xl = centred_depthwise_conv(x_lat, conv_w, conv_b)
    h_lat, h_ctx = [], []
    for d in range(2):
        reverse = d == 1
        end = 0 if reverse else -1
        a, u = rglru_coeffs(xc, w_a[d], b_a[d], w_x[d], b_x[d], lam[d])
        hc = linear_scan(a, u, jnp.zeros_like(a[:, 0]), reverse)
        a, u = rglru_coeffs(xl, w_a[d], b_a[d], w_x[d], b_x[d], lam[d])
        h_lat.append(linear_scan(a, u, hc[:, end], reverse))
        h_ctx.append(hc)
    y_lat = (h_lat[0] + h_lat[1]).astype(x_lat.dtype)
    y_ctx = (h_ctx[0] + h_ctx[1]).astype(x_ctx.dtype) if need_ctx else None
    return y_lat, y_ctx


def modulation(cond, w_mod, b_mod):
    mod = jax.nn.silu(cond) @ w_mod + b_mod
    return jnp.split(mod, 3, axis=-1)


def merge_branches(ys, z, m, w_branch, w_out):
    zs = jnp.split(z, N_BRANCH, axis=-1)
    ms = jnp.split(m, N_BRANCH, axis=-1)
    acc = sum(jax.nn.sigmoid(ms[i]) * ((ys[i] * jax.nn.silu(zs[i])) @ w_branch[i]) for i in range(N_BRANCH))
    return acc @ w_out


def hybrid_layer(x, ctx, c, c_ctx, w_mod, b_mod, w_in, na_rel_bias, mla_q_norm, mla_w_uq,
                 mla_kv_norm, mla_w_ukv, lru_conv_w, lru_conv_b, lru_w_a, lru_b_a, lru_w_x,
                 lru_b_x, lru_lambda, gqa_q_norm, gqa_k_norm, w_branch, w_out, ln_g, ln_b, need_ctx):
    n = x.shape[1]
    t = jnp.arange(n, dtype=jnp.int32)
    rows = (t // GRID_W).astype(jnp.float32)
    cols = (t % GRID_W).astype(jnp.float32)

    shift, scale, gate = modulation(c[:, None, :], w_mod, b_mod)
    shift_c, scale_c, gate_c = modulation(c_ctx, w_mod, b_mod)
    p_lat = (layer_norm(x) * (1.0 + scale) + shift) @ w_in
    w_in_ctx = w_in if need_ctx else w_in[:, :MIX_COLS]
    p_ctx = (layer_norm(ctx) * (1.0 + scale_c) + shift_c) @ w_in_ctx
    lat = split_cols(p_lat[..., :MIX_COLS], MIX_SPLITS)
    cx = split_cols(p_ctx[..., :MIX_COLS], MIX_SPLITS)

    q, k, v = (split_heads(tt, NA_HEADS) for tt in lat[0:3])
    qc, kc, vc = (split_heads(tt, NA_HEADS) for tt in cx[0:3])
    ya = merge_heads(neighbourhood_attention(q, k, v, kc, vc, na_rel_bias))
    ya_c = merge_heads(attend(qc, kc, vc, NA_SCALE)) if need_ctx else None

    q, k, v = mla_project(lat[3], lat[4], lat[5], mla_q_norm, mla_w_uq, mla_kv_norm, mla_w_ukv, rows, cols)
    qc, kc, vc = mla_project(cx[3], cx[4], cx[5], mla_q_norm, mla_w_uq, mla_kv_norm, mla_w_ukv, None, None)
    yb = merge_heads(prefix_attention(q, k, v, kc, vc, MLA_SCALE))
    yb_c = merge_heads(attend(qc, kc, vc, MLA_SCALE)) if need_ctx else None

    yc, yc_c = rglru_mixer(lat[6], cx[6], lru_conv_w, lru_conv_b, lru_w_a, lru_b_a,
                           lru_w_x, lru_b_x, lru_lambda, need_ctx)

    q, k, v = gqa_project(lat[7], lat[8], lat[9], gqa_q_norm, gqa_k_norm, rows, cols)
    qc, kc, vc = gqa_project(cx[7], cx[8], cx[9], gqa_q_norm, gqa_k_norm, None, None)
    yd = merge_heads(prefix_attention(q, k, v, kc, vc, GQA_SCALE))
    yd_c = merge_heads(attend(qc, kc, vc, GQA_SCALE)) if need_ctx else None

    z_lat = p_lat[..., MIX_COLS:MIX_COLS + SILU_COLS]
    m_lat = p_lat[..., MIX_COLS + SILU_COLS:]
    out = merge_branches([ya, yb, yc, yd], z_lat, m_lat, w_branch, w_out)
    x_new = layer_norm(DEEPNORM_ALPHA * x + gate * out) * ln_g + ln_b
    if need_ctx:
        z_c = p_ctx[..., MIX_COLS:MIX_COLS + SILU_COLS]
        m_c = p_ctx[..., MIX_COLS + SILU_COLS:]
        out_c = merge_branches([ya_c, yb_c, yc_c, yd_c], z_c, m_c, w_branch, w_out)
        ctx = layer_norm(DEEPNORM_ALPHA * ctx + gate_c * out_c) * ln_g + ln_b
    return x_new, ctx


def setup_inputs(seed: int = 0) -> dict:
    key = jax.random.key(seed)
    ks = jax.random.split(key, 32)
    f32 = jnp.float32
    L = DEPTH

    def nrm(k, shape, s):
        return jax.random.normal(k, shape, f32) * s

    lam_u = jax.random.uniform(ks[17], (L, 2, LRU_W), f32, 0.9, 0.999)
    a_base = lam_u ** (1.0 / LRU_C)
    lru_lambda = jnp.log(a_base) - jnp.log1p(-a_base)
    return {
        "x": nrm(ks[0], (BATCH, SEQ, D_MODEL), 1.0),
        "c": nrm(ks[1], (BATCH, D_MODEL), 1.0),
        "ctx": nrm(ks[2], (BATCH, CTX_LEN, D_MODEL), 1.0),
        "c_ctx": nrm(ks[3], (D_MODEL,), 1.0),
        "w_mod": nrm(ks[4], (L, D_MODEL, 3 * D_MODEL), 0.5 * D_MODEL ** -0.5),
        "b_mod": nrm(ks[5], (L, 3 * D_MODEL), 0.01),
        "w_in": nrm(ks[6], (L, D_MODEL, N_IN), D_MODEL ** -0.5),
        "na_rel_bias": nrm(ks[7], (L, NA_HEADS, 2 * WIN_H - 1, 2 * WIN_W - 1), 0.1),
        "mla_q_norm": 1.0 + nrm(ks[8], (L, MLA_Q_LORA), 0.01),
        "mla_w_uq": nrm(ks[9], (L, MLA_Q_LORA, MLA_HEADS * (MLA_NOPE + MLA_ROPE)), MLA_Q_LORA ** -0.5),
        "mla_kv_norm": 1.0 + nrm(ks[10], (L, MLA_KV_LORA), 0.01),
        "mla_w_ukv": nrm(ks[11], (L, MLA_KV_LORA, MLA_HEADS * (MLA_NOPE + MLA_V)), MLA_KV_LORA ** -0.5),
        "lru_conv_w": nrm(ks[12], (L, CONV_W, LRU_W), CONV_W ** -0.5),
        "lru_conv_b": nrm(ks[13], (L, LRU_W), 0.01),
        "lru_w_a": nrm(ks[14], (L, 2, LRU_BLOCKS, LRU_BW, LRU_BW), LRU_BW ** -0.5),
        "lru_b_a": nrm(ks[15], (L, 2, LRU_W), 0.01),
        "lru_w_x": nrm(ks[16], (L, 2, LRU_BLOCKS, LRU_BW, LRU_BW), LRU_BW ** -0.5),
        "lru_b_x": nrm(ks[18], (L, 2, LRU_W), 0.01),
        "lru_lambda": lru_lambda,
        "gqa_q_norm": 1.0 + nrm(ks[19], (L, HEAD_DIM), 0.01),
        "gqa_k_norm": 1.0 + nrm(ks[20], (L, HEAD_DIM), 0.01),
        "w_branch": nrm(ks[21], (L, N_BRANCH, BRANCH_W, D_MODEL), DEEPNORM_BETA * BRANCH_W ** -0.5),
        "w_out": nrm(ks[22], (L, D_MODEL, D_MODEL), DEEPNORM_BETA * D_MODEL ** -0.5),
        "ln_g": 1.0 + nrm(ks[23], (L, D_MODEL), 0.01),
        "ln_b": nrm(ks[24], (L, D_MODEL), 0.01),
    }


def reference(x, c, ctx, c_ctx, w_mod, b_mod, w_in, na_rel_bias, mla_q_norm, mla_w_uq,
              mla_kv_norm, mla_w_ukv, lru_conv_w, lru_conv_b, lru_w_a, lru_b_a, lru_w_x,
              lru_b_x, lru_lambda, gqa_q_norm, gqa_k_norm, w_branch, w_out, ln_g, ln_b):
    for l in range(DEPTH):
        need_ctx = l < DEPTH - 1
        x, ctx = hybrid_layer(x, ctx, c, c_ctx, w_mod[l], b_mod[l], w_in[l], na_rel_bias[l],
                              mla_q_norm[l], mla_w_uq[l], mla_kv_norm[l], mla_w_ukv[l],
                              lru_conv_w[l], lru_conv_b[l], lru_w_a[l], lru_b_a[l], lru_w_x[l],
                              lru_b_x[l], lru_lambda[l], gqa_q_norm[l], gqa_k_norm[l],
                              w_branch[l], w_out[l], ln_g[l], ln_b[l], need_ctx)
    return x
```

```python
from contextlib import ExitStack
import numpy as np
import ml_dtypes
import concourse.bass as bass
import concourse.mybir as mybir
from concourse.bass_utils import run_bass_kernel_spmd

F32 = mybir.dt.float32
BF16 = mybir.dt.bfloat16
AF = mybir.ActivationFunctionType
ALU = mybir.AluOpType
AX = mybir.AxisListType

D_MODEL = 1024
SEQ = 4096
CTX = 256
T = SEQ + CTX
DEPTH = 2
GRID_W = 64
WIN_H, WIN_W = 8, 16
EPS = 1e-6
THETA = 10000.0
NA_SCALE = 64 ** -0.5
MLA_SCALE = 96 ** -0.5
GQA_SCALE = 64 ** -0.5
ALPHA = (2 * DEPTH) ** 0.25
MIX_COLS = 1952
N_IN = 7072
NEG = -200.0
BLOCKS = [(0, 256)] + [(256 + 512 * j, 512) for j in range(8)]
NCOL = 45


STRICT_SAME_ENGINE = True


class _Op:
    __slots__ = ("eng", "fn", "reads", "writes", "group", "idx", "waits", "sig", "is_dma", "gval", "bar")


class _Rec:
    def __init__(self):
        self.call = None

    def __getattr__(self, name):
        def f(*a, **k):
            self.call = (name, a, k)
        return f


class Sched:
    def __init__(self, nc):
        self.nc = nc
        self.ops = []
        self.stack = ExitStack()

    def op(self, eng, fn, reads=(), writes=()):
        o = _Op()
        if fn is not None:
            rec = _Rec()
            fn(rec)
            assert rec.call is not None
            fn = rec.call
        o.eng, o.fn, o.reads, o.writes = eng, fn, tuple(reads), tuple(writes)
        o.group, o.is_dma, o.sig, o.waits, o.gval, o.bar = None, False, False, [], 0, False
        self.ops.append(o)
        return o

    def dma(self, eng, out, in_, reads=(), writes=(), group=None):
        o = self.op(eng, lambda e: e.dma_start(out=out, in_=in_), reads, writes)
        o.is_dma, o.group = True, group
        assert group is not None
        return o

    def barrier(self):
        o = self.op("*", None)
        o.bar = True

    def finish(self, final_groups):
        nc, ops = self.nc, self.ops
        ENGS = ["pe", "act", "dve", "pool", "sp"]
        last_writer, readers = {}, {}
        eng_ops = {e: [] for e in ENGS}
        last_comp = {e: 0 for e in ENGS}
        grp_cnt = {}
        known = {e: {} for e in ENGS}
        pending = {e: [] for e in ENGS}
        for gi, o in enumerate(ops):
            if o.bar:
                for e in ENGS:
                    w = []
                    for e2 in ENGS:
                        if (e2 != e or e != "pe") and last_comp[e2] > 0:
                            w.append((("e", e2), last_comp[e2]))
                    for g, c in grp_cnt.items():
                        w.append((("g", g), c))
                    pending[e] = w
                last_writer, readers = {}, {}
                continue
            lst = eng_ops[o.eng]
            lst.append(o)
            o.idx = len(lst)
            raw, deps = set(), set()
            for r in o.reads:
                if r in last_writer:
                    deps.add(last_writer[r]); raw.add(last_writer[r])
            for w in o.writes:
                if w in last_writer:
                    deps.add(last_writer[w])
                for rd in readers.get(w, ()):
                    deps.add(rd)
            deps.discard(gi)
            need = {}
            for key, val in pending[o.eng]:
                if val > need.get(key, 0):
                    need[key] = val
            pending[o.eng] = []
            for d in deps:
                p = ops[d]
                if p.is_dma:
                    key, val = ("g", p.group), grp_cnt[p.group]
                else:
                    if p.eng == o.eng and not o.is_dma:
                        if o.eng == "pe" or (d not in raw and not STRICT_SAME_ENGINE):
                            continue
                    key, val = ("e", p.eng), p.idx
                if val > need.get(key, 0):
                    need[key] = val
            kn = known[o.eng]
            for key, val in need.items():
                if kn.get(key, 0) >= val:
                    continue
                kn[key] = val
                o.waits.append((key, val))
                if key[0] == "e":
                    eng_ops[key[1]][val - 1].sig = True
            if o.is_dma:
                grp_cnt[o.group] = grp_cnt.get(o.group, 0) + 1
                o.gval = grp_cnt[o.group]
            else:
                last_comp[o.eng] = o.idx
            for r in o.reads:
                readers.setdefault(r, []).append(gi)
            for w in o.writes:
                last_writer[w] = gi
                readers[w] = []
        waited = {}
        for lst in eng_ops.values():
            for o in lst:
                for key, val in o.waits:
                    if key[0] == "g":
                        waited.setdefault(key[1], set()).add(val)
        for eng, lst in eng_ops.items():
            seen = {}
            for o in lst:
                for key, val in o.waits:
                    if key[0] == "g":
                        seen[key[1]] = max(seen.get(key[1], 0), val)
                if o.is_dma and o.gval > 1 and (o.gval - 1) in waited.get(o.group, ()) and seen.get(o.group, 0) < o.gval - 1:
                    o.waits.append((("g", o.group), o.gval - 1))
                    seen[o.group] = o.gval - 1
        sigcnt = {}
        for eng, lst in eng_ops.items():
            c, arr = 0, []
            for o in lst:
                if o.sig and not o.is_dma:
                    c += 1
                arr.append(c)
            sigcnt[eng] = arr
        st = self.stack
        esem = {eng: st.enter_context(nc.semaphore("s_" + eng)) for eng in ENGS}
        gsem = {g: st.enter_context(nc.semaphore("g_%d" % i)) for i, g in enumerate(grp_cnt)}
        self.n_sems = len(esem) + len(gsem)
        self.stats = {e: len(l) for e, l in eng_ops.items()}
        self.maxsem = {e: (a[-1] if a else 0) for e, a in sigcnt.items()}
        self.maxgrp = max(grp_cnt.values()) if grp_cnt else 0
        block = st.enter_context(nc.Block())
        attach = {"pe": block.tensor, "act": block.scalar, "dve": block.vector,
                  "pool": block.gpsimd, "sp": block.sync}

        def emit(eng, lst):
            def body(e):
                for o in lst:
                    for key, val in o.waits:
                        if key[0] == "e":
                            e.wait_ge(esem[key[1]], sigcnt[key[1]][val - 1])
                        else:
                            e.wait_ge(gsem[key[1]], 16 * val)
                    name, a, k = o.fn
                    ins = getattr(e, name)(*a, **k)
                    if o.is_dma:
                        ins.then_inc(gsem[o.group], 16)
                    elif o.sig:
                        ins.then_inc(esem[eng], 1)
                if eng == "sp":
                    for g in final_groups:
                        if g in gsem:
                            e.wait_ge(gsem[g], 16 * grp_cnt[g])
            attach[eng](body)

        for eng in ENGS:
            emit(eng, eng_ops[eng])
        st.close()


class Builder:
    def __init__(self, nc):
        self.nc = nc
        self.s = Sched(nc)
        self.lo = 16384
        self.p = self.lo
        self.hi = 229300
        self.cnt = 0
        self.psb = [nc.alloc_psum_tensor("psb%d" % i, [128, 512], F32) for i in range(8)]
        self.psi = 0

    def alloc(self, name, shape, dtype):
        n = 1
        for d in shape[1:]:
            n *= d
        nbytes = n * (2 if dtype == BF16 else 4)
        nbytes = (nbytes + 63) // 64 * 64
        self.cnt += 1
        t = self.nc.alloc_sbuf_tensor_at("%s_%d" % (name, self.cnt), list(shape), dtype, offset=self.p)
        self.p += nbytes
        assert self.p <= self.hi, ("SBUF overflow", name, self.p)
        return t

    def mark(self):
        return self.p

    def release(self, m):
        self.s.barrier()
        self.p = m

    def ps(self):
        i = self.psi % 8
        self.psi += 1
        return self.psb[i], ("psb", i)


class RPool:
    def __init__(self, b, name, shape, dtype, n):
        self.name = name
        self.bufs = [b.alloc(name, shape, dtype) for _ in range(n)]
        self.i = 0

    def next(self):
        k = self.i % len(self.bufs)
        self.i += 1
        return self.bufs[k], (self.name, k)


def _perm(width, half):
    idx = np.arange(width)
    return np.where((idx % (2 * half)) < half, idx + half, idx - half)


def _na_plan():
    variants, plan = {}, []
    n_rows = SEQ // GRID_W
    for r in range(n_rows):
        rs = int(np.clip(r - WIN_H // 2, 0, n_rows - WIN_H))
        g0 = rs & ~1
        tiles = []
        g = g0
        while g < rs + WIN_H:
            top = (g - r) if rs <= g < rs + WIN_H else None
            bot = (g + 1 - r) if rs <= g + 1 < rs + WIN_H else None
            key = (top, bot)
            if key not in variants:
                variants[key] = len(variants)
            tiles.append((g, variants[key]))
            g += 2
        plan.append(tiles)
    return variants, plan


NA_VARIANTS, NA_PLAN = _na_plan()
NV = len(NA_VARIANTS)


def _na_bias_tiles(rel_bias):
    H = rel_bias.shape[0]
    out = np.full((H, NV, 128, GRID_W), NEG, np.float32)
    qc = np.arange(GRID_W)
    cs = np.clip(qc - WIN_W // 2, 0, GRID_W - WIN_W)
    kc = np.arange(GRID_W)
    valid = (kc[:, None] >= cs[None, :]) & (kc[:, None] < cs[None, :] + WIN_W)
    dc = kc[:, None] - qc[None, :] + WIN_W - 1
    dc_c = np.clip(dc, 0, 2 * WIN_W - 2)
    for (top, bot), v in NA_VARIANTS.items():
        for half, d in ((0, top), (1, bot)):
            if d is None:
                continue
            g = rel_bias[:, d + WIN_H - 1, :][:, dc_c]
            out[:, v, half * 64:(half + 1) * 64, :] = np.where(valid[None], g, np.float32(NEG))
    return out


def _rope_tables():
    t = np.arange(SEQ)
    rows = (t // GRID_W).astype(np.float64)
    cols = (t % GRID_W).astype(np.float64)

    def tab(d_head_rot):
        half = d_head_rot // 2
        nf = half // 2
        inv = THETA ** (-np.arange(0, half, 2, dtype=np.float64) / half)
        C = np.ones((d_head_rot, T), np.float64)
        S = np.zeros((d_head_rot, T), np.float64)
        for ax, pos in enumerate((rows, cols)):
            ang = pos[None, :] * inv[:, None]
            c, s = np.cos(ang), np.sin(ang)
            b0 = ax * half
            C[b0:b0 + nf, CTX:] = c
            C[b0 + nf:b0 + half, CTX:] = c
            S[b0:b0 + nf, CTX:] = -s
            S[b0 + nf:b0 + half, CTX:] = s
        return C.astype(np.float32), S.astype(np.float32)

    C64, S64 = tab(64)
    C32, S32 = tab(32)
    tabD = np.stack([np.concatenate([C64, C64], 0), np.concatenate([S64, S64], 0)])
    CB = np.concatenate([np.ones((64, T), np.float32), C32], 0)
    SB = np.concatenate([np.zeros((64, T), np.float32), S32], 0)
    tabB = np.stack([CB, SB])
    tabK = np.stack([C32, S32])
    return tabD, tabB, tabK


def build_program(debug=False, PH="MPCABDF", n_layers=DEPTH):
    nc = bass.Bass("TRN2", target_bir_lowering=False)
    b = Builder(nc)
    s = b.s

    def din(name, shape, dt=F32):
        return nc.dram_tensor(name, list(shape), dt, kind="ExternalInput").ap()

    def dscr(name, shape, dt, out=False):
        return nc.dram_tensor(name, list(shape), dt, kind=("ExternalOutput" if out else "Internal")).ap()

    x_in = din("x", [SEQ, D_MODEL])
    ctx_in = din("ctx", [CTX, D_MODEL])
    cv_in = din("cv", [128, 8, 2])
    w_mod = din("w_mod", [DEPTH, 1024, 3072])
    w_in = din("w_in", [DEPTH, 1024, N_IN])
    w_insw = din("w_insw", [DEPTH, 1024, 416])
    w_uq = din("w_uq", [DEPTH, 256, 384])
    w_uqsw = din("w_uqsw", [DEPTH, 256, 384])
    w_ukv = din("w_ukv", [DEPTH, 128, 512])
    w_br = din("w_br", [DEPTH, 1024, 1024])
    w_out = din("w_out", [DEPTH, 1024, 1024])
    lruw = din("lruw", [DEPTH, 2, 2, 2, 128, 128])
    cols_in = din("cols", [DEPTH, 128, NCOL])
    rep_in = din("rep", [DEPTH, 3, 128, 1024])
    tabD = din("tabD", [2, 128, T])
    tabB = din("tabB", [2, 96, T])
    tabK = din("tabK", [2, 32, T])
    nab = din("nab", [DEPTH, 4, NV, 128, 64])
    cst = din("cst", [4, 128, 128])
    out_d = dscr("out", [SEQ, D_MODEL], F32, out=True)

    dbg = debug
    xn_d = dscr("xn_d", [1024, T], BF16, dbg)
    zs_d = dscr("zs_d", [1024, T], BF16, dbg)
    y_d = dscr("y_d", [1024, T], F32, dbg)
    qA_d = dscr("qA_d", [256, T], BF16, dbg)
    kA_d = dscr("kA_d", [256, T], BF16, dbg)
    vaug_d = dscr("vaug_d", [10, 128, 34, 128], BF16, dbg)
    qB_d = dscr("qB_d", [4, 96, T], BF16, dbg)
    kB_d = dscr("kB_d", [4, 96, T], BF16, dbg)
    qD_d = dscr("qD_d", [256, T], BF16, dbg)
    kD_d = dscr("kD_d", [128, T], BF16, dbg)
    x1_d = dscr("x1_d", [SEQ, D_MODEL], F32, dbg)
    ctx1_d = dscr("ctx1_d", [CTX, D_MODEL], F32, dbg)

    gcount = [0]

    def G(name, key=None):
        return name if key is None else "%s:%s" % (name, key)

    ident = b.alloc("ident", [128, 128], F32)
    ones256 = b.alloc("ones256", [128, 128], F32)
    ones128 = b.alloc("ones128", [128, 128], F32)
    bones64 = b.alloc("bones64", [128, 128], F32)
    ident_bf = b.alloc("ident_bf", [128, 128], BF16)
    for j, tt in enumerate((ident, ones256, ones128, bones64)):
        s.dma("sp", tt[:], cst[j], writes=[("cst", j)], group="cst")
    s.dma("pool", ident_bf[:], cst[0], writes=["ident_bf"], group="cstbf")
    cvs = b.alloc("cvs", [128, 8, 2], F32)
    scv = b.alloc("scv", [128, 8, 2], F32)
    s.dma("sp", cvs[:], cv_in, writes=["cvs"], group="cvs")
    s.op("act", lambda e: e.activation(out=scv[:], in_=cvs[:], func=AF.Silu), reads=["cvs"], writes=["scv"])
    CSTK = [("cst", j) for j in range(4)]
    epsc = b.alloc("epsc", [128, 1], F32)
    s.op("pool", lambda e: e.memset(epsc[:, :], EPS), writes=["epsc"])

    def rstd(out_ap, in_ap, reads, wkey):
        np_ = out_ap.shape[0]
        s.op("act", lambda e: e.activation(out=out_ap, in_=in_ap, func=AF.Sqrt, bias=epsc[0:np_, 0:1]), reads=list(reads) + ["epsc"], writes=[wkey])
        s.op("dve", lambda e: e.reciprocal(out=out_ap, in_=out_ap), reads=[wkey], writes=[wkey])

    base_mark = b.mark()
    onesb = b.alloc("onesb", [128, 34, 128], BF16)
    s.op("pool", lambda e: e.memset(onesb[:, :, :], 1.0), writes=["onesb"])
    for hh_ in range(10):
        s.dma("sp", vaug_d[hh_], onesb[:, :, :], reads=["onesb"], group="onesinit")

    for l in range(n_layers):
        need_ctx = l < DEPTH - 1
        x_src = x_in if l == 0 else x1_d
        c_src = ctx_in if l == 0 else ctx1_d
        x_dst = x1_d if l < DEPTH - 1 else out_d
        b.release(base_mark)
        cols = b.alloc("cols", [128, NCOL], F32)
        s.dma("sp", cols[:], cols_in[l], writes=["cols"], group=G("cols"))
        modfm = b.alloc("modfm", [128, 16, 2], F32)
        gate_rep = b.alloc("gate_rep", [128, 2, 1024], F32)
        lng = b.alloc("lng", [128, 1024], F32)
        lnb = b.alloc("lnb", [128, 1024], F32)
        s.dma("sp", lng[:], rep_in[l, 1], writes=["lng"], group=G("lng"))
        s.dma("sp", lnb[:], rep_in[l, 2], writes=["lnb"], group=G("lnb"))
        nsp = b.alloc("nsp", [128, 8], F32)
        layer_mark = b.mark()

        xrT = b.alloc("xrT", [128, 2, T], F32)
        pc_mark = b.mark()
        NP = 3392
        wP = b.alloc("wP", [128, 8, NP], BF16)
        for kc in range(8):
            for (c0, c1) in ((0, 1024), (1024, 2048), (2048, 2976)):
                s.dma("pool", wP[:, kc, c0:c1], w_in[l, kc * 128:(kc + 1) * 128, c0:c1], writes=[("wP", kc)], group=G("wP"))
            s.dma("pool", wP[:, kc, 2976:NP], w_insw[l, kc * 128:(kc + 1) * 128, :], writes=[("wP", kc)], group=G("wP"))
        WPK = [("wP", kc) for kc in range(8)]
        wuq = b.alloc("wuq", [128, 2, 384], BF16)
        wuqs = b.alloc("wuqs", [128, 2, 384], BF16)
        wukv = b.alloc("wukv", [128, 512], BF16)
        for ch in range(2):
            s.dma("pool", wuq[:, ch, :], w_uq[l, ch * 128:(ch + 1) * 128, :], writes=["wuq"], group=G("wuq"))
            s.dma("pool", wuqs[:, ch, :], w_uqsw[l, ch * 128:(ch + 1) * 128, :], writes=["wuqs"], group=G("wuq"))
        s.dma("pool", wukv[:, :], w_ukv[l], writes=["wukv"], group=G("wuq"))

        m_mark = b.mark()
        wm_pool = RPool(b, "wm", [128, 8, 512], F32, 4)
        screp = b.alloc("screp", [128, 2, 8, 128], F32)
        bgate = b.alloc("bgate", [128, 1024], F32)
        s.dma("sp", bgate[:], rep_in[l, 0], writes=["bgate"], group=G("bgate"))
        for j in range(2):
            for kc in range(8):
                s.op("act", lambda e, j=j, kc=kc: e.activation(out=screp[:, j, kc, :], in_=ones256[:, :], func=AF.Identity,
                                                                 scale=scv[:, kc, j:j + 1]),
                     reads=["scv", ("cst", 1)], writes=[("screp", j, kc)])
        for grp in range(6):
            wm, wmk = wm_pool.next()
            for kc in range(8):
                s.dma("sp", wm[:, kc, :], w_mod[l, kc * 128:(kc + 1) * 128, grp * 512:(grp + 1) * 512],
                      writes=[wmk], group=G("wm", wmk))
            if grp < 4:
                for q in range(4):
                    n = grp * 4 + q
                    ps, pk = b.ps()
                    for kc in range(8):
                        s.op("pe", lambda e, ps=ps, wm=wm, kc=kc, q=q: e.matmul(ps[:, 0:2], lhsT=wm[:, kc, q * 128:(q + 1) * 128],
                                                                                rhs=scv[:, kc, :], start=(kc == 0), stop=(kc == 7)),
                             reads=[wmk, "scv"], writes=[pk])
                    addc = 0.0 if n < 8 else 1.0
                    s.op("dve", lambda e, ps=ps, n=n, addc=addc: e.tensor_scalar(out=modfm[:, n, :], in0=ps[:, 0:2], scalar1=cols[:, n:n + 1],
                                                                                  scalar2=addc, op0=ALU.add, op1=ALU.add),
                         reads=[pk, "cols"], writes=[("modfm", n)])
            else:
                half = grp - 4
                for j in range(2):
                    if j == 1 and not need_ctx:
                        continue
                    ps, pk = b.ps()
                    for kc in range(8):
                        s.op("pe", lambda e, ps=ps, wm=wm, kc=kc, j=j: e.matmul(ps[:, :], lhsT=screp[:, j, kc, :], rhs=wm[:, kc, :],
                                                                                start=(kc == 0), stop=(kc == 7)),
                             reads=[wmk] + [("screp", j, kc)], writes=[pk])
                    s.op("dve", lambda e, ps=ps, j=j, half=half: e.scalar_tensor_tensor(
                        out=gate_rep[:, j, half * 512:(half + 1) * 512], in0=ps[:, :], scalar=256.0,
                        in1=bgate[:, half * 512:(half + 1) * 512], op0=ALU.mult, op1=ALU.add),
                         reads=[pk, "bgate"], writes=[("gate_rep", j, half)])
        MODK = [("modfm", n) for n in range(16)]
        GATEK = [("gate_rep", j, h) for j in range(2) for h in range(2)]
        sp = b.alloc("sp_tmp", [128, 8, 4], F32)
        lam = cols[:, 41:45]
        V = lambda i: sp[:, i, :]
        chain = []

        def dv(fn, rd, wr):
            s.op("dve", fn, reads=rd, writes=wr)
        s.op("act", lambda e: e.activation(out=V(0), in_=lam, func=AF.Abs), reads=["cols"], writes=[("sp", 0)])
        s.op("act", lambda e: e.activation(out=V(1), in_=V(0), func=AF.Exp, scale=-1.0), reads=[("sp", 0)], writes=[("sp", 1)])
        dv(lambda e: e.tensor_scalar(out=V(2), in0=V(1), scalar1=2.0, scalar2=None, op0=ALU.add), [("sp", 1)], [("sp", 2)])
        dv(lambda e: e.reciprocal(out=V(2), in_=V(2)), [("sp", 2)], [("sp", 2)])
        dv(lambda e: e.tensor_tensor(out=V(3), in0=V(1), in1=V(2), op=ALU.mult), [("sp", 1), ("sp", 2)], [("sp", 3)])
        dv(lambda e: e.tensor_tensor(out=V(4), in0=V(3), in1=V(3), op=ALU.mult), [("sp", 3)], [("sp", 4)])
        dv(lambda e: e.tensor_scalar(out=V(5), in0=V(4), scalar1=1.0 / 11, scalar2=1.0 / 9, op0=ALU.mult, op1=ALU.add), [("sp", 4)], [("sp", 5)])
        for cst_ in (1.0 / 7, 1.0 / 5, 1.0 / 3, 1.0):
            dv(lambda e: e.tensor_tensor(out=V(5), in0=V(5), in1=V(4), op=ALU.mult), [("sp", 5), ("sp", 4)], [("sp", 5)])
            dv(lambda e, c=cst_: e.tensor_scalar(out=V(5), in0=V(5), scalar1=c, scalar2=None, op0=ALU.add), [("sp", 5)], [("sp", 5)])
        dv(lambda e: e.tensor_tensor(out=V(5), in0=V(5), in1=V(3), op=ALU.mult), [("sp", 5), ("sp", 3)], [("sp", 5)])
        dv(lambda e: e.tensor_scalar(out=V(6), in0=lam, scalar1=-1.0, scalar2=0.0, op0=ALU.mult, op1=ALU.max), ["cols"], [("sp", 6)])
        dv(lambda e: e.scalar_tensor_tensor(out=V(7), in0=V(5), scalar=2.0, in1=V(6), op0=ALU.mult, op1=ALU.add),
           [("sp", 5), ("sp", 6)], [("sp", 7)])
        dv(lambda e: e.tensor_scalar(out=nsp[:, 0:4], in0=V(7), scalar1=-8.0, scalar2=None, op0=ALU.mult), [("sp", 7)], ["nsp"])
        dv(lambda e: e.tensor_scalar(out=nsp[:, 4:8], in0=V(7), scalar1=-16.0, scalar2=None, op0=ALU.mult), [("sp", 7)], ["nsp2"])

        b.release(m_mark)
        xt_pool = RPool(b, "xt", [128, 1024], F32, 4)
        st_pool = RPool(b, "st", [128, 16], F32, 4)
        xh_pool = RPool(b, "xh", [128, 4, 1024], F32, 1)
        xn_pool = RPool(b, "xn", [128, 8, 512], BF16, 2)
        f32_pool = RPool(b, "pf", [128, 512], F32, 8)
        bf_pool = RPool(b, "pb", [128, 512], BF16, 5)
        tab_pool = RPool(b, "tab", [128, 6, 512], F32, 1)
        cq_pool = RPool(b, "cqg", [128, 3, 512], BF16, 1)
        sq_pool = RPool(b, "sq", [128, 3, 512], F32, 1)
        vst_pool = RPool(b, "vst", [128, 640], BF16, 2)
        rc_pool = RPool(b, "rc", [128, 1], F32, 4)
        evi = [0]

        def evac_engine():
            evi[0] += 1
            return "act" if evi[0] % 2 else "dve"

        def pblock(bi):
            t0, nt = BLOCKS[bi]
            ntile = nt // 128
            is_ctx = bi == 0
            full = (not is_ctx) or need_ctx
            mc = 1 if is_ctx else 0
            src = c_src if is_ctx else x_src
            xts = []
            for i in range(ntile):
                xt, xtk = xt_pool.next()
                r0 = (t0 + i * 128) if is_ctx else (t0 - CTX + i * 128)
                s.dma("sp", xt[:, :], src[r0:r0 + 128, :], writes=[xtk], group=G("xt", xtk))
                xts.append((xt, xtk))
            yield "L"
            xh, xhk = xh_pool.next()
            for i in range(ntile):
                yield "a"
                xt, xtk = xts[i]
                st, stk = st_pool.next()
                s.op("dve", lambda e, st=st, xt=xt: e.bn_stats(out=st[:, 0:6], in_=xt[:, 0:512]), reads=[xtk], writes=[(stk, 0)])
                s.op("dve", lambda e, st=st, xt=xt: e.bn_stats(out=st[:, 6:12], in_=xt[:, 512:1024]), reads=[xtk], writes=[(stk, 1)])
                s.op("dve", lambda e, st=st: e.bn_aggr(out=st[:, 12:14], in_=st[:, 0:12].rearrange("p (a b) -> p a b", b=6)),
                     reads=[(stk, 0), (stk, 1)], writes=[(stk, 2)])
                rstd(st[:, 14:15], st[:, 13:14], [(stk, 2)], (stk, 3))
                s.op("dve", lambda e, st=st, xt=xt, xh=xh, i=i: e.tensor_scalar(out=xh[:, i, :], in0=xt[:, :], scalar1=st[:, 12:13],
                                                                                scalar2=st[:, 14:15], op0=ALU.subtract, op1=ALU.mult),
                     reads=[xtk, (stk, 2), (stk, 3)], writes=[(xhk, i)])
            yield "A"
            xn, xnk = xn_pool.next()
            for kc in range(8):
                yield "b"
                ps, pk = b.ps()
                for i in range(ntile):
                    s.op("pe", lambda e, ps=ps, xh=xh, i=i, kc=kc: e.transpose(ps[:, i * 128:(i + 1) * 128], xh[:, i, kc * 128:(kc + 1) * 128], ident[:, :]),
                         reads=[(xhk, i), ("cst", 0)], writes=[pk])
                eng = evac_engine()
                if eng == "act":
                    s.op("act", lambda e, ps=ps, xn=xn, kc=kc, mc=mc, nt=nt: e.activation(
                        out=xn[:, kc, 0:nt], in_=ps[:, 0:nt], func=AF.Identity, scale=modfm[:, 8 + kc, mc:mc + 1], bias=modfm[:, kc, mc:mc + 1]),
                         reads=[pk] + MODK, writes=[(xnk, kc)])
                else:
                    s.op("dve", lambda e, ps=ps, xn=xn, kc=kc, mc=mc, nt=nt: e.tensor_scalar(
                        out=xn[:, kc, 0:nt], in0=ps[:, 0:nt], scalar1=modfm[:, 8 + kc, mc:mc + 1], scalar2=modfm[:, kc, mc:mc + 1],
                        op0=ALU.mult, op1=ALU.add), reads=[pk] + MODK, writes=[(xnk, kc)])
            XNK = [(xnk, kc) for kc in range(8)]
            if full:
                s.dma("sp", xn_d.rearrange("(kc p) t -> p kc t", p=128)[:, :, t0:t0 + nt], xn[:, :, 0:nt], reads=XNK, group=G("xn_d", xnk))
            yield "B"
            tb, tbk = ptab[bi]

            def fm(c0, width):
                ps, pk = b.ps()
                for kc in range(8):
                    s.op("pe", lambda e, ps=ps, kc=kc: e.matmul(ps[0:width, 0:nt], lhsT=wP[:, kc, c0:c0 + width], rhs=xn[:, kc, 0:nt],
                                                                start=(kc == 0), stop=(kc == 7)),
                         reads=[("wP", kc), (xnk, kc)], writes=[pk])
                return ps, pk

            def store_fm(dst_ap, rows, src_t, src_k):
                s.dma("sp", dst_ap, src_t[0:rows, 0:nt], reads=[src_k], group=G("st", src_k))

            for ch in range(2):
                yield "c"
                if full:
                    ps, pk = fm(ch * 128, 128)
                    o, ok = bf_pool.next()
                    s.op("act", lambda e, ps=ps, o=o: e.activation(out=o[:, 0:nt], in_=ps[:, 0:nt], func=AF.Copy, scale=NA_SCALE),
                         reads=[pk], writes=[ok])
                    store_fm(qA_d[ch * 128:(ch + 1) * 128, t0:t0 + nt], 128, o, ok)
                ps, pk = fm(256 + ch * 128, 128)
                o, ok = bf_pool.next()
                s.op("dve", lambda e, ps=ps, o=o: e.tensor_copy(out=o[:, 0:nt], in_=ps[:, 0:nt]), reads=[pk], writes=[ok])
                store_fm(kA_d[ch * 128:(ch + 1) * 128, t0:t0 + nt], 128, o, ok)
            yield "C0"
            for ch in range(2):
                yield "c"
                ps, pk = fm(1184 + ch * 128, 128)
                s.op("act", lambda e, ps=ps, ch=ch: e.activation(out=xrT[:, ch, t0:t0 + nt], in_=ps[:, 0:nt], func=AF.Copy),
                     reads=[pk], writes=[("xrT", ch, bi)])
            if full:
                for j in range(8):
                    yield "c"
                    ps, pk = fm(1952 + j * 128, 128)
                    o, ok = bf_pool.next()
                    s.op("act", lambda e, ps=ps, o=o: e.activation(out=o[:, 0:nt], in_=ps[:, 0:nt], func=AF.Silu), reads=[pk], writes=[ok])
                    store_fm(zs_d[j * 128:(j + 1) * 128, t0:t0 + nt], 128, o, ok)
            yield "C1"
            dlist = []
            if full:
                dlist += [("q", 1440, 2976, 19, 20, qD_d, 0), ("q", 1568, 3104, 19, 20, qD_d, 128)]
            dlist += [("k", 1696, 3232, 21, 22, kD_d, 0)]
            for (kind, ca, cb_, gcol, gscol, dst, drow) in dlist:
                yield "c"
                psa, pka = fm(ca, 128)
                psb_, pkb = fm(cb_, 128)
                sqt, sqk = f32_pool.next()
                s.op("act", lambda e, psa=psa, sqt=sqt: e.activation(out=sqt[:, 0:nt], in_=psa[:, 0:nt], func=AF.Square), reads=[pka], writes=[sqk])
                psm, pkm = b.ps()
                s.op("pe", lambda e, psm=psm, sqt=sqt: e.matmul(psm[:, 0:nt], lhsT=bones64[:, :], rhs=sqt[:, 0:nt], start=True, stop=True),
                     reads=[sqk, ("cst", 3)], writes=[pkm])
                rs_, rsk = f32_pool.next()
                rstd(rs_[:, 0:nt], psm[:, 0:nt], [pkm], rsk)
                e1, e1k = f32_pool.next()
                e2, e2k = f32_pool.next()
                s.op("act", lambda e, psa=psa, e1=e1, gcol=gcol: e.activation(out=e1[:, 0:nt], in_=psa[:, 0:nt], func=AF.Identity,
                                                                              scale=cols[:, gcol:gcol + 1]), reads=[pka, "cols"], writes=[e1k])
                s.op("act", lambda e, psb_=psb_, e2=e2, gscol=gscol: e.activation(out=e2[:, 0:nt], in_=psb_[:, 0:nt], func=AF.Identity,
                                                                                  scale=cols[:, gscol:gscol + 1]), reads=[pkb, "cols"], writes=[e2k])
                s.op("pool", lambda e, e1=e1: e.tensor_tensor(out=e1[:, 0:nt], in0=e1[:, 0:nt], in1=tb[:, 0, 0:nt], op=ALU.mult),
                     reads=[e1k, (tbk, "D")], writes=[e1k])
                s.op("pool", lambda e, e2=e2: e.tensor_tensor(out=e2[:, 0:nt], in0=e2[:, 0:nt], in1=tb[:, 1, 0:nt], op=ALU.mult),
                     reads=[e2k, (tbk, "D")], writes=[e2k])
                s.op("pool", lambda e, e1=e1, e2=e2: e.tensor_tensor(out=e1[:, 0:nt], in0=e1[:, 0:nt], in1=e2[:, 0:nt], op=ALU.add),
                     reads=[e1k, e2k], writes=[e1k])
                o, ok = bf_pool.next()
                sc_ = GQA_SCALE if kind == "q" else 1.0
                s.op("dve", lambda e, e1=e1, rs_=rs_, o=o, sc_=sc_: e.scalar_tensor_tensor(out=o[:, 0:nt], in0=e1[:, 0:nt], scalar=sc_, in1=rs_[:, 0:nt],
                                                                                           op0=ALU.mult, op1=ALU.mult), reads=[e1k, rsk], writes=[ok])
                store_fm(dst[drow:drow + 128, t0:t0 + nt], 128, o, ok)
            cq, cqk = cq_pool.next()
            sq, sqk3 = sq_pool.next()
            chunks = ([(0, 768, 16), (1, 896, 17)] if full else []) + [(2, 1024, 18)]
            for (slot, c0, gcol) in chunks:
                yield "c"
                ps, pk = fm(c0, 128)
                s.op("act", lambda e, ps=ps, slot=slot, gcol=gcol: e.activation(out=cq[:, slot, 0:nt], in_=ps[:, 0:nt], func=AF.Identity,
                                                                                scale=cols[:, gcol:gcol + 1]), reads=[pk, "cols"], writes=[(cqk, slot)])
                s.op("act", lambda e, ps=ps, slot=slot: e.activation(out=sq[:, slot, 0:nt], in_=ps[:, 0:nt], func=AF.Square),
                     reads=[pk], writes=[(sqk3, slot)])
            if full:
                psm, pkm = b.ps()
                for ch in range(2):
                    s.op("pe", lambda e, psm=psm, ch=ch: e.matmul(psm[:, 0:nt], lhsT=ones256[:, :], rhs=sq[:, ch, 0:nt], start=(ch == 0), stop=(ch == 1)),
                         reads=[(sqk3, ch), ("cst", 1)], writes=[pkm])
                rq, rqk = f32_pool.next()
                rstd(rq[:, 0:nt], psm[:, 0:nt], [pkm], rqk)
                cp, cpk = f32_pool.next()
                sp_, spk = f32_pool.next()
                s.op("dve", lambda e, cp=cp, rq=rq: e.scalar_tensor_tensor(out=cp[0:96, 0:nt], in0=tb[0:96, 2, 0:nt], scalar=MLA_SCALE, in1=rq[0:96, 0:nt],
                                                                           op0=ALU.mult, op1=ALU.mult), reads=[(tbk, "B"), rqk], writes=[cpk])
                s.op("dve", lambda e, sp_=sp_, rq=rq: e.scalar_tensor_tensor(out=sp_[0:96, 0:nt], in0=tb[0:96, 3, 0:nt], scalar=MLA_SCALE, in1=rq[0:96, 0:nt],
                                                                             op0=ALU.mult, op1=ALU.mult), reads=[(tbk, "B"), rqk], writes=[spk])
                for h in range(4):
                    yield "c"
                    psa, pka = b.ps()
                    psb_, pkb = b.ps()
                    for ch in range(2):
                        s.op("pe", lambda e, psa=psa, ch=ch, h=h: e.matmul(psa[0:96, 0:nt], lhsT=wuq[:, ch, h * 96:(h + 1) * 96], rhs=cq[:, ch, 0:nt],
                                                                           start=(ch == 0), stop=(ch == 1)), reads=["wuq", (cqk, ch)], writes=[pka])
                    for ch in range(2):
                        s.op("pe", lambda e, psb_=psb_, ch=ch, h=h: e.matmul(psb_[0:96, 0:nt], lhsT=wuqs[:, ch, h * 96:(h + 1) * 96], rhs=cq[:, ch, 0:nt],
                                                                             start=(ch == 0), stop=(ch == 1)), reads=["wuqs", (cqk, ch)], writes=[pkb])
                    t1, t1k = f32_pool.next()
                    t2, t2k = f32_pool.next()
                    s.op("dve", lambda e, psa=psa, t1=t1, cp=cp: e.tensor_tensor(out=t1[0:96, 0:nt], in0=psa[0:96, 0:nt], in1=cp[0:96, 0:nt], op=ALU.mult),
                         reads=[pka, cpk], writes=[t1k])
                    s.op("dve", lambda e, psb_=psb_, t2=t2, sp_=sp_: e.tensor_tensor(out=t2[0:96, 0:nt], in0=psb_[0:96, 0:nt], in1=sp_[0:96, 0:nt], op=ALU.mult),
                         reads=[pkb, spk], writes=[t2k])
                    o, ok = bf_pool.next()
                    s.op("pool", lambda e, t1=t1, t2=t2, o=o: e.tensor_tensor(out=o[0:96, 0:nt], in0=t1[0:96, 0:nt], in1=t2[0:96, 0:nt], op=ALU.add),
                         reads=[t1k, t2k], writes=[ok])
                    store_fm(qB_d[h, :, t0:t0 + nt], 96, o, ok)
            psm, pkm = b.ps()
            s.op("pe", lambda e, psm=psm: e.matmul(psm[:, 0:nt], lhsT=ones128[:, :], rhs=sq[:, 2, 0:nt], start=True, stop=True),
                 reads=[(sqk3, 2), ("cst", 2)], writes=[pkm])
            rkv, rkvk = f32_pool.next()
            rstd(rkv[:, 0:nt], psm[:, 0:nt], [pkm], rkvk)
            for h in range(4):
                yield "c"
                ps, pk = b.ps()
                s.op("pe", lambda e, ps=ps, h=h: e.matmul(ps[0:64, 0:nt], lhsT=wukv[:, h * 64:(h + 1) * 64], rhs=cq[:, 2, 0:nt], start=True, stop=True),
                     reads=["wukv", (cqk, 2)], writes=[pk])
                o, ok = bf_pool.next()
                s.op("dve", lambda e, ps=ps, o=o, rkv=rkv: e.tensor_tensor(out=o[0:64, 0:nt], in0=ps[0:64, 0:nt], in1=rkv[0:64, 0:nt], op=ALU.mult),
                     reads=[pk, rkvk], writes=[ok])
                store_fm(kB_d[h, 0:64, t0:t0 + nt], 64, o, ok)
            psa, pka = fm(1152, 32)
            psb_, pkb = fm(3360, 32)
            t1, t1k = f32_pool.next()
            t2, t2k = f32_pool.next()
            s.op("dve", lambda e, psa=psa, t1=t1: e.tensor_tensor(out=t1[0:32, 0:nt], in0=psa[0:32, 0:nt], in1=tb[0:32, 4, 0:nt], op=ALU.mult),
                 reads=[pka, (tbk, "K")], writes=[t1k])
            s.op("dve", lambda e, psb_=psb_, t2=t2: e.tensor_tensor(out=t2[0:32, 0:nt], in0=psb_[0:32, 0:nt], in1=tb[0:32, 5, 0:nt], op=ALU.mult),
                 reads=[pkb, (tbk, "K")], writes=[t2k])
            o, ok = bf_pool.next()
            s.op("pool", lambda e, t1=t1, t2=t2, o=o: e.tensor_tensor(out=o[0:32, 0:nt], in0=t1[0:32, 0:nt], in1=t2[0:32, 0:nt], op=ALU.add),
                 reads=[t1k, t2k], writes=[ok])
            for h in range(4):
                store_fm(kB_d[h, 64:96, t0:t0 + nt], 32, o, ok)
            for i in range(ntile):
                yield "c"
                vs, vsk = vst_pool.next()
                tok = slice(i * 128, (i + 1) * 128)
                psa, pka = b.ps()
                for kc in range(8):
                    s.op("pe", lambda e, psa=psa, kc=kc, tok=tok: e.matmul(psa[:, 0:256], lhsT=xn[:, kc, tok], rhs=wP[:, kc, 512:768],
                                                                           start=(kc == 0), stop=(kc == 7)), reads=[("wP", kc), (xnk, kc)], writes=[pka])
                psd, pkd = b.ps()
                for kc in range(8):
                    s.op("pe", lambda e, psd=psd, kc=kc, tok=tok: e.matmul(psd[:, 0:128], lhsT=xn[:, kc, tok], rhs=wP[:, kc, 1824:1952],
                                                                           start=(kc == 0), stop=(kc == 7)), reads=[("wP", kc), (xnk, kc)], writes=[pkd])
                s.op("dve", lambda e, psa=psa, vs=vs: e.tensor_copy(out=vs[:, 0:256], in_=psa[:, 0:256]), reads=[pka], writes=[(vsk, 0)])
                s.op("act", lambda e, psd=psd, vs=vs: e.activation(out=vs[:, 256:384], in_=psd[:, 0:128], func=AF.Copy), reads=[pkd], writes=[(vsk, 1)])
                psc, pkc = b.ps()
                s.op("pe", lambda e, psc=psc, tok=tok: e.matmul(psc[:, 0:1], lhsT=sq[:, 2, tok], rhs=ones128[:, 0:1], start=True, stop=True),
                     reads=[(sqk3, 2), ("cst", 2)], writes=[pkc])
                rc, rck = rc_pool.next()
                rstd(rc[:, 0:1], psc[:, 0:1], [pkc], rck)
                psv, pkv = b.ps()
                s.op("pe", lambda e, psv=psv, tok=tok: e.matmul(psv[:, 0:256], lhsT=cq[:, 2, tok], rhs=wukv[:, 256:512], start=True, stop=True),
                     reads=["wukv", (cqk, 2)], writes=[pkv])
                s.op("act", lambda e, psv=psv, vs=vs, rc=rc: e.activation(out=vs[:, 384:640], in_=psv[:, 0:256], func=AF.Identity, scale=rc[:, 0:1]),
                     reads=[pkv, rck], writes=[(vsk, 2)])
                gt = (t0 + i * 128) // 128
                s.dma("sp", vaug_d[0:4, :, gt, 0:64].rearrange("h p c -> p h c"), vs[:, 0:256].rearrange("p (h c) -> p h c", c=64),
                      reads=[(vsk, 0)], group=G("st", vsk))
                s.dma("sp", vaug_d[8:10, :, gt, 0:64].rearrange("h p c -> p h c"), vs[:, 256:384].rearrange("p (h c) -> p h c", c=64),
                      reads=[(vsk, 1)], group=G("st", vsk))
                s.dma("sp", vaug_d[4:8, :, gt, 0:64].rearrange("h p c -> p h c"), vs[:, 384:640].rearrange("p (h c) -> p h c", c=64),
                      reads=[(vsk, 2)], group=G("st", vsk))

        ptab = {}

        def load_tables(bi):
            t0, nt = BLOCKS[bi]
            tb, tbk = tab_pool.next()
            s.dma("sp", tb[:, 0:2, 0:nt], tabD[:, :, t0:t0 + nt].rearrange("a p t -> p a t"), writes=[(tbk, "D")], group=G("tab"))
            s.dma("sp", tb[0:96, 2:4, 0:nt], tabB[:, :, t0:t0 + nt].rearrange("a p t -> p a t"), writes=[(tbk, "B")], group=G("tab"))
            s.dma("sp", tb[0:32, 4:6, 0:nt], tabK[:, :, t0:t0 + nt].rearrange("a p t -> p a t"), writes=[(tbk, "K")], group=G("tab"))
            ptab[bi] = (tb, tbk)

        gens = [pblock(bi) for bi in range(len(BLOCKS))]

        def advance(bi_, stops):
            for tag in gens[bi_]:
                if tag in stops:
                    return tag
            return None

        load_tables(0)
        advance(0, ("B",))
        for j in range(len(BLOCKS)):
            has_next = j + 1 < len(BLOCKS)
            if has_next:
                advance(j + 1, ("L",))
            advance(j, ("C0",))
            if has_next:
                advance(j + 1, ("A",))
            advance(j, ("C1",))
            if has_next:
                advance(j + 1, ("B",))
            advance(j, ())
            if has_next:
                load_tables(j + 1)

        b.release(pc_mark)
        if "C" in PH:
            wl = b.alloc("wl", [128, 2, 2, 2, 128], F32)
            for d in range(2):
                for g_ in range(2):
                    for ch in range(2):
                        s.dma("sp", wl[:, d, g_, ch, :], lruw[l, d, g_, ch], writes=["wl"], group=G("wl"))
            xcs = [b.alloc("xc%d" % ch_, [128, T], F32) for ch_ in range(2)]
            Abuf = b.alloc("Abuf", [128, T], F32)
            Ubuf = b.alloc("Ubuf", [128, T], F32)
            H0 = b.alloc("H0", [128, T], F32)
            H1 = b.alloc("H1", [128, T], F32)
            Hs = [H0, H1]
            NBK = len(BLOCKS)
            XRK = lambda ch: [("xrT", ch, bi) for bi in range(NBK)]
            AK = [("A", bi) for bi in range(NBK)]
            UK = ["U"]
            HK = lambda d: [("H", d, bi) for bi in range(NBK)]
            for ch in range(2):
                xr = xrT[:, ch, :]
                xc = xcs[ch]
                s.op("dve", lambda e: e.tensor_scalar(out=xc[:, :], in0=xr, scalar1=cols[:, 23 + ch * 4 + 2:24 + ch * 4 + 2],
                                                      scalar2=cols[:, 31 + ch:32 + ch], op0=ALU.mult, op1=ALU.add),
                     reads=XRK(ch) + ["cols"], writes=[("xc", ch)])
                for (s0, s1) in ((0, CTX), (CTX, T)):
                    for j, off in ((0, -2), (1, -1), (3, 1)):
                        lo = max(s0, s0 - off)
                        hi_ = min(s1, s1 - off)
                        s.op("dve", lambda e: e.scalar_tensor_tensor(
                            out=xc[:, lo:hi_], in0=xr[:, lo + off:hi_ + off], scalar=cols[:, 23 + ch * 4 + j:24 + ch * 4 + j],
                            in1=xc[:, lo:hi_], op0=ALU.mult, op1=ALU.add), reads=XRK(ch) + ["cols", ("xc", ch)], writes=[("xc", ch)])
            for ch in range(2):
                xc = xcs[ch]
                for d in range(2):
                    Hd = Hs[d]
                    ca = 33 + d * 2 + ch
                    cx = 37 + d * 2 + ch
                    cn = d * 2 + ch
                    for bi, (t0, nt) in enumerate(BLOCKS):
                        sl = slice(t0, t0 + nt)
                        psa, pka = b.ps()
                        s.op("pe", lambda e: e.matmul(psa[:, 0:nt], lhsT=wl[:, d, 0, ch, :], rhs=xc[:, sl], start=True, stop=True),
                             reads=["wl", ("xc", ch)], writes=[pka])
                        psx, pkx = b.ps()
                        s.op("pe", lambda e: e.matmul(psx[:, 0:nt], lhsT=wl[:, d, 1, ch, :], rhs=xc[:, sl], start=True, stop=True),
                             reads=["wl", ("xc", ch)], writes=[pkx])
                        s.op("act", lambda e: e.activation(out=Abuf[:, sl], in_=psa[:, 0:nt], func=AF.Sigmoid, bias=cols[:, ca:ca + 1]),
                             reads=[pka, "cols"], writes=[("A", bi)])
                        s.op("act", lambda e: e.activation(out=Hd[:, sl], in_=psx[:, 0:nt], func=AF.Sigmoid, bias=cols[:, cx:cx + 1]),
                             reads=[pkx, "cols"], writes=[("H", d, bi)])
                    s.op("pool", lambda e: e.tensor_tensor(out=Hd[:, :], in0=Hd[:, :], in1=xc[:, :], op=ALU.mult),
                         reads=HK(d) + [("xc", ch)], writes=HK(d))
                    s.op("act", lambda e: e.activation(out=Ubuf[:, :], in_=Abuf[:, :], func=AF.Exp, scale=nsp[:, 4 + cn:5 + cn]),
                         reads=AK + ["nsp2"], writes=UK)
                    s.op("act", lambda e: e.activation(out=Abuf[:, :], in_=Abuf[:, :], func=AF.Exp, scale=nsp[:, cn:cn + 1]),
                         reads=AK + ["nsp"], writes=AK)
                    s.op("dve", lambda e: e.tensor_scalar(out=Ubuf[:, :], in0=Ubuf[:, :], scalar1=-1.0, scalar2=1.0, op0=ALU.mult, op1=ALU.add),
                         reads=UK, writes=UK)
                    s.op("act", lambda e: e.activation(out=Ubuf[:, :], in_=Ubuf[:, :], func=AF.Sqrt), reads=UK, writes=UK)
                    s.op("dve", lambda e: e.tensor_tensor(out=Ubuf[:, :], in0=Ubuf[:, :], in1=Hd[:, :], op=ALU.mult),
                         reads=UK + HK(d), writes=UK)
                    if d == 0:
                        s.op("dve", lambda e: e.tensor_tensor_scan(out=Hd[:, :], data0=Abuf[:, :], data1=Ubuf[:, :], initial=0.0, op0=ALU.mult, op1=ALU.add),
                             reads=AK + UK + HK(d), writes=HK(d))
                    else:
                        s.op("dve", lambda e: e.tensor_tensor_scan(out=Hd[:, 0:CTX][:, ::-1], data0=Abuf[:, 0:CTX][:, ::-1], data1=Ubuf[:, 0:CTX][:, ::-1],
                                                                   initial=0.0, op0=ALU.mult, op1=ALU.add), reads=AK + UK + HK(d), writes=[("H", d, 0)])
                        s.op("dve", lambda e: e.tensor_tensor_scan(out=Hd[:, CTX:T][:, ::-1], data0=Abuf[:, CTX:T][:, ::-1], data1=Ubuf[:, CTX:T][:, ::-1],
                                                                   initial=Hd[:, 0:1], op0=ALU.mult, op1=ALU.add),
                             reads=AK + UK + HK(d), writes=HK(d)[1:])
                s.op("pool", lambda e: e.tensor_tensor(out=H0[:, :], in0=H0[:, :], in1=H1[:, :], op=ALU.add),
                     reads=HK(0) + HK(1), writes=HK(0))
                s.dma("sp", y_d[512 + ch * 128:512 + (ch + 1) * 128, :], H0[:, :], reads=HK(0), group=G("yC"))

        b.release(layer_mark)
        wM = b.alloc("wM", [128, 8, 4096], BF16)
        wB = b.alloc("wB", [128, 8, 1024], BF16)
        wO = b.alloc("wO", [128, 8, 1024], BF16)
        f_mark = b.mark()
        ka_pool = RPool(b, "ka", [128, T], BF16, 2)
        va_pool = RPool(b, "va", [128, 34, 128], BF16, 2)
        q_pool = RPool(b, "qsb", [128, 512], BF16, 3)
        qz_pools = [RPool(b, "qz%d" % g_, [128, 512], BF16, 3) for g_ in range(2)]
        for g_ in range(2):
            for i_, t_ in enumerate(qz_pools[g_].bufs):
                s.op("pool", lambda e: e.memset(t_[:, :], 0.0), writes=[(("qz%d" % g_, i_), "z")])
        qaz = [b.alloc("qaz%d" % hh_, [128, T], BF16) for hh_ in range(2)]
        for hh_ in range(2):
            s.op("pool", lambda e: e.memset(qaz[hh_][:, :], 0.0), writes=[("qaz", hh_, "z")])
        pT_pool = RPool(b, "pT", [128, 512], BF16, 4)
        rd_pool = RPool(b, "rden", [64, 512], F32, 2)
        yo_pool = RPool(b, "yo", [64, 512], F32, 2)
        yoA_pool = RPool(b, "yoA", [64, 512], F32, 2)
        nab_sb = b.alloc("nab_sb", [128, 4, NV, 64], BF16)
        for h in range(4):
            s.dma("pool", nab_sb[:, h, :, :], nab[l, h].rearrange("v p q -> p v q"), writes=["nab_sb"], group=G("nab"))

        if "F" in PH:
            for kc in range(8):
                for c4 in range(4):
                    s.dma("pool", wM[:, kc, c4 * 1024:(c4 + 1) * 1024], w_in[l, kc * 128:(kc + 1) * 128, 2976 + c4 * 1024:2976 + (c4 + 1) * 1024],
                          writes=[("wM", kc)], group=G("wM"))
                s.dma("pool", wB[:, kc, :], w_br[l, kc * 128:(kc + 1) * 128, :], writes=[("wB", kc)], group=G("wB"))
                s.dma("pool", wO[:, kc, :], w_out[l, kc * 128:(kc + 1) * 128, :], writes=[("wO", kc)], group=G("wO"))
        ALLK = list(range(34))
        lat_blocks = [(256 + 512 * j, 512, ALLK) for j in range(8)]
        ctx_blocks = [(0, 256, [0, 1])] if need_ctx else []
        QB = ctx_blocks + lat_blocks

        jobs = []
        if "B" in PH:
            for h in range(4):
                jobs.append(dict(kind="full", k_src=kB_d[h], krows=96, v=4 + h,
                                 runs=[(qB_d[h], 96, 0, 256 + h * 64, None)]))
        if "D" in PH:
            for g_ in range(2):
                jobs.append(dict(kind="full", k_src=kD_d[:, :], krows=128, v=8 + g_,
                                 runs=[(qD_d[h * 64:(h + 1) * 64, :], 64, g_ * 64, 768 + h * 64, qz_pools[g_]) for h in (2 * g_, 2 * g_ + 1)]))
        if "A" in PH:
            for h in range(4):
                jobs.append(dict(kind="na", k_src=kA_d[(h // 2) * 128:(h // 2 + 1) * 128, :], krows=128, v=h, h=h))

        def issue_loads(job):
            ka, kak = ka_pool.next()
            va, vak = va_pool.next()
            s.dma("sp", ka[0:job["krows"], :], job["k_src"], writes=[kak], group=G("ka", kak))
            s.dma("sp", va[:, :, :], vaug_d[job["v"]], writes=[vak], group=G("va", vak))
            job["ka"], job["kak"], job["va"], job["vak"] = ka, kak, va, vak
            if job["kind"] == "na":
                h = job["h"]
                hh = h % 2
                s.dma("sp", qaz[hh][hh * 64:(hh + 1) * 64, :], qA_d[h * 64:(h + 1) * 64, :], reads=[("qaz", hh, "z")], writes=[("qaz", hh)], group=G("qa", hh))

        nrm_pool = RPool(b, "nrm", [128, 512], F32, 2)

        def full_stream(fjobs):
            runs = []
            for job in fjobs:
                for ri_, run in enumerate(job["runs"]):
                    runs.append((job, ri_ == 0) + tuple(run))
            items = [(ri, bi_, n_) for ri in range(len(runs)) for bi_, (_, _, kts) in enumerate(QB) for n_ in range(len(kts))]
            qsl, psos, infl = {}, {}, {}
            LOOK = 3

            def load_q(ri, bi_):
                job, first, q_src, dk, kb, yrow0, qp = runs[ri]
                tq0, nq, _ = QB[bi_]
                qs, qk = (qp or q_pool).next()
                s.dma("sp", qs[kb:kb + dk, 0:nq], q_src[:, tq0:tq0 + nq], reads=[(qk, "z")], writes=[qk], group=G("q", qk))
                qsl[(ri, bi_)] = (qs, qk)

            load_q(0, 0)
            for it in range(len(items) + LOOK):
                if it < len(items):
                    ri, bi_, n_ = items[it]
                    job, first, q_src, dk, kb, yrow0, qp = runs[ri]
                    ka, kak, va, vak = job["ka"], job["kak"], job["va"], job["vak"]
                    k0_, k1_ = (0, 128) if qp is not None else (kb, kb + dk)
                    tq0, nq, kts = QB[bi_]
                    if n_ == 0:
                        psos[(ri, bi_)] = b.ps()
                        if bi_ + 1 < len(QB):
                            load_q(ri, bi_ + 1)
                        elif ri + 1 < len(runs):
                            load_q(ri + 1, 0)
                    qs, qk = qsl[(ri, bi_)]
                    kt = kts[n_]
                    pss, pks = b.ps()
                    while any(pks == pk_ for (_, pk_) in psos.values()):
                        pss, pks = b.ps()
                    s.op("pe", lambda e: e.matmul(pss[:, 0:nq], lhsT=ka[k0_:k1_, kt * 128:(kt + 1) * 128], rhs=qs[k0_:k1_, 0:nq],
                                                  start=True, stop=True), reads=[kak, qk, (qk, "z")], writes=[pks])
                    infl[it] = (pss, pks)
                m_ = it - LOOK
                if m_ >= 0:
                    ri, bi_, n_ = items[m_]
                    job, first, q_src, dk, kb, yrow0, qp = runs[ri]
                    va, vak = job["va"], job["vak"]
                    tq0, nq, kts = QB[bi_]
                    kt = kts[n_]
                    if n_ == 0 and bi_ == 0 and first and job["next"] is not None:
                        issue_loads(job["next"])
                    pss, pks = infl.pop(m_)
                    pso, pko = psos[(ri, bi_)]
                    pT, pTk = pT_pool.next()
                    s.op("act", lambda e: e.activation(out=pT[:, 0:nq], in_=pss[:, 0:nq], func=AF.Exp), reads=[pks], writes=[pTk])
                    s.op("pe", lambda e: e.matmul(pso[:, 0:nq], lhsT=va[:, kt, :], rhs=pT[:, 0:nq],
                                                  start=(n_ == 0), stop=(n_ == len(kts) - 1)), reads=[vak, pTk], writes=[pko])
                    if n_ == len(kts) - 1:
                        nr, nrk = nrm_pool.next()
                        s.op("dve", lambda e: e.tensor_copy(out=nr[:, 0:nq], in_=pso[:, 0:nq]), reads=[pko], writes=[nrk])
                        rd, rdk = rd_pool.next()
                        s.op("dve", lambda e: e.reciprocal(out=rd[0:64, 0:nq], in_=nr[64:128, 0:nq]), reads=[nrk], writes=[rdk])
                        yo, yok = yo_pool.next()
                        s.op("dve", lambda e: e.tensor_tensor(out=yo[0:64, 0:nq], in0=nr[0:64, 0:nq], in1=rd[0:64, 0:nq], op=ALU.mult),
                             reads=[nrk, rdk], writes=[yok])
                        s.dma("sp", y_d[yrow0:yrow0 + 64, tq0:tq0 + nq], yo[0:64, 0:nq], reads=[yok], group=G("yst", yok))
                        del psos[(ri, bi_)]

        def na_attention(job):
            ka, kak, va, vak = job["ka"], job["kak"], job["va"], job["vak"]
            h = job["h"]
            hh = h % 2
            groups = []
            if need_ctx:
                groups.append([("c", j) for j in range(4)])
            for g8 in range(8):
                groups.append([("l", g8 * 8 + j) for j in range(8)])
            flat = []
            for grp_rows in groups:
                for gi_, (kind, r) in enumerate(grp_rows):
                    flat.append((kind, r, gi_, len(grp_rows), grp_rows))
            st1, cur = {}, {}

            def stage1(i):
                kind, r, gi_, ng, grp_rows = flat[i]
                if kind == "c":
                    tq0 = r * 64
                    tiles = [(0, None), (1, None)]
                else:
                    tq0 = CTX + r * 64
                    tiles = [(0, None), (1, None)] + [(2 + g // 2, v) for (g, v) in NA_PLAN[r]]
                pss, pks = b.ps()
                for j, (kt, v) in enumerate(tiles):
                    s.op("pe", lambda e: e.matmul(pss[:, j * 64:(j + 1) * 64], lhsT=ka[:, kt * 128:(kt + 1) * 128],
                                                  rhs=qaz[hh][:, tq0:tq0 + 64], start=True, stop=(v is None)),
                         reads=[kak, ("qaz", hh), ("qaz", hh, "z")], writes=[pks])
                    if v is not None:
                        s.op("pe", lambda e: e.matmul(pss[:, j * 64:(j + 1) * 64], lhsT=ident_bf[:, :], rhs=nab_sb[:, h, v, :],
                                                      start=False, stop=True), reads=["ident_bf", "nab_sb"], writes=[pks])
                st1[i] = (pss, pks, tiles, tq0)

            def stage2(i):
                kind, r, gi_, ng, grp_rows = flat[i]
                pss, pks, tiles, tq0 = st1.pop(i)
                nk = len(tiles)
                if gi_ == 0:
                    cur["yo"] = yoA_pool.next()
                yo, yok = cur["yo"]
                pT, pTk = pT_pool.next()
                s.op("act", lambda e: e.activation(out=pT[:, 0:nk * 64], in_=pss[:, 0:nk * 64], func=AF.Exp), reads=[pks], writes=[pTk])
                pso, pko = b.ps()
                for j, (kt, v) in enumerate(tiles):
                    s.op("pe", lambda e: e.matmul(pso[:, 0:64], lhsT=va[:, kt, :], rhs=pT[:, j * 64:(j + 1) * 64],
                                                  start=(j == 0), stop=(j == nk - 1)), reads=[vak, pTk], writes=[pko])
                rd, rdk = rd_pool.next()
                s.op("dve", lambda e: e.reciprocal(out=rd[0:64, 0:64], in_=pso[64:128, 0:64]), reads=[pko], writes=[rdk])
                c0 = gi_ * 64
                s.op("dve", lambda e: e.tensor_tensor(out=yo[0:64, c0:c0 + 64], in0=pso[0:64, 0:64], in1=rd[0:64, 0:64], op=ALU.mult),
                     reads=[pko, rdk], writes=[(yok, gi_)])
                if gi_ == ng - 1:
                    k0, r0_ = grp_rows[0]
                    grp_t0 = r0_ * 64 if k0 == "c" else CTX + r0_ * 64
                    s.dma("sp", y_d[h * 64:(h + 1) * 64, grp_t0:grp_t0 + ng * 64], yo[0:64, 0:ng * 64],
                          reads=[(yok, n_) for n_ in range(ng)], group=G("yst", yok))

            NLOOK = 2
            for i in range(len(flat) + NLOOK):
                if i < len(flat):
                    stage1(i)
                if i - NLOOK >= 0:
                    stage2(i - NLOOK)

        for ji, job in enumerate(jobs):
            job["next"] = jobs[ji + 1] if ji + 1 < len(jobs) else None
        fjobs = [j_ for j_ in jobs if j_["kind"] == "full"]
        njobs = [j_ for j_ in jobs if j_["kind"] == "na"]
        if jobs:
            issue_loads(jobs[0])
        if fjobs:
            full_stream(fjobs)
        for job in njobs:
            if job["next"] is not None:
                issue_loads(job["next"])
            na_attention(job)

        b.release(f_mark)
        if "F" in PH:
            fxn_pool = RPool(b, "fxn", [128, 8, 512], BF16, 2)
            fy_pool = RPool(b, "fy", [128, 8, 512], F32, 1)
            fz_pool = RPool(b, "fz", [128, 8, 512], BF16, 1)
            fyg_pool = RPool(b, "fyg", [128, 8, 512], BF16, 1)
            acc_pool = RPool(b, "acc", [128, 512], F32, 2)
            sg_pool = RPool(b, "sg", [128, 512], F32, 2)
            tmp_pool = RPool(b, "ftmp", [128, 512], F32, 2)
            accT_pool = RPool(b, "accT", [128, 8, 512], BF16, 2)
            fx_pool = RPool(b, "fx", [128, 1024], F32, 1)
            fv_pool = RPool(b, "fv", [128, 1024], F32, 2)
            fst_pool = RPool(b, "fst", [128, 16], F32, 4)
            fblocks = [bi for bi in range(len(BLOCKS)) if not (bi == 0 and not need_ctx)]
            floaded = {}

            def f_load(bi):
                t0, nt = BLOCKS[bi]
                xn, xnk = fxn_pool.next()
                fy, fyk = fy_pool.next()
                fz, fzk = fz_pool.next()
                s.dma("sp", xn[:, :, 0:nt], xn_d.rearrange("(kc p) t -> p kc t", p=128)[:, :, t0:t0 + nt], writes=[xnk], group=G("fxn"))
                for half in range(2):
                    s.dma("sp", fy[:, half * 4:(half + 1) * 4, 0:nt], y_d.rearrange("(kc p) t -> p kc t", p=128)[:, half * 4:(half + 1) * 4, t0:t0 + nt],
                          writes=[fyk], group=G("fy"))
                s.dma("sp", fz[:, :, 0:nt], zs_d.rearrange("(kc p) t -> p kc t", p=128)[:, :, t0:t0 + nt], writes=[fzk], group=G("fz"))
                floaded[bi] = (xn, xnk, fy, fyk, fz, fzk)

            def make_epilogue(bi, accT, accTk):
                t0, nt = BLOCKS[bi]
                is_ctx = bi == 0
                gj = 1 if is_ctx else 0
                parts = []
                for i in range(nt // 128):
                    r0 = (t0 + i * 128) if is_ctx else (t0 - CTX + i * 128)
                    src = c_src if is_ctx else x_src
                    dst = ctx1_d if is_ctx else x_dst
                    box = {}

                    def part1(i=i, r0=r0, src=src, box=box):
                        fx, fxk = fx_pool.next()
                        s.dma("sp", fx[:, :], src[r0:r0 + 128, :], writes=[fxk], group=G("fx", fxk))
                        fv, fvk = fv_pool.next()
                        for half in range(2):
                            pso, pko = b.ps()
                            hs = slice(half * 512, (half + 1) * 512)
                            for fc in range(8):
                                s.op("pe", lambda e: e.matmul(pso[:, :], lhsT=accT[:, fc, i * 128:(i + 1) * 128], rhs=wO[:, fc, hs],
                                                              start=(fc == 0), stop=(fc == 7)), reads=[("wO", fc), (accTk, fc)], writes=[pko])
                            tm, tmk = tmp_pool.next()
                            s.op("dve", lambda e: e.tensor_tensor(out=tm[:, :], in0=pso[:, :], in1=gate_rep[:, gj, hs], op=ALU.mult),
                                 reads=[pko] + GATEK, writes=[tmk])
                            s.op("dve", lambda e: e.scalar_tensor_tensor(out=fv[:, hs], in0=fx[:, hs], scalar=ALPHA, in1=tm[:, :],
                                                                         op0=ALU.mult, op1=ALU.add), reads=[tmk, fxk], writes=[(fvk, half)])
                        st, stk = fst_pool.next()
                        s.op("dve", lambda e: e.bn_stats(out=st[:, 0:6], in_=fv[:, 0:512]), reads=[(fvk, 0)], writes=[(stk, 0)])
                        s.op("dve", lambda e: e.bn_stats(out=st[:, 6:12], in_=fv[:, 512:1024]), reads=[(fvk, 1)], writes=[(stk, 1)])
                        s.op("dve", lambda e: e.bn_aggr(out=st[:, 12:14], in_=st[:, 0:12].rearrange("p (a b) -> p a b", b=6)),
                             reads=[(stk, 0), (stk, 1)], writes=[(stk, 2)])
                        rstd(st[:, 14:15], st[:, 13:14], [(stk, 2)], (stk, 3))
                        s.op("dve", lambda e: e.tensor_scalar(out=fv[:, :], in0=fv[:, :], scalar1=st[:, 12:13], scalar2=st[:, 14:15],
                                                              op0=ALU.subtract, op1=ALU.mult), reads=[(fvk, 0), (fvk, 1), (stk, 2), (stk, 3)], writes=[(fvk, 0), (fvk, 1)])
                        box["fv"] = (fv, fvk)

                    def part2(r0=r0, dst=dst, box=box):
                        fv, fvk = box["fv"]
                        s.op("dve", lambda e: e.tensor_tensor(out=fv[:, :], in0=fv[:, :], in1=lng[:, :], op=ALU.mult), reads=[(fvk, 0), (fvk, 1), "lng"], writes=[(fvk, 0), (fvk, 1)])
                        s.op("dve", lambda e: e.tensor_tensor(out=fv[:, :], in0=fv[:, :], in1=lnb[:, :], op=ALU.add), reads=[(fvk, 0), (fvk, 1), "lnb"], writes=[(fvk, 0), (fvk, 1)])
                        s.dma("sp", dst[r0:r0 + 128, :], fv[:, :], reads=[(fvk, 0), (fvk, 1)], group=G("outf" if (l == n_layers - 1) else "x1", fvk))

                    parts += [part1, part2]
                return parts

            pending_epi = []
            fready = {}

            def f_yg(bi):
                nt_ = BLOCKS[bi][1]
                xn_, xnk_, fy, fyk, fz, fzk = floaded.pop(bi)
                fyg, fygk = fyg_pool.next()
                for c in range(8):
                    eng = "pool" if c % 2 else "dve"
                    s.op(eng, lambda e: e.tensor_tensor(out=fyg[:, c, 0:nt_], in0=fy[:, c, 0:nt_], in1=fz[:, c, 0:nt_], op=ALU.mult),
                         reads=[fyk, fzk], writes=[(fygk, c)])
                fready[bi] = (xn_, xnk_, fyg, fygk)

            f_load(fblocks[0])
            for fi_, bi in enumerate(fblocks):
                t0, nt = BLOCKS[bi]
                is_ctx = bi == 0
                ntile = nt // 128
                gj = 1 if is_ctx else 0
                if fi_ == 0:
                    f_yg(bi)
                xn, xnk, fyg, fygk = fready.pop(bi)
                accT, accTk = accT_pool.next()
                epi_parts = pending_epi
                pending_epi = []
                for fc in range(8):
                    if fc > 0 and epi_parts:
                        epi_parts.pop(0)()
                        if fc == 7:
                            while epi_parts:
                                epi_parts.pop(0)()
                    if fc == 4 and fi_ + 1 < len(fblocks):
                        f_load(fblocks[fi_ + 1])
                    acc, acck = acc_pool.next()
                    for i in range(4):
                        psm, pkm = b.ps()
                        for kc in range(8):
                            s.op("pe", lambda e, psm=psm, kc=kc, i=i, fc=fc, xn=xn: e.matmul(psm[:, 0:nt], lhsT=wM[:, kc, i * 1024 + fc * 128:i * 1024 + (fc + 1) * 128],
                                                                                            rhs=xn[:, kc, 0:nt], start=(kc == 0), stop=(kc == 7)),
                                 reads=[("wM", kc), xnk], writes=[pkm])
                        sg, sgk = sg_pool.next()
                        s.op("act", lambda e, psm=psm, sg=sg: e.activation(out=sg[:, 0:nt], in_=psm[:, 0:nt], func=AF.Sigmoid), reads=[pkm], writes=[sgk])
                        psb_, pkb = b.ps()
                        for c2 in range(2):
                            s.op("pe", lambda e, psb_=psb_, c2=c2, i=i, fc=fc, fyg=fyg: e.matmul(psb_[:, 0:nt], lhsT=wB[:, i * 2 + c2, fc * 128:(fc + 1) * 128],
                                                                                                rhs=fyg[:, i * 2 + c2, 0:nt], start=(c2 == 0), stop=(c2 == 1)),
                                 reads=[("wB", i * 2 + c2), (fygk, i * 2 + c2)], writes=[pkb])
                        if i == 0:
                            s.op("dve", lambda e, psb_=psb_, sg=sg, acc=acc: e.tensor_tensor(out=acc[:, 0:nt], in0=psb_[:, 0:nt], in1=sg[:, 0:nt], op=ALU.mult),
                                 reads=[pkb, sgk], writes=[acck])
                        else:
                            tm, tmk = tmp_pool.next()
                            s.op("dve", lambda e, psb_=psb_, sg=sg, tm=tm: e.tensor_tensor(out=tm[:, 0:nt], in0=psb_[:, 0:nt], in1=sg[:, 0:nt], op=ALU.mult),
                                 reads=[pkb, sgk], writes=[tmk])
                            if i < 3:
                                s.op("pool", lambda e, tm=tm, acc=acc: e.tensor_tensor(out=acc[:, 0:nt], in0=acc[:, 0:nt], in1=tm[:, 0:nt], op=ALU.add),
                                     reads=[tmk, acck], writes=[acck])
                            else:
                                s.op("pool", lambda e, tm=tm, acc=acc, fc=fc, accT=accT: e.tensor_tensor(out=accT[:, fc, 0:nt], in0=acc[:, 0:nt], in1=tm[:, 0:nt], op=ALU.add),
                                     reads=[tmk, acck], writes=[(accTk, fc)])
                if fi_ + 1 < len(fblocks):
                    f_yg(fblocks[fi_ + 1])
                pending_epi = make_epilogue(bi, accT, accTk)
            for part in pending_epi:
                part()
        s.barrier()

    s.finish(final_groups=[g for g in ("outf:('fv', 0)", "outf:('fv', 1)")])
    return nc, s


def _prep_shared(inp):
    f = np.float32
    L = DEPTH
    p64, p32 = _perm(64, 16), _perm(32, 8)
    w_in = np.ascontiguousarray(inp["w_in"], dtype=f)
    dq = np.concatenate([1440 + h * 64 + p64 for h in range(4)])
    dk = np.concatenate([1696 + h * 64 + p64 for h in range(2)])
    kr = 1152 + p32
    w_insw = np.ascontiguousarray(w_in[:, :, np.concatenate([dq, dk, kr])])
    w_uq = np.ascontiguousarray(inp["mla_w_uq"], dtype=f)
    uq_idx = np.concatenate([np.concatenate([h * 96 + np.arange(64), h * 96 + 64 + p32]) for h in range(4)])
    w_uqsw = np.ascontiguousarray(w_uq[:, :, uq_idx])
    ukv = np.asarray(inp["mla_w_ukv"], dtype=f).reshape(L, 128, 4, 128)
    w_ukv = np.ascontiguousarray(np.concatenate([ukv[:, :, :, :64].reshape(L, 128, 256), ukv[:, :, :, 64:].reshape(L, 128, 256)], -1))
    w_br = np.ascontiguousarray(np.asarray(inp["w_branch"], dtype=f).reshape(L, 1024, 1024))
    lruw = np.zeros((L, 2, 2, 2, 128, 128), f)
    for gi, nm in enumerate(("lru_w_a", "lru_w_x")):
        w = np.asarray(inp[nm], dtype=f)
        for ch in range(2):
            for k2 in range(2):
                lruw[:, :, gi, ch, k2 * 64:(k2 + 1) * 64, k2 * 64:(k2 + 1) * 64] = w[:, :, ch * 2 + k2]
    cols = np.zeros((L, 128, NCOL), f)
    bm = np.asarray(inp["b_mod"], dtype=f)
    cols[:, :, 0:16] = bm[:, 0:2048].reshape(L, 16, 128).transpose(0, 2, 1)
    cols[:, :, 16:18] = np.asarray(inp["mla_q_norm"], f).reshape(L, 2, 128).transpose(0, 2, 1)
    cols[:, :, 18] = np.asarray(inp["mla_kv_norm"], f)
    gq = np.asarray(inp["gqa_q_norm"], f)
    gk = np.asarray(inp["gqa_k_norm"], f)
    cols[:, :, 19] = np.tile(gq, (1, 2))
    cols[:, :, 20] = np.tile(gq[:, p64], (1, 2))
    cols[:, :, 21] = np.tile(gk, (1, 2))
    cols[:, :, 22] = np.tile(gk[:, p64], (1, 2))
    cw = np.asarray(inp["lru_conv_w"], f)
    for ch in range(2):
        for j in range(4):
            cols[:, :, 23 + ch * 4 + j] = cw[:, j, ch * 128:(ch + 1) * 128]
        cols[:, :, 31 + ch] = np.asarray(inp["lru_conv_b"], f)[:, ch * 128:(ch + 1) * 128]
        for d in range(2):
            cols[:, :, 33 + d * 2 + ch] = np.asarray(inp["lru_b_a"], f)[:, d, ch * 128:(ch + 1) * 128]
            cols[:, :, 37 + d * 2 + ch] = np.asarray(inp["lru_b_x"], f)[:, d, ch * 128:(ch + 1) * 128]
            cols[:, :, 41 + d * 2 + ch] = np.asarray(inp["lru_lambda"], f)[:, d, ch * 128:(ch + 1) * 128]
    rep = np.zeros((L, 3, 128, 1024), f)
    rep[:, 0] = bm[:, None, 2048:3072]
    rep[:, 1] = np.asarray(inp["ln_g"], f)[:, None, :]
    rep[:, 2] = np.asarray(inp["ln_b"], f)[:, None, :]
    tabD, tabB, tabK = _rope_tables()
    nab = np.stack([_na_bias_tiles(np.asarray(inp["na_rel_bias"], f)[l]) for l in range(L)])
    cst = np.zeros((4, 128, 128), f)
    cst[0] = np.eye(128, dtype=f)
    cst[1] = 1.0 / 256
    cst[2] = 1.0 / 128
    cst[3, :64, :64] = 1.0 / 64
    cst[3, 64:, 64:] = 1.0 / 64
    return dict(w_mod=np.ascontiguousarray(inp["w_mod"], dtype=f), w_in=w_in, w_insw=w_insw, w_uq=w_uq, w_uqsw=w_uqsw,
                w_ukv=w_ukv, w_br=w_br, w_out=np.ascontiguousarray(inp["w_out"], dtype=f), lruw=lruw, cols=cols, rep=rep,
                tabD=tabD, tabB=tabB, tabK=tabK, nab=nab, cst=cst)


def make_in_maps(inp, n_cores=8):
    shared = _prep_shared(inp)
    x = np.asarray(inp["x"], np.float32)
    ctx = np.asarray(inp["ctx"], np.float32)
    c = np.asarray(inp["c"], np.float32)
    cc = np.asarray(inp["c_ctx"], np.float32)
    maps = []
    for bidx in range(n_cores):
        cv = np.stack([c[bidx].reshape(8, 128).T, cc.reshape(8, 128).T], -1).astype(np.float32)
        m = dict(shared)
        m["x"] = np.ascontiguousarray(x[bidx])
        m["ctx"] = np.ascontiguousarray(ctx[bidx])
        m["cv"] = np.ascontiguousarray(cv)
        maps.append(m)
    return maps


def kernel(**inputs):
    nc, _ = build_program(debug=False)
    maps = make_in_maps(inputs, 8)
    res = run_bass_kernel_spmd(nc, maps, core_ids=list(range(8)))
    return np.stack([np.asarray(r["out"], np.float32) for r in res.results], 0)
```

```python
from contextlib import ExitStack
import numpy as np
import ml_dtypes
import concourse.bass as bass
import concourse.mybir as mybir
from concourse.bass_utils import run_bass_kernel_spmd

F32 = mybir.dt.float32
BF16 = mybir.dt.bfloat16
AF = mybir.ActivationFunctionType
ALU = mybir.AluOpType
AX = mybir.AxisListType

D_MODEL = 1024
SEQ = 4096
CTX = 256
T = SEQ + CTX
DEPTH = 2
GRID_W = 64
WIN_H, WIN_W = 8, 16
EPS = 1e-6
THETA = 10000.0
NA_SCALE = 64 ** -0.5
MLA_SCALE = 96 ** -0.5
GQA_SCALE = 64 ** -0.5
ALPHA = (2 * DEPTH) ** 0.25
MIX_COLS = 1952
N_IN = 7072
NEG = -200.0
BLOCKS = [(0, 256)] + [(256 + 512 * j, 512) for j in range(8)]
NCOL = 45


STRICT_SAME_ENGINE = True


class _Op:
    __slots__ = ("eng", "fn", "reads", "writes", "group", "idx", "waits", "sig", "is_dma", "gval", "bar")


class _Rec:
    def __init__(self):
        self.call = None

    def __getattr__(self, name):
        def f(*a, **k):
            self.call = (name, a, k)
        return f


class Sched:
    def __init__(self, nc):
        self.nc = nc
        self.ops = []
        self.stack = ExitStack()

    def op(self, eng, fn, reads=(), writes=()):
        o = _Op()
        if fn is not None:
            rec = _Rec()
            fn(rec)
            assert rec.call is not None
            fn = rec.call
        o.eng, o.fn, o.reads, o.writes = eng, fn, tuple(reads), tuple(writes)
        o.group, o.is_dma, o.sig, o.waits, o.gval, o.bar = None, False, False, [], 0, False
        self.ops.append(o)
        return o

    def dma(self, eng, out, in_, reads=(), writes=(), group=None):
        o = self.op(eng, lambda e: e.dma_start(out=out, in_=in_), reads, writes)
        o.is_dma, o.group = True, group
        assert group is not None
        return o

    def barrier(self):
        o = self.op("*", None)
        o.bar = True

    def finish(self, final_groups):
        nc, ops = self.nc, self.ops
        ENGS = ["pe", "act", "dve", "pool", "sp"]
        last_writer, readers = {}, {}
        eng_ops = {e: [] for e in ENGS}
        last_comp = {e: 0 for e in ENGS}
        grp_cnt = {}
        known = {e: {} for e in ENGS}
        pending = {e: [] for e in ENGS}
        for gi, o in enumerate(ops):
            if o.bar:
                for e in ENGS:
                    w = []
                    for e2 in ENGS:
                        if (e2 != e or e != "pe") and last_comp[e2] > 0:
                            w.append((("e", e2), last_comp[e2]))
                    for g, c in grp_cnt.items():
                        w.append((("g", g), c))
                    pending[e] = w
                last_writer, readers = {}, {}
                continue
            lst = eng_ops[o.eng]
            lst.append(o)
            o.idx = len(lst)
            raw, deps = set(), set()
            for r in o.reads:
                if r in last_writer:
                    deps.add(last_writer[r]); raw.add(last_writer[r])
            for w in o.writes:
                if w in last_writer:
                    deps.add(last_writer[w])
                for rd in readers.get(w, ()):
                    deps.add(rd)
            deps.discard(gi)
            need = {}
            for key, val in pending[o.eng]:
                if val > need.get(key, 0):
                    need[key] = val
            pending[o.eng] = []
            for d in deps:
                p = ops[d]
                if p.is_dma:
                    key, val = ("g", p.group), grp_cnt[p.group]
                else:
                    if p.eng == o.eng and not o.is_dma:
                        if o.eng == "pe" or (d not in raw and not STRICT_SAME_ENGINE):
                            continue
                    key, val = ("e", p.eng), p.idx
                if val > need.get(key, 0):
                    need[key] = val
            kn = known[o.eng]
            for key, val in need.items():
                if kn.get(key, 0) >= val:
                    continue
                kn[key] = val
                o.waits.append((key, val))
                if key[0] == "e":
                    eng_ops[key[1]][val - 1].sig = True
            if o.is_dma:
                grp_cnt[o.group] = grp_cnt.get(o.group, 0) + 1
                o.gval = grp_cnt[o.group]
            else:
                last_comp[o.eng] = o.idx
            for r in o.reads:
                readers.setdefault(r, []).append(gi)
            for w in o.writes:
                last_writer[w] = gi
                readers[w] = []
        waited = {}
        for lst in eng_ops.values():
            for o in lst:
                for key, val in o.waits:
                    if key[0] == "g":
                        waited.setdefault(key[1], set()).add(val)
        for eng, lst in eng_ops.items():
            seen = {}
            for o in lst:
                for key, val in o.waits:
                    if key[0] == "g":
                        seen[key[1]] = max(seen.get(key[1], 0), val)
                if o.is_dma and o.gval > 1 and (o.gval - 1) in waited.get(o.group, ()) and seen.get(o.group, 0) < o.gval - 1:
                    o.waits.append((("g", o.group), o.gval - 1))
                    seen[o.group] = o.gval - 1
        sigcnt = {}
        for eng, lst in eng_ops.items():
            c, arr = 0, []
            for o in lst:
                if o.sig and not o.is_dma:
                    c += 1
                arr.append(c)
            sigcnt[eng] = arr
        st = self.stack
        esem = {eng: st.enter_context(nc.semaphore("s_" + eng)) for eng in ENGS}
        gsem = {g: st.enter_context(nc.semaphore("g_%d" % i)) for i, g in enumerate(grp_cnt)}
        self.n_sems = len(esem) + len(gsem)
        self.stats = {e: len(l) for e, l in eng_ops.items()}
        self.maxsem = {e: (a[-1] if a else 0) for e, a in sigcnt.items()}
        self.maxgrp = max(grp_cnt.values()) if grp_cnt else 0
        block = st.enter_context(nc.Block())
        attach = {"pe": block.tensor, "act": block.scalar, "dve": block.vector,
                  "pool": block.gpsimd, "sp": block.sync}

        def emit(eng, lst):
            def body(e):
                for o in lst:
                    for key, val in o.waits:
                        if key[0] == "e":
                            e.wait_ge(esem[key[1]], sigcnt[key[1]][val - 1])
                        else:
                            e.wait_ge(gsem[key[1]], 16 * val)
                    name, a, k = o.fn
                    ins = getattr(e, name)(*a, **k)
                    if o.is_dma:
                        ins.then_inc(gsem[o.group], 16)
                    elif o.sig:
                        ins.then_inc(esem[eng], 1)
                if eng == "sp":
                    for g in final_groups:
                        if g in gsem:
                            e.wait_ge(gsem[g], 16 * grp_cnt[g])
            attach[eng](body)

        for eng in ENGS:
            emit(eng, eng_ops[eng])
        st.close()


class Builder:
    def __init__(self, nc):
        self.nc = nc
        self.s = Sched(nc)
        self.lo = 16384
        self.p = self.lo
        self.hi = 229300
        self.cnt = 0
        self.psb = [nc.alloc_psum_tensor("psb%d" % i, [128, 512], F32) for i in range(8)]
        self.psi = 0

    def alloc(self, name, shape, dtype):
        n = 1
        for d in shape[1:]:
            n *= d
        nbytes = n * (2 if dtype == BF16 else 4)
        nbytes = (nbytes + 63) // 64 * 64
        self.cnt += 1
        t = self.nc.alloc_sbuf_tensor_at("%s_%d" % (name, self.cnt), list(shape), dtype, offset=self.p)
        self.p += nbytes
        assert self.p <= self.hi, ("SBUF overflow", name, self.p)
        return t

    def mark(self):
        return self.p

    def release(self, m):
        self.s.barrier()
        self.p = m

    def ps(self):
        i = self.psi % 8
        self.psi += 1
        return self.psb[i], ("psb", i)


class RPool:
    def __init__(self, b, name, shape, dtype, n):
        self.name = name
        self.bufs = [b.alloc(name, shape, dtype) for _ in range(n)]
        self.i = 0

    def next(self):
        k = self.i % len(self.bufs)
        self.i += 1
        return self.bufs[k], (self.name, k)


def _perm(width, half):
    idx = np.arange(width)
    return np.where((idx % (2 * half)) < half, idx + half, idx - half)


def _na_plan():
    variants, plan = {}, []
    n_rows = SEQ // GRID_W
    for r in range(n_rows):
        rs = int(np.clip(r - WIN_H // 2, 0, n_rows - WIN_H))
        g0 = rs & ~1
        tiles = []
        g = g0
        while g < rs + WIN_H:
            top = (g - r) if rs <= g < rs + WIN_H else None
            bot = (g + 1 - r) if rs <= g + 1 < rs + WIN_H else None
            key = (top, bot)
            if key not in variants:
                variants[key] = len(variants)
            tiles.append((g, variants[key]))
            g += 2
        plan.append(tiles)
    return variants, plan


NA_VARIANTS, NA_PLAN = _na_plan()
NV = len(NA_VARIANTS)


def _na_bias_tiles(rel_bias):
    H = rel_bias.shape[0]
    out = np.full((H, NV, 128, GRID_W), NEG, np.float32)
    qc = np.arange(GRID_W)
    cs = np.clip(qc - WIN_W // 2, 0, GRID_W - WIN_W)
    kc = np.arange(GRID_W)
    valid = (kc[:, None] >= cs[None, :]) & (kc[:, None] < cs[None, :] + WIN_W)
    dc = kc[:, None] - qc[None, :] + WIN_W - 1
    dc_c = np.clip(dc, 0, 2 * WIN_W - 2)
    for (top, bot), v in NA_VARIANTS.items():
        for half, d in ((0, top), (1, bot)):
            if d is None:
                continue
            g = rel_bias[:, d + WIN_H - 1, :][:, dc_c]
            out[:, v, half * 64:(half + 1) * 64, :] = np.where(valid[None], g, np.float32(NEG))
    return out


def _rope_tables():
    t = np.arange(SEQ)
    rows = (t // GRID_W).astype(np.float64)
    cols = (t % GRID_W).astype(np.float64)

    def tab(d_head_rot):
        half = d_head_rot // 2
        nf = half // 2
        inv = THETA ** (-np.arange(0, half, 2, dtype=np.float64) / half)
        C = np.ones((d_head_rot, T), np.float64)
        S = np.zeros((d_head_rot, T), np.float64)
        for ax, pos in enumerate((rows, cols)):
            ang = pos[None, :] * inv[:, None]
            c, s = np.cos(ang), np.sin(ang)
            b0 = ax * half
            C[b0:b0 + nf, CTX:] = c
            C[b0 + nf:b0 + half, CTX:] = c
            S[b0:b0 + nf, CTX:] = -s
            S[b0 + nf:b0 + half, CTX:] = s
        return C.astype(np.float32), S.astype(np.float32)

    C64, S64 = tab(64)
    C32, S32 = tab(32)
    tabD = np.stack([np.concatenate([C64, C64], 0), np.concatenate([S64, S64], 0)])
    CB = np.concatenate([np.ones((64, T), np.float32), C32], 0)
    SB = np.concatenate([np.zeros((64, T), np.float32), S32], 0)
    tabB = np.stack([CB, SB])
    tabK = np.stack([C32, S32])
    return tabD, tabB, tabK


def build_program(debug=False, PH="MPCABDF", n_layers=DEPTH):
    nc = bass.Bass("TRN2", target_bir_lowering=False)
    b = Builder(nc)
    s = b.s

    def din(name, shape, dt=F32):
        return nc.dram_tensor(name, list(shape), dt, kind="ExternalInput").ap()

    def dscr(name, shape, dt, out=False):
        return nc.dram_tensor(name, list(shape), dt, kind=("ExternalOutput" if out else "Internal")).ap()

    x_in = din("x", [SEQ, D_MODEL])
    ctx_in = din("ctx", [CTX, D_MODEL])
    cv_in = din("cv", [128, 8, 2])
    w_mod = din("w_mod", [DEPTH, 1024, 3072])
    w_in = din("w_in", [DEPTH, 1024, N_IN])
    w_insw = din("w_insw", [DEPTH, 1024, 416])
    w_uq = din("w_uq", [DEPTH, 256, 384])
    w_uqsw = din("w_uqsw", [DEPTH, 256, 384])
    w_ukv = din("w_ukv", [DEPTH, 128, 512])
    w_br = din("w_br", [DEPTH, 1024, 1024])
    w_out = din("w_out", [DEPTH, 1024, 1024])
    lruw = din("lruw", [DEPTH, 2, 2, 2, 128, 128])
    cols_in = din("cols", [DEPTH, 128, NCOL])
    rep_in = din("rep", [DEPTH, 3, 128, 1024])
    tabD = din("tabD", [2, 128, T])
    tabB = din("tabB", [2, 96, T])
    tabK = din("tabK", [2, 32, T])
    nab = din("nab", [DEPTH, 4, NV, 128, 64])
    cst = din("cst", [4, 128, 128])
    out_d = dscr("out", [SEQ, D_MODEL], F32, out=True)

    dbg = debug
    xn_d = dscr("xn_d", [1024, T], BF16, dbg)
    zs_d = dscr("zs_d", [1024, T], BF16, dbg)
    y_d = dscr("y_d", [1024, T], F32, dbg)
    qA_d = dscr("qA_d", [256, T], BF16, dbg)
    kA_d = dscr("kA_d", [256, T], BF16, dbg)
    vaug_d = dscr("vaug_d", [10, 128, 34, 128], BF16, dbg)
    qB_d = dscr("qB_d", [4, 96, T], BF16, dbg)
    kB_d = dscr("kB_d", [4, 96, T], BF16, dbg)
    qD_d = dscr("qD_d", [256, T], BF16, dbg)
    kD_d = dscr("kD_d", [128, T], BF16, dbg)
    x1_d = dscr("x1_d", [SEQ, D_MODEL], F32, dbg)
    ctx1_d = dscr("ctx1_d", [CTX, D_MODEL], F32, dbg)

    gcount = [0]

    def G(name, key=None):
        return name if key is None else "%s:%s" % (name, key)

    ident = b.alloc("ident", [128, 128], F32)
    ones256 = b.alloc("ones256", [128, 128], F32)
    ones128 = b.alloc("ones128", [128, 128], F32)
    bones64 = b.alloc("bones64", [128, 128], F32)
    ident_bf = b.alloc("ident_bf", [128, 128], BF16)
    for j, tt in enumerate((ident, ones256, ones128, bones64)):
        s.dma("sp", tt[:], cst[j], writes=[("cst", j)], group="cst")
    s.dma("pool", ident_bf[:], cst[0], writes=["ident_bf"], group="cstbf")
    cvs = b.alloc("cvs", [128, 8, 2], F32)
    scv = b.alloc("scv", [128, 8, 2], F32)
    s.dma("sp", cvs[:], cv_in, writes=["cvs"], group="cvs")
    s.op("act", lambda e: e.activation(out=scv[:], in_=cvs[:], func=AF.Silu), reads=["cvs"], writes=["scv"])
    CSTK = [("cst", j) for j in range(4)]
    epsc = b.alloc("epsc", [128, 1], F32)
    s.op("pool", lambda e: e.memset(epsc[:, :], EPS), writes=["epsc"])

    def rstd(out_ap, in_ap, reads, wkey):
        np_ = out_ap.shape[0]
        s.op("act", lambda e: e.activation(out=out_ap, in_=in_ap, func=AF.Sqrt, bias=epsc[0:np_, 0:1]), reads=list(reads) + ["epsc"], writes=[wkey])
        s.op("dve", lambda e: e.reciprocal(out=out_ap, in_=out_ap), reads=[wkey], writes=[wkey])

    base_mark = b.mark()
    onesb = b.alloc("onesb", [128, 34, 128], BF16)
    s.op("pool", lambda e: e.memset(onesb[:, :, :], 1.0), writes=["onesb"])
    for hh_ in range(10):
        s.dma("sp", vaug_d[hh_], onesb[:, :, :], reads=["onesb"], group="onesinit")

    for l in range(n_layers):
        need_ctx = l < DEPTH - 1
        x_src = x_in if l == 0 else x1_d
        c_src = ctx_in if l == 0 else ctx1_d
        x_dst = x1_d if l < DEPTH - 1 else out_d
        b.release(base_mark)
        cols = b.alloc("cols", [128, NCOL], F32)
        s.dma("sp", cols[:], cols_in[l], writes=["cols"], group=G("cols"))
        modfm = b.alloc("modfm", [128, 16, 2], F32)
        gate_rep = b.alloc("gate_rep", [128, 2, 1024], F32)
        lng = b.alloc("lng", [128, 1024], F32)
        lnb = b.alloc("lnb", [128, 1024], F32)
        s.dma("sp", lng[:], rep_in[l, 1], writes=["lng"], group=G("lng"))
        s.dma("sp", lnb[:], rep_in[l, 2], writes=["lnb"], group=G("lnb"))
        nsp = b.alloc("nsp", [128, 8], F32)
        layer_mark = b.mark()

        xrT = b.alloc("xrT", [128, 2, T], F32)
        pc_mark = b.mark()
        NP = 3392
        wP = b.alloc("wP", [128, 8, NP], BF16)
        for kc in range(8):
            for (c0, c1) in ((0, 1024), (1024, 2048), (2048, 2976)):
                s.dma("pool", wP[:, kc, c0:c1], w_in[l, kc * 128:(kc + 1) * 128, c0:c1], writes=[("wP", kc)], group=G("wP"))
            s.dma("pool", wP[:, kc, 2976:NP], w_insw[l, kc * 128:(kc + 1) * 128, :], writes=[("wP", kc)], group=G("wP"))
        WPK = [("wP", kc) for kc in range(8)]
        wuq = b.alloc("wuq", [128, 2, 384], BF16)
        wuqs = b.alloc("wuqs", [128, 2, 384], BF16)
        wukv = b.alloc("wukv", [128, 512], BF16)
        for ch in range(2):
            s.dma("pool", wuq[:, ch, :], w_uq[l, ch * 128:(ch + 1) * 128, :], writes=["wuq"], group=G("wuq"))
            s.dma("pool", wuqs[:, ch, :], w_uqsw[l, ch * 128:(ch + 1) * 128, :], writes=["wuqs"], group=G("wuq"))
        s.dma("pool", wukv[:, :], w_ukv[l], writes=["wukv"], group=G("wuq"))

        m_mark = b.mark()
        wm_pool = RPool(b, "wm", [128, 8, 512], F32, 4)
        screp = b.alloc("screp", [128, 2, 8, 128], F32)
        bgate = b.alloc("bgate", [128, 1024], F32)
        s.dma("sp", bgate[:], rep_in[l, 0], writes=["bgate"], group=G("bgate"))
        for j in range(2):
            for kc in range(8):
                s.op("act", lambda e, j=j, kc=kc: e.activation(out=screp[:, j, kc, :], in_=ones256[:, :], func=AF.Identity,
                                                                 scale=scv[:, kc, j:j + 1]),
                     reads=["scv", ("cst", 1)], writes=[("screp", j, kc)])
        for grp in range(6):
            wm, wmk = wm_pool.next()
            for kh in range(2):
                s.dma("sp", wm[:, kh * 4:(kh + 1) * 4, :],
                      w_mod[l].rearrange("(kc p) n -> p kc n", p=128)[:, kh * 4:(kh + 1) * 4, grp * 512:(grp + 1) * 512],
                      writes=[wmk], group=G("wm", wmk))
            if grp < 4:
                for q in range(4):
                    n = grp * 4 + q
                    ps, pk = b.ps()
                    for kc in range(8):
                        s.op("pe", lambda e, ps=ps, wm=wm, kc=kc, q=q: e.matmul(ps[:, 0:2], lhsT=wm[:, kc, q * 128:(q + 1) * 128],
                                                                                rhs=scv[:, kc, :], start=(kc == 0), stop=(kc == 7)),
                             reads=[wmk, "scv"], writes=[pk])
                    addc = 0.0 if n < 8 else 1.0
                    s.op("dve", lambda e, ps=ps, n=n, addc=addc: e.tensor_scalar(out=modfm[:, n, :], in0=ps[:, 0:2], scalar1=cols[:, n:n + 1],
                                                                                  scalar2=addc, op0=ALU.add, op1=ALU.add),
                         reads=[pk, "cols"], writes=[("modfm", n)])
            else:
                half = grp - 4
                for j in range(2):
                    if j == 1 and not need_ctx:
                        continue
                    ps, pk = b.ps()
                    for kc in range(8):
                        s.op("pe", lambda e, ps=ps, wm=wm, kc=kc, j=j: e.matmul(ps[:, :], lhsT=screp[:, j, kc, :], rhs=wm[:, kc, :],
                                                                                start=(kc == 0), stop=(kc == 7)),
                             reads=[wmk] + [("screp", j, kc)], writes=[pk])
                    s.op("dve", lambda e, ps=ps, j=j, half=half: e.scalar_tensor_tensor(
                        out=gate_rep[:, j, half * 512:(half + 1) * 512], in0=ps[:, :], scalar=256.0,
                        in1=bgate[:, half * 512:(half + 1) * 512], op0=ALU.mult, op1=ALU.add),
                         reads=[pk, "bgate"], writes=[("gate_rep", j, half)])
        MODK = [("modfm", n) for n in range(16)]
        GATEK = [("gate_rep", j, h) for j in range(2) for h in range(2)]
        sp = b.alloc("sp_tmp", [128, 8, 4], F32)
        lam = cols[:, 41:45]
        V = lambda i: sp[:, i, :]
        chain = []

        def dv(fn, rd, wr):
            s.op("dve", fn, reads=rd, writes=wr)
        s.op("act", lambda e: e.activation(out=V(0), in_=lam, func=AF.Abs), reads=["cols"], writes=[("sp", 0)])
        s.op("act", lambda e: e.activation(out=V(1), in_=V(0), func=AF.Exp, scale=-1.0), reads=[("sp", 0)], writes=[("sp", 1)])
        dv(lambda e: e.tensor_scalar(out=V(2), in0=V(1), scalar1=2.0, scalar2=None, op0=ALU.add), [("sp", 1)], [("sp", 2)])
        dv(lambda e: e.reciprocal(out=V(2), in_=V(2)), [("sp", 2)], [("sp", 2)])
        dv(lambda e: e.tensor_tensor(out=V(3), in0=V(1), in1=V(2), op=ALU.mult), [("sp", 1), ("sp", 2)], [("sp", 3)])
        dv(lambda e: e.tensor_tensor(out=V(4), in0=V(3), in1=V(3), op=ALU.mult), [("sp", 3)], [("sp", 4)])
        dv(lambda e: e.tensor_scalar(out=V(5), in0=V(4), scalar1=1.0 / 11, scalar2=1.0 / 9, op0=ALU.mult, op1=ALU.add), [("sp", 4)], [("sp", 5)])
        for cst_ in (1.0 / 7, 1.0 / 5, 1.0 / 3, 1.0):
            dv(lambda e: e.tensor_tensor(out=V(5), in0=V(5), in1=V(4), op=ALU.mult), [("sp", 5), ("sp", 4)], [("sp", 5)])
            dv(lambda e, c=cst_: e.tensor_scalar(out=V(5), in0=V(5), scalar1=c, scalar2=None, op0=ALU.add), [("sp", 5)], [("sp", 5)])
        dv(lambda e: e.tensor_tensor(out=V(5), in0=V(5), in1=V(3), op=ALU.mult), [("sp", 5), ("sp", 3)], [("sp", 5)])
        dv(lambda e: e.tensor_scalar(out=V(6), in0=lam, scalar1=-1.0, scalar2=0.0, op0=ALU.mult, op1=ALU.max), ["cols"], [("sp", 6)])
        dv(lambda e: e.scalar_tensor_tensor(out=V(7), in0=V(5), scalar=2.0, in1=V(6), op0=ALU.mult, op1=ALU.add),
           [("sp", 5), ("sp", 6)], [("sp", 7)])
        dv(lambda e: e.tensor_scalar(out=nsp[:, 0:4], in0=V(7), scalar1=-8.0, scalar2=None, op0=ALU.mult), [("sp", 7)], ["nsp"])
        dv(lambda e: e.tensor_scalar(out=nsp[:, 4:8], in0=V(7), scalar1=-16.0, scalar2=None, op0=ALU.mult), [("sp", 7)], ["nsp2"])

        b.release(m_mark)
        xt_pool = RPool(b, "xt", [128, 1024], F32, 4)
        st_pool = RPool(b, "st", [128, 16], F32, 4)
        xh_pool = RPool(b, "xh", [128, 4, 1024], F32, 1)
        xn_pool = RPool(b, "xn", [128, 8, 512], BF16, 2)
        f32_pool = RPool(b, "pf", [128, 512], F32, 8)
        bf_pool = RPool(b, "pb", [128, 512], BF16, 5)
        tab_pool = RPool(b, "tab", [128, 6, 512], F32, 1)
        cq_pool = RPool(b, "cqg", [128, 3, 512], BF16, 1)
        sq_pool = RPool(b, "sq", [128, 3, 512], F32, 1)
        vst_pool = RPool(b, "vst", [128, 640], BF16, 2)
        rc_pool = RPool(b, "rc", [128, 1], F32, 4)
        evi = [0]

        def evac_engine():
            evi[0] += 1
            return "act" if evi[0] % 2 else "dve"

        def pblock(bi):
            t0, nt = BLOCKS[bi]
            ntile = nt // 128
            is_ctx = bi == 0
            full = (not is_ctx) or need_ctx
            mc = 1 if is_ctx else 0
            src = c_src if is_ctx else x_src
            xts = []
            for i in range(ntile):
                xt, xtk = xt_pool.next()
                r0 = (t0 + i * 128) if is_ctx else (t0 - CTX + i * 128)
                s.dma("sp", xt[:, :], src[r0:r0 + 128, :], writes=[xtk], group=G("xt", xtk))
                xts.append((xt, xtk))
            yield "L"
            xh, xhk = xh_pool.next()
            for i in range(ntile):
                yield "a"
                xt, xtk = xts[i]
                st, stk = st_pool.next()
                s.op("dve", lambda e, st=st, xt=xt: e.bn_stats(out=st[:, 0:6], in_=xt[:, 0:512]), reads=[xtk], writes=[(stk, 0)])
                s.op("dve", lambda e, st=st, xt=xt: e.bn_stats(out=st[:, 6:12], in_=xt[:, 512:1024]), reads=[xtk], writes=[(stk, 1)])
                s.op("dve", lambda e, st=st: e.bn_aggr(out=st[:, 12:14], in_=st[:, 0:12].rearrange("p (a b) -> p a b", b=6)),
                     reads=[(stk, 0), (stk, 1)], writes=[(stk, 2)])
                rstd(st[:, 14:15], st[:, 13:14], [(stk, 2)], (stk, 3))
                s.op("dve", lambda e, st=st, xt=xt, xh=xh, i=i: e.tensor_scalar(out=xh[:, i, :], in0=xt[:, :], scalar1=st[:, 12:13],
                                                                                scalar2=st[:, 14:15], op0=ALU.subtract, op1=ALU.mult),
                     reads=[xtk, (stk, 2), (stk, 3)], writes=[(xhk, i)])
            yield "A"
            xn, xnk = xn_pool.next()
            for kc in range(8):
                yield "b"
                ps, pk = b.ps()
                for i in range(ntile):
                    s.op("pe", lambda e, ps=ps, xh=xh, i=i, kc=kc: e.transpose(ps[:, i * 128:(i + 1) * 128], xh[:, i, kc * 128:(kc + 1) * 128], ident[:, :]),
                         reads=[(xhk, i), ("cst", 0)], writes=[pk])
                eng = evac_engine()
                if eng == "act":
                    s.op("act", lambda e, ps=ps, xn=xn, kc=kc, mc=mc, nt=nt: e.activation(
                        out=xn[:, kc, 0:nt], in_=ps[:, 0:nt], func=AF.Identity, scale=modfm[:, 8 + kc, mc:mc + 1], bias=modfm[:, kc, mc:mc + 1]),
                         reads=[pk] + MODK, writes=[(xnk, kc)])
                else:
                    s.op("dve", lambda e, ps=ps, xn=xn, kc=kc, mc=mc, nt=nt: e.tensor_scalar(
                        out=xn[:, kc, 0:nt], in0=ps[:, 0:nt], scalar1=modfm[:, 8 + kc, mc:mc + 1], scalar2=modfm[:, kc, mc:mc + 1],
                        op0=ALU.mult, op1=ALU.add), reads=[pk] + MODK, writes=[(xnk, kc)])
            XNK = [(xnk, kc) for kc in range(8)]
            if full:
                s.dma("sp", xn_d.rearrange("(kc p) t -> p kc t", p=128)[:, :, t0:t0 + nt], xn[:, :, 0:nt], reads=XNK, group=G("xn_d", xnk))
            yield "B"
            tb, tbk = ptab[bi]

            def fm(c0, width):
                ps, pk = b.ps()
                for kc in range(8):
                    s.op("pe", lambda e, ps=ps, kc=kc: e.matmul(ps[0:width, 0:nt], lhsT=wP[:, kc, c0:c0 + width], rhs=xn[:, kc, 0:nt],
                                                                start=(kc == 0), stop=(kc == 7)),
                         reads=[("wP", kc), (xnk, kc)], writes=[pk])
                return ps, pk

            def store_fm(dst_ap, rows, src_t, src_k):
                s.dma("sp", dst_ap, src_t[0:rows, 0:nt], reads=[src_k], group=G("st", src_k))

            for ch in range(2):
                yield "c"
                if full:
                    ps, pk = fm(ch * 128, 128)
                    o, ok = bf_pool.next()
                    s.op("act", lambda e, ps=ps, o=o: e.activation(out=o[:, 0:nt], in_=ps[:, 0:nt], func=AF.Copy, scale=NA_SCALE),
                         reads=[pk], writes=[ok])
                    store_fm(qA_d[ch * 128:(ch + 1) * 128, t0:t0 + nt], 128, o, ok)
                ps, pk = fm(256 + ch * 128, 128)
                o, ok = bf_pool.next()
                s.op("dve", lambda e, ps=ps, o=o: e.tensor_copy(out=o[:, 0:nt], in_=ps[:, 0:nt]), reads=[pk], writes=[ok])
                store_fm(kA_d[ch * 128:(ch + 1) * 128, t0:t0 + nt], 128, o, ok)
            yield "C0"
            for ch in range(2):
                yield "c"
                ps, pk = fm(1184 + ch * 128, 128)
                s.op("act", lambda e, ps=ps, ch=ch: e.activation(out=xrT[:, ch, t0:t0 + nt], in_=ps[:, 0:nt], func=AF.Copy),
                     reads=[pk], writes=[("xrT", ch, bi)])
            if full:
                for j in range(8):
                    yield "c"
                    ps, pk = fm(1952 + j * 128, 128)
                    o, ok = bf_pool.next()
                    s.op("act", lambda e, ps=ps, o=o: e.activation(out=o[:, 0:nt], in_=ps[:, 0:nt], func=AF.Silu), reads=[pk], writes=[ok])
                    store_fm(zs_d[j * 128:(j + 1) * 128, t0:t0 + nt], 128, o, ok)
            yield "C1"
            dlist = []
            if full:
                dlist += [("q", 1440, 2976, 19, 20, qD_d, 0), ("q", 1568, 3104, 19, 20, qD_d, 128)]
            dlist += [("k", 1696, 3232, 21, 22, kD_d, 0)]
            for (kind, ca, cb_, gcol, gscol, dst, drow) in dlist:
                yield "c"
                psa, pka = fm(ca, 128)
                psb_, pkb = fm(cb_, 128)
                sqt, sqk = f32_pool.next()
                s.op("act", lambda e, psa=psa, sqt=sqt: e.activation(out=sqt[:, 0:nt], in_=psa[:, 0:nt], func=AF.Square), reads=[pka], writes=[sqk])
                psm, pkm = b.ps()
                s.op("pe", lambda e, psm=psm, sqt=sqt: e.matmul(psm[:, 0:nt], lhsT=bones64[:, :], rhs=sqt[:, 0:nt], start=True, stop=True),
                     reads=[sqk, ("cst", 3)], writes=[pkm])
                rs_, rsk = f32_pool.next()
                rstd(rs_[:, 0:nt], psm[:, 0:nt], [pkm], rsk)
                e1, e1k = f32_pool.next()
                e2, e2k = f32_pool.next()
                s.op("act", lambda e, psa=psa, e1=e1, gcol=gcol: e.activation(out=e1[:, 0:nt], in_=psa[:, 0:nt], func=AF.Identity,
                                                                              scale=cols[:, gcol:gcol + 1]), reads=[pka, "cols"], writes=[e1k])
                s.op("act", lambda e, psb_=psb_, e2=e2, gscol=gscol: e.activation(out=e2[:, 0:nt], in_=psb_[:, 0:nt], func=AF.Identity,
                                                                                  scale=cols[:, gscol:gscol + 1]), reads=[pkb, "cols"], writes=[e2k])
                s.op("pool", lambda e, e1=e1: e.tensor_tensor(out=e1[:, 0:nt], in0=e1[:, 0:nt], in1=tb[:, 0, 0:nt], op=ALU.mult),
                     reads=[e1k, (tbk, "D")], writes=[e1k])
                s.op("pool", lambda e, e2=e2: e.tensor_tensor(out=e2[:, 0:nt], in0=e2[:, 0:nt], in1=tb[:, 1, 0:nt], op=ALU.mult),
                     reads=[e2k, (tbk, "D")], writes=[e2k])
                s.op("pool", lambda e, e1=e1, e2=e2: e.tensor_tensor(out=e1[:, 0:nt], in0=e1[:, 0:nt], in1=e2[:, 0:nt], op=ALU.add),
                     reads=[e1k, e2k], writes=[e1k])
                o, ok = bf_pool.next()
                sc_ = GQA_SCALE if kind == "q" else 1.0
                s.op("dve", lambda e, e1=e1, rs_=rs_, o=o, sc_=sc_: e.scalar_tensor_tensor(out=o[:, 0:nt], in0=e1[:, 0:nt], scalar=sc_, in1=rs_[:, 0:nt],
                                                                                           op0=ALU.mult, op1=ALU.mult), reads=[e1k, rsk], writes=[ok])
                store_fm(dst[drow:drow + 128, t0:t0 + nt], 128, o, ok)
            cq, cqk = cq_pool.next()
            sq, sqk3 = sq_pool.next()
            chunks = ([(0, 768, 16), (1, 896, 17)] if full else []) + [(2, 1024, 18)]
            for (slot, c0, gcol) in chunks:
                yield "c"
                ps, pk = fm(c0, 128)
                s.op("act", lambda e, ps=ps, slot=slot, gcol=gcol: e.activation(out=cq[:, slot, 0:nt], in_=ps[:, 0:nt], func=AF.Identity,
                                                                                scale=cols[:, gcol:gcol + 1]), reads=[pk, "cols"], writes=[(cqk, slot)])
                s.op("act", lambda e, ps=ps, slot=slot: e.activation(out=sq[:, slot, 0:nt], in_=ps[:, 0:nt], func=AF.Square),
                     reads=[pk], writes=[(sqk3, slot)])
            if full:
                psm, pkm = b.ps()
                for ch in range(2):
                    s.op("pe", lambda e, psm=psm, ch=ch: e.matmul(psm[:, 0:nt], lhsT=ones256[:, :], rhs=sq[:, ch, 0:nt], start=(ch == 0), stop=(ch == 1)),
                         reads=[(sqk3, ch), ("cst", 1)], writes=[pkm])
                rq, rqk = f32_pool.next()
                rstd(rq[:, 0:nt], psm[:, 0:nt], [pkm], rqk)
                cp, cpk = f32_pool.next()
                sp_, spk = f32_pool.next()
                s.op("dve", lambda e, cp=cp, rq=rq: e.scalar_tensor_tensor(out=cp[0:96, 0:nt], in0=tb[0:96, 2, 0:nt], scalar=MLA_SCALE, in1=rq[0:96, 0:nt],
                                                                           op0=ALU.mult, op1=ALU.mult), reads=[(tbk, "B"), rqk], writes=[cpk])
                s.op("dve", lambda e, sp_=sp_, rq=rq: e.scalar_tensor_tensor(out=sp_[0:96, 0:nt], in0=tb[0:96, 3, 0:nt], scalar=MLA_SCALE, in1=rq[0:96, 0:nt],
                                                                             op0=ALU.mult, op1=ALU.mult), reads=[(tbk, "B"), rqk], writes=[spk])
                for h in range(4):
                    yield "c"
                    psa, pka = b.ps()
                    psb_, pkb = b.ps()
                    for ch in range(2):
                        s.op("pe", lambda e, psa=psa, ch=ch, h=h: e.matmul(psa[0:96, 0:nt], lhsT=wuq[:, ch, h * 96:(h + 1) * 96], rhs=cq[:, ch, 0:nt],
                                                                           start=(ch == 0), stop=(ch == 1)), reads=["wuq", (cqk, ch)], writes=[pka])
                    for ch in range(2):
                        s.op("pe", lambda e, psb_=psb_, ch=ch, h=h: e.matmul(psb_[0:96, 0:nt], lhsT=wuqs[:, ch, h * 96:(h + 1) * 96], rhs=cq[:, ch, 0:nt],
                                                                             start=(ch == 0), stop=(ch == 1)), reads=["wuqs", (cqk, ch)], writes=[pkb])
                    t1, t1k = f32_pool.next()
                    t2, t2k = f32_pool.next()
                    s.op("dve", lambda e, psa=psa, t1=t1, cp=cp: e.tensor_tensor(out=t1[0:96, 0:nt], in0=psa[0:96, 0:nt], in1=cp[0:96, 0:nt], op=ALU.mult),
                         reads=[pka, cpk], writes=[t1k])
                    s.op("dve", lambda e, psb_=psb_, t2=t2, sp_=sp_: e.tensor_tensor(out=t2[0:96, 0:nt], in0=psb_[0:96, 0:nt], in1=sp_[0:96, 0:nt], op=ALU.mult),
                         reads=[pkb, spk], writes=[t2k])
                    o, ok = bf_pool.next()
                    s.op("pool", lambda e, t1=t1, t2=t2, o=o: e.tensor_tensor(out=o[0:96, 0:nt], in0=t1[0:96, 0:nt], in1=t2[0:96, 0:nt], op=ALU.add),
                         reads=[t1k, t2k], writes=[ok])
                    store_fm(qB_d[h, :, t0:t0 + nt], 96, o, ok)
            psm, pkm = b.ps()
            s.op("pe", lambda e, psm=psm: e.matmul(psm[:, 0:nt], lhsT=ones128[:, :], rhs=sq[:, 2, 0:nt], start=True, stop=True),
                 reads=[(sqk3, 2), ("cst", 2)], writes=[pkm])
            rkv, rkvk = f32_pool.next()
            rstd(rkv[:, 0:nt], psm[:, 0:nt], [pkm], rkvk)
            for h in range(4):
                yield "c"
                ps, pk = b.ps()
                s.op("pe", lambda e, ps=ps, h=h: e.matmul(ps[0:64, 0:nt], lhsT=wukv[:, h * 64:(h + 1) * 64], rhs=cq[:, 2, 0:nt], start=True, stop=True),
                     reads=["wukv", (cqk, 2)], writes=[pk])
                o, ok = bf_pool.next()
                s.op("dve", lambda e, ps=ps, o=o, rkv=rkv: e.tensor_tensor(out=o[0:64, 0:nt], in0=ps[0:64, 0:nt], in1=rkv[0:64, 0:nt], op=ALU.mult),
                     reads=[pk, rkvk], writes=[ok])
                store_fm(kB_d[h, 0:64, t0:t0 + nt], 64, o, ok)
            psa, pka = fm(1152, 32)
            psb_, pkb = fm(3360, 32)
            t1, t1k = f32_pool.next()
            t2, t2k = f32_pool.next()
            s.op("dve", lambda e, psa=psa, t1=t1: e.tensor_tensor(out=t1[0:32, 0:nt], in0=psa[0:32, 0:nt], in1=tb[0:32, 4, 0:nt], op=ALU.mult),
                 reads=[pka, (tbk, "K")], writes=[t1k])
            s.op("dve", lambda e, psb_=psb_, t2=t2: e.tensor_tensor(out=t2[0:32, 0:nt], in0=psb_[0:32, 0:nt], in1=tb[0:32, 5, 0:nt], op=ALU.mult),
                 reads=[pkb, (tbk, "K")], writes=[t2k])
            o, ok = bf_pool.next()
            s.op("pool", lambda e, t1=t1, t2=t2, o=o: e.tensor_tensor(out=o[0:32, 0:nt], in0=t1[0:32, 0:nt], in1=t2[0:32, 0:nt], op=ALU.add),
                 reads=[t1k, t2k], writes=[ok])
            for h in range(4):
                store_fm(kB_d[h, 64:96, t0:t0 + nt], 32, o, ok)
            for i in range(ntile):
                yield "c"
                vs, vsk = vst_pool.next()
                tok = slice(i * 128, (i + 1) * 128)
                psa, pka = b.ps()
                for kc in range(8):
                    s.op("pe", lambda e, psa=psa, kc=kc, tok=tok: e.matmul(psa[:, 0:256], lhsT=xn[:, kc, tok], rhs=wP[:, kc, 512:768],
                                                                           start=(kc == 0), stop=(kc == 7)), reads=[("wP", kc), (xnk, kc)], writes=[pka])
                psd, pkd = b.ps()
                for kc in range(8):
                    s.op("pe", lambda e, psd=psd, kc=kc, tok=tok: e.matmul(psd[:, 0:128], lhsT=xn[:, kc, tok], rhs=wP[:, kc, 1824:1952],
                                                                           start=(kc == 0), stop=(kc == 7)), reads=[("wP", kc), (xnk, kc)], writes=[pkd])
                s.op("dve", lambda e, psa=psa, vs=vs: e.tensor_copy(out=vs[:, 0:256], in_=psa[:, 0:256]), reads=[pka], writes=[(vsk, 0)])
                s.op("act", lambda e, psd=psd, vs=vs: e.activation(out=vs[:, 256:384], in_=psd[:, 0:128], func=AF.Copy), reads=[pkd], writes=[(vsk, 1)])
                psc, pkc = b.ps()
                s.op("pe", lambda e, psc=psc, tok=tok: e.matmul(psc[:, 0:1], lhsT=sq[:, 2, tok], rhs=ones128[:, 0:1], start=True, stop=True),
                     reads=[(sqk3, 2), ("cst", 2)], writes=[pkc])
                rc, rck = rc_pool.next()
                rstd(rc[:, 0:1], psc[:, 0:1], [pkc], rck)
                psv, pkv = b.ps()
                s.op("pe", lambda e, psv=psv, tok=tok: e.matmul(psv[:, 0:256], lhsT=cq[:, 2, tok], rhs=wukv[:, 256:512], start=True, stop=True),
                     reads=["wukv", (cqk, 2)], writes=[pkv])
                s.op("act", lambda e, psv=psv, vs=vs, rc=rc: e.activation(out=vs[:, 384:640], in_=psv[:, 0:256], func=AF.Identity, scale=rc[:, 0:1]),
                     reads=[pkv, rck], writes=[(vsk, 2)])
                gt = (t0 + i * 128) // 128
                s.dma("sp", vaug_d[0:4, :, gt, 0:64].rearrange("h p c -> p h c"), vs[:, 0:256].rearrange("p (h c) -> p h c", c=64),
                      reads=[(vsk, 0)], group=G("st", vsk))
                s.dma("sp", vaug_d[8:10, :, gt, 0:64].rearrange("h p c -> p h c"), vs[:, 256:384].rearrange("p (h c) -> p h c", c=64),
                      reads=[(vsk, 1)], group=G("st", vsk))
                s.dma("sp", vaug_d[4:8, :, gt, 0:64].rearrange("h p c -> p h c"), vs[:, 384:640].rearrange("p (h c) -> p h c", c=64),
                      reads=[(vsk, 2)], group=G("st", vsk))

        ptab = {}

        def load_tables(bi):
            t0, nt = BLOCKS[bi]
            tb, tbk = tab_pool.next()
            s.dma("sp", tb[:, 0:2, 0:nt], tabD[:, :, t0:t0 + nt].rearrange("a p t -> p a t"), writes=[(tbk, "D")], group=G("tab"))
            s.dma("sp", tb[0:96, 2:4, 0:nt], tabB[:, :, t0:t0 + nt].rearrange("a p t -> p a t"), writes=[(tbk, "B")], group=G("tab"))
            s.dma("sp", tb[0:32, 4:6, 0:nt], tabK[:, :, t0:t0 + nt].rearrange("a p t -> p a t"), writes=[(tbk, "K")], group=G("tab"))
            ptab[bi] = (tb, tbk)

        gens = [pblock(bi) for bi in range(len(BLOCKS))]

        def advance(bi_, stops):
            for tag in gens[bi_]:
                if tag in stops:
                    return tag
            return None

        load_tables(0)
        advance(0, ("B",))
        for j in range(len(BLOCKS)):
            has_next = j + 1 < len(BLOCKS)
            if has_next:
                advance(j + 1, ("L",))
            advance(j, ("C0",))
            if has_next:
                advance(j + 1, ("A",))
            advance(j, ("C1",))
            if has_next:
                advance(j + 1, ("B",))
            advance(j, ())
            if has_next:
                load_tables(j + 1)

        b.release(pc_mark)
        if "C" in PH:
            wl = b.alloc("wl", [128, 2, 2, 2, 128], F32)
            for d in range(2):
                for g_ in range(2):
                    for ch in range(2):
                        s.dma("sp", wl[:, d, g_, ch, :], lruw[l, d, g_, ch], writes=["wl"], group=G("wl"))
            xcs = [b.alloc("xc%d" % ch_, [128, T], F32) for ch_ in range(2)]
            Abuf = b.alloc("Abuf", [128, T], F32)
            Ubuf = b.alloc("Ubuf", [128, T], F32)
            H0 = b.alloc("H0", [128, T], F32)
            H1 = b.alloc("H1", [128, T], F32)
            Hs = [H0, H1]
            NBK = len(BLOCKS)
            XRK = lambda ch: [("xrT", ch, bi) for bi in range(NBK)]
            AK = [("A", bi) for bi in range(NBK)]
            UK = ["U"]
            HK = lambda d: [("H", d, bi) for bi in range(NBK)]
            for ch in range(2):
                xr = xrT[:, ch, :]
                xc = xcs[ch]
                s.op("dve", lambda e: e.tensor_scalar(out=xc[:, :], in0=xr, scalar1=cols[:, 23 + ch * 4 + 2:24 + ch * 4 + 2],
                                                      scalar2=cols[:, 31 + ch:32 + ch], op0=ALU.mult, op1=ALU.add),
                     reads=XRK(ch) + ["cols"], writes=[("xc", ch)])
                for (s0, s1) in ((0, CTX), (CTX, T)):
                    for j, off in ((0, -2), (1, -1), (3, 1)):
                        lo = max(s0, s0 - off)
                        hi_ = min(s1, s1 - off)
                        s.op("dve", lambda e: e.scalar_tensor_tensor(
                            out=xc[:, lo:hi_], in0=xr[:, lo + off:hi_ + off], scalar=cols[:, 23 + ch * 4 + j:24 + ch * 4 + j],
                            in1=xc[:, lo:hi_], op0=ALU.mult, op1=ALU.add), reads=XRK(ch) + ["cols", ("xc", ch)], writes=[("xc", ch)])
            for ch in range(2):
                xc = xcs[ch]
                for d in range(2):
                    Hd = Hs[d]
                    ca = 33 + d * 2 + ch
                    cx = 37 + d * 2 + ch
                    cn = d * 2 + ch
                    for bi, (t0, nt) in enumerate(BLOCKS):
                        sl = slice(t0, t0 + nt)
                        psa, pka = b.ps()
                        s.op("pe", lambda e: e.matmul(psa[:, 0:nt], lhsT=wl[:, d, 0, ch, :], rhs=xc[:, sl], start=True, stop=True),
                             reads=["wl", ("xc", ch)], writes=[pka])
                        psx, pkx = b.ps()
                        s.op("pe", lambda e: e.matmul(psx[:, 0:nt], lhsT=wl[:, d, 1, ch, :], rhs=xc[:, sl], start=True, stop=True),
                             reads=["wl", ("xc", ch)], writes=[pkx])
                        s.op("act", lambda e: e.activation(out=Abuf[:, sl], in_=psa[:, 0:nt], func=AF.Sigmoid, bias=cols[:, ca:ca + 1]),
                             reads=[pka, "cols"], writes=[("A", bi)])
                        s.op("act", lambda e: e.activation(out=Hd[:, sl], in_=psx[:, 0:nt], func=AF.Sigmoid, bias=cols[:, cx:cx + 1]),
                             reads=[pkx, "cols"], writes=[("H", d, bi)])
                    s.op("pool", lambda e: e.tensor_tensor(out=Hd[:, :], in0=Hd[:, :], in1=xc[:, :], op=ALU.mult),
                         reads=HK(d) + [("xc", ch)], writes=HK(d))
                    s.op("act", lambda e: e.activation(out=Ubuf[:, :], in_=Abuf[:, :], func=AF.Exp, scale=nsp[:, 4 + cn:5 + cn]),
                         reads=AK + ["nsp2"], writes=UK)
                    s.op("act", lambda e: e.activation(out=Abuf[:, :], in_=Abuf[:, :], func=AF.Exp, scale=nsp[:, cn:cn + 1]),
                         reads=AK + ["nsp"], writes=AK)
                    s.op("dve", lambda e: e.tensor_scalar(out=Ubuf[:, :], in0=Ubuf[:, :], scalar1=-1.0, scalar2=1.0, op0=ALU.mult, op1=ALU.add),
                         reads=UK, writes=UK)
                    s.op("act", lambda e: e.activation(out=Ubuf[:, :], in_=Ubuf[:, :], func=AF.Sqrt), reads=UK, writes=UK)
                    s.op("dve", lambda e: e.tensor_tensor(out=Ubuf[:, :], in0=Ubuf[:, :], in1=Hd[:, :], op=ALU.mult),
                         reads=UK + HK(d), writes=UK)
                    if d == 0:
                        s.op("dve", lambda e: e.tensor_tensor_scan(out=Hd[:, :], data0=Abuf[:, :], data1=Ubuf[:, :], initial=0.0, op0=ALU.mult, op1=ALU.add),
                             reads=AK + UK + HK(d), writes=HK(d))
                    else:
                        s.op("dve", lambda e: e.tensor_tensor_scan(out=Hd[:, 0:CTX][:, ::-1], data0=Abuf[:, 0:CTX][:, ::-1], data1=Ubuf[:, 0:CTX][:, ::-1],
                                                                   initial=0.0, op0=ALU.mult, op1=ALU.add), reads=AK + UK + HK(d), writes=[("H", d, 0)])
                        s.op("dve", lambda e: e.tensor_tensor_scan(out=Hd[:, CTX:T][:, ::-1], data0=Abuf[:, CTX:T][:, ::-1], data1=Ubuf[:, CTX:T][:, ::-1],
                                                                   initial=Hd[:, 0:1], op0=ALU.mult, op1=ALU.add),
                             reads=AK + UK + HK(d), writes=HK(d)[1:])
                s.op("pool", lambda e: e.tensor_tensor(out=H0[:, :], in0=H0[:, :], in1=H1[:, :], op=ALU.add),
                     reads=HK(0) + HK(1), writes=HK(0))
                s.dma("sp", y_d[512 + ch * 128:512 + (ch + 1) * 128, :], H0[:, :], reads=HK(0), group=G("yC"))

        b.release(layer_mark)
        wM = b.alloc("wM", [128, 8, 4096], BF16)
        wB = b.alloc("wB", [128, 8, 1024], BF16)
        wO = b.alloc("wO", [128, 8, 1024], BF16)
        f_mark = b.mark()
        ka_pool = RPool(b, "ka", [128, T], BF16, 2)
        va_pool = RPool(b, "va", [128, 34, 128], BF16, 2)
        q_pool = RPool(b, "qsb", [128, 512], BF16, 3)
        qz_pools = [RPool(b, "qz%d" % g_, [128, 512], BF16, 3) for g_ in range(2)]
        for g_ in range(2):
            for i_, t_ in enumerate(qz_pools[g_].bufs):
                s.op("pool", lambda e: e.memset(t_[:, :], 0.0), writes=[(("qz%d" % g_, i_), "z")])
        qaz = [b.alloc("qaz%d" % hh_, [128, T], BF16) for hh_ in range(2)]
        for hh_ in range(2):
            s.op("pool", lambda e: e.memset(qaz[hh_][:, :], 0.0), writes=[("qaz", hh_, "z")])
        pT_pool = RPool(b, "pT", [128, 512], BF16, 4)
        rd_pool = RPool(b, "rden", [64, 512], F32, 2)
        yo_pool = RPool(b, "yo", [64, 512], F32, 2)
        yoA_pool = RPool(b, "yoA", [64, 512], F32, 2)
        nab_sb = b.alloc("nab_sb", [128, 4, NV, 64], BF16)
        for h in range(4):
            s.dma("pool", nab_sb[:, h, :, :], nab[l, h].rearrange("v p q -> p v q"), writes=["nab_sb"], group=G("nab"))

        if "F" in PH:
            for kc in range(8):
                for c4 in range(4):
                    s.dma("pool", wM[:, kc, c4 * 1024:(c4 + 1) * 1024], w_in[l, kc * 128:(kc + 1) * 128, 2976 + c4 * 1024:2976 + (c4 + 1) * 1024],
                          writes=[("wM", kc)], group=G("wM"))
                s.dma("pool", wB[:, kc, :], w_br[l, kc * 128:(kc + 1) * 128, :], writes=[("wB", kc)], group=G("wB"))
                s.dma("pool", wO[:, kc, :], w_out[l, kc * 128:(kc + 1) * 128, :], writes=[("wO", kc)], group=G("wO"))
        ALLK = list(range(34))
        lat_blocks = [(256 + 512 * j, 512, ALLK) for j in range(8)]
        ctx_blocks = [(0, 256, [0, 1])] if need_ctx else []
        QB = ctx_blocks + lat_blocks

        jobs = []
        if "B" in PH:
            for h in range(4):
                jobs.append(dict(kind="full", k_src=kB_d[h], krows=96, v=4 + h,
                                 runs=[(qB_d[h], 96, 0, 256 + h * 64, None)]))
        if "D" in PH:
            for g_ in range(2):
                jobs.append(dict(kind="full", k_src=kD_d[:, :], krows=128, v=8 + g_,
                                 runs=[(qD_d[h * 64:(h + 1) * 64, :], 64, g_ * 64, 768 + h * 64, qz_pools[g_]) for h in (2 * g_, 2 * g_ + 1)]))
        if "A" in PH:
            for h in range(4):
                jobs.append(dict(kind="na", k_src=kA_d[(h // 2) * 128:(h // 2 + 1) * 128, :], krows=128, v=h, h=h))

        def issue_loads(job):
            ka, kak = ka_pool.next()
            va, vak = va_pool.next()
            s.dma("sp", ka[0:job["krows"], :], job["k_src"], writes=[kak], group=G("ka", kak))
            s.dma("sp", va[:, :, :], vaug_d[job["v"]], writes=[vak], group=G("va", vak))
            job["ka"], job["kak"], job["va"], job["vak"] = ka, kak, va, vak
            if job["kind"] == "na":
                h = job["h"]
                hh = h % 2
                s.dma("sp", qaz[hh][hh * 64:(hh + 1) * 64, :], qA_d[h * 64:(h + 1) * 64, :], reads=[("qaz", hh, "z")], writes=[("qaz", hh)], group=G("qa", hh))

        nrm_pool = RPool(b, "nrm", [128, 512], F32, 2)

        def full_stream(fjobs):
            runs = []
            for job in fjobs:
                for ri_, run in enumerate(job["runs"]):
                    runs.append((job, ri_ == 0) + tuple(run))
            items = [(ri, bi_, n_) for ri in range(len(runs)) for bi_, (_, _, kts) in enumerate(QB) for n_ in range(len(kts))]
            qsl, psos, infl = {}, {}, {}
            LOOK = 3

            def load_q(ri, bi_):
                job, first, q_src, dk, kb, yrow0, qp = runs[ri]
                tq0, nq, _ = QB[bi_]
                qs, qk = (qp or q_pool).next()
                s.dma("sp", qs[kb:kb + dk, 0:nq], q_src[:, tq0:tq0 + nq], reads=[(qk, "z")], writes=[qk], group=G("q", qk))
                qsl[(ri, bi_)] = (qs, qk)

            load_q(0, 0)
            for it in range(len(items) + LOOK):
                if it < len(items):
                    ri, bi_, n_ = items[it]
                    job, first, q_src, dk, kb, yrow0, qp = runs[ri]
                    ka, kak, va, vak = job["ka"], job["kak"], job["va"], job["vak"]
                    k0_, k1_ = (0, 128) if qp is not None else (kb, kb + dk)
                    tq0, nq, kts = QB[bi_]
                    if n_ == 0:
                        psos[(ri, bi_)] = b.ps()
                        if bi_ + 1 < len(QB):
                            load_q(ri, bi_ + 1)
                        elif ri + 1 < len(runs):
                            load_q(ri + 1, 0)
                    qs, qk = qsl[(ri, bi_)]
                    kt = kts[n_]
                    pss, pks = b.ps()
                    while any(pks == pk_ for (_, pk_) in psos.values()):
                        pss, pks = b.ps()
                    s.op("pe", lambda e: e.matmul(pss[:, 0:nq], lhsT=ka[k0_:k1_, kt * 128:(kt + 1) * 128], rhs=qs[k0_:k1_, 0:nq],
                                                  start=True, stop=True), reads=[kak, qk, (qk, "z")], writes=[pks])
                    infl[it] = (pss, pks)
                m_ = it - LOOK
                if m_ >= 0:
                    ri, bi_, n_ = items[m_]
                    job, first, q_src, dk, kb, yrow0, qp = runs[ri]
                    va, vak = job["va"], job["vak"]
                    tq0, nq, kts = QB[bi_]
                    kt = kts[n_]
                    if n_ == 0 and bi_ == 0 and first and job["next"] is not None:
                        issue_loads(job["next"])
                    pss, pks = infl.pop(m_)
                    pso, pko = psos[(ri, bi_)]
                    pT, pTk = pT_pool.next()
                    s.op("act", lambda e: e.activation(out=pT[:, 0:nq], in_=pss[:, 0:nq], func=AF.Exp), reads=[pks], writes=[pTk])
                    s.op("pe", lambda e: e.matmul(pso[:, 0:nq], lhsT=va[:, kt, :], rhs=pT[:, 0:nq],
                                                  start=(n_ == 0), stop=(n_ == len(kts) - 1)), reads=[vak, pTk], writes=[pko])
                    if n_ == len(kts) - 1:
                        nr, nrk = nrm_pool.next()
                        s.op("dve", lambda e: e.tensor_copy(out=nr[:, 0:nq], in_=pso[:, 0:nq]), reads=[pko], writes=[nrk])
                        rd, rdk = rd_pool.next()
                        s.op("dve", lambda e: e.reciprocal(out=rd[0:64, 0:nq], in_=nr[64:128, 0:nq]), reads=[nrk], writes=[rdk])
                        yo, yok = yo_pool.next()
                        s.op("dve", lambda e: e.tensor_tensor(out=yo[0:64, 0:nq], in0=nr[0:64, 0:nq], in1=rd[0:64, 0:nq], op=ALU.mult),
                             reads=[nrk, rdk], writes=[yok])
                        s.dma("sp", y_d[yrow0:yrow0 + 64, tq0:tq0 + nq], yo[0:64, 0:nq], reads=[yok], group=G("yst", yok))
                        del psos[(ri, bi_)]

        def na_attention(job):
            ka, kak, va, vak = job["ka"], job["kak"], job["va"], job["vak"]
            h = job["h"]
            hh = h % 2
            groups = []
            if need_ctx:
                groups.append([("c", j) for j in range(4)])
            for g8 in range(8):
                groups.append([("l", g8 * 8 + j) for j in range(8)])
            flat = []
            for grp_rows in groups:
                for gi_, (kind, r) in enumerate(grp_rows):
                    flat.append((kind, r, gi_, len(grp_rows), grp_rows))
            st1, cur = {}, {}

            def stage1(i):
                kind, r, gi_, ng, grp_rows = flat[i]
                if kind == "c":
                    tq0 = r * 64
                    tiles = [(0, None), (1, None)]
                else:
                    tq0 = CTX + r * 64
                    tiles = [(0, None), (1, None)] + [(2 + g // 2, v) for (g, v) in NA_PLAN[r]]
                pss, pks = b.ps()
                for j, (kt, v) in enumerate(tiles):
                    s.op("pe", lambda e: e.matmul(pss[:, j * 64:(j + 1) * 64], lhsT=ka[:, kt * 128:(kt + 1) * 128],
                                                  rhs=qaz[hh][:, tq0:tq0 + 64], start=True, stop=(v is None)),
                         reads=[kak, ("qaz", hh), ("qaz", hh, "z")], writes=[pks])
                    if v is not None:
                        s.op("pe", lambda e: e.matmul(pss[:, j * 64:(j + 1) * 64], lhsT=ident_bf[:, :], rhs=nab_sb[:, h, v, :],
                                                      start=False, stop=True), reads=["ident_bf", "nab_sb"], writes=[pks])
                st1[i] = (pss, pks, tiles, tq0)

            def stage2(i):
                kind, r, gi_, ng, grp_rows = flat[i]
                pss, pks, tiles, tq0 = st1.pop(i)
                nk = len(tiles)
                if gi_ == 0:
                    cur["yo"] = yoA_pool.next()
                yo, yok = cur["yo"]
                pT, pTk = pT_pool.next()
                s.op("act", lambda e: e.activation(out=pT[:, 0:nk * 64], in_=pss[:, 0:nk * 64], func=AF.Exp), reads=[pks], writes=[pTk])
                pso, pko = b.ps()
                for j, (kt, v) in enumerate(tiles):
                    s.op("pe", lambda e: e.matmul(pso[:, 0:64], lhsT=va[:, kt, :], rhs=pT[:, j * 64:(j + 1) * 64],
                                                  start=(j == 0), stop=(j == nk - 1)), reads=[vak, pTk], writes=[pko])
                rd, rdk = rd_pool.next()
                s.op("dve", lambda e: e.reciprocal(out=rd[0:64, 0:64], in_=pso[64:128, 0:64]), reads=[pko], writes=[rdk])
                c0 = gi_ * 64
                s.op("dve", lambda e: e.tensor_tensor(out=yo[0:64, c0:c0 + 64], in0=pso[0:64, 0:64], in1=rd[0:64, 0:64], op=ALU.mult),
                     reads=[pko, rdk], writes=[(yok, gi_)])
                if gi_ == ng - 1:
                    k0, r0_ = grp_rows[0]
                    grp_t0 = r0_ * 64 if k0 == "c" else CTX + r0_ * 64
                    s.dma("sp", y_d[h * 64:(h + 1) * 64, grp_t0:grp_t0 + ng * 64], yo[0:64, 0:ng * 64],
                          reads=[(yok, n_) for n_ in range(ng)], group=G("yst", yok))

            NLOOK = 2
            for i in range(len(flat) + NLOOK):
                if i < len(flat):
                    stage1(i)
                if i - NLOOK >= 0:
                    stage2(i - NLOOK)

        for ji, job in enumerate(jobs):
            job["next"] = jobs[ji + 1] if ji + 1 < len(jobs) else None
        fjobs = [j_ for j_ in jobs if j_["kind"] == "full"]
        njobs = [j_ for j_ in jobs if j_["kind"] == "na"]
        if jobs:
            issue_loads(jobs[0])
        if fjobs:
            full_stream(fjobs)
        for job in njobs:
            if job["next"] is not None:
                issue_loads(job["next"])
            na_attention(job)

        b.release(f_mark)
        if "F" in PH:
            fxn_pool = RPool(b, "fxn", [128, 8, 512], BF16, 2)
            fy_pool = RPool(b, "fy", [128, 8, 512], F32, 1)
            fz_pool = RPool(b, "fz", [128, 8, 512], BF16, 1)
            fyg_pool = RPool(b, "fyg", [128, 8, 512], BF16, 1)
            acc_pool = RPool(b, "acc", [128, 512], F32, 2)
            sg_pool = RPool(b, "sg", [128, 512], F32, 2)
            tmp_pool = RPool(b, "ftmp", [128, 512], F32, 2)
            accT_pool = RPool(b, "accT", [128, 8, 512], BF16, 2)
            fx_pool = RPool(b, "fx", [128, 1024], F32, 1)
            fv_pool = RPool(b, "fv", [128, 1024], F32, 2)
            fst_pool = RPool(b, "fst", [128, 16], F32, 4)
            fblocks = [bi for bi in range(len(BLOCKS)) if not (bi == 0 and not need_ctx)]
            floaded = {}

            def f_load(bi):
                t0, nt = BLOCKS[bi]
                xn, xnk = fxn_pool.next()
                fy, fyk = fy_pool.next()
                fz, fzk = fz_pool.next()
                s.dma("sp", xn[:, :, 0:nt], xn_d.rearrange("(kc p) t -> p kc t", p=128)[:, :, t0:t0 + nt], writes=[xnk], group=G("fxn"))
                for half in range(2):
                    s.dma("sp", fy[:, half * 4:(half + 1) * 4, 0:nt], y_d.rearrange("(kc p) t -> p kc t", p=128)[:, half * 4:(half + 1) * 4, t0:t0 + nt],
                          writes=[fyk], group=G("fy"))
                s.dma("sp", fz[:, :, 0:nt], zs_d.rearrange("(kc p) t -> p kc t", p=128)[:, :, t0:t0 + nt], writes=[fzk], group=G("fz"))
                floaded[bi] = (xn, xnk, fy, fyk, fz, fzk)

            def make_epilogue(bi, accT, accTk):
                t0, nt = BLOCKS[bi]
                is_ctx = bi == 0
                gj = 1 if is_ctx else 0
                parts = []
                for i in range(nt // 128):
                    r0 = (t0 + i * 128) if is_ctx else (t0 - CTX + i * 128)
                    src = c_src if is_ctx else x_src
                    dst = ctx1_d if is_ctx else x_dst
                    box = {}

                    def part1(i=i, r0=r0, src=src, box=box):
                        fx, fxk = fx_pool.next()
                        s.dma("sp", fx[:, :], src[r0:r0 + 128, :], writes=[fxk], group=G("fx", fxk))
                        fv, fvk = fv_pool.next()
                        for half in range(2):
                            pso, pko = b.ps()
                            hs = slice(half * 512, (half + 1) * 512)
                            for fc in range(8):
                                s.op("pe", lambda e: e.matmul(pso[:, :], lhsT=accT[:, fc, i * 128:(i + 1) * 128], rhs=wO[:, fc, hs],
                                                              start=(fc == 0), stop=(fc == 7)), reads=[("wO", fc), (accTk, fc)], writes=[pko])
                            tm, tmk = tmp_pool.next()
                            s.op("dve", lambda e: e.tensor_tensor(out=tm[:, :], in0=pso[:, :], in1=gate_rep[:, gj, hs], op=ALU.mult),
                                 reads=[pko] + GATEK, writes=[tmk])
                            s.op("dve", lambda e: e.scalar_tensor_tensor(out=fv[:, hs], in0=fx[:, hs], scalar=ALPHA, in1=tm[:, :],
                                                                         op0=ALU.mult, op1=ALU.add), reads=[tmk, fxk], writes=[(fvk, half)])
                        st, stk = fst_pool.next()
                        s.op("dve", lambda e: e.bn_stats(out=st[:, 0:6], in_=fv[:, 0:512]), reads=[(fvk, 0)], writes=[(stk, 0)])
                        s.op("dve", lambda e: e.bn_stats(out=st[:, 6:12], in_=fv[:, 512:1024]), reads=[(fvk, 1)], writes=[(stk, 1)])
                        s.op("dve", lambda e: e.bn_aggr(out=st[:, 12:14], in_=st[:, 0:12].rearrange("p (a b) -> p a b", b=6)),
                             reads=[(stk, 0), (stk, 1)], writes=[(stk, 2)])
                        rstd(st[:, 14:15], st[:, 13:14], [(stk, 2)], (stk, 3))
                        s.op("dve", lambda e: e.tensor_scalar(out=fv[:, :], in0=fv[:, :], scalar1=st[:, 12:13], scalar2=st[:, 14:15],
                                                              op0=ALU.subtract, op1=ALU.mult), reads=[(fvk, 0), (fvk, 1), (stk, 2), (stk, 3)], writes=[(fvk, 0), (fvk, 1)])
                        box["fv"] = (fv, fvk)

                    def part2(r0=r0, dst=dst, box=box):
                        fv, fvk = box["fv"]
                        s.op("dve", lambda e: e.tensor_tensor(out=fv[:, :], in0=fv[:, :], in1=lng[:, :], op=ALU.mult), reads=[(fvk, 0), (fvk, 1), "lng"], writes=[(fvk, 0), (fvk, 1)])
                        s.op("dve", lambda e: e.tensor_tensor(out=fv[:, :], in0=fv[:, :], in1=lnb[:, :], op=ALU.add), reads=[(fvk, 0), (fvk, 1), "lnb"], writes=[(fvk, 0), (fvk, 1)])
                        s.dma("sp", dst[r0:r0 + 128, :], fv[:, :], reads=[(fvk, 0), (fvk, 1)], group=G("outf" if (l == n_layers - 1) else "x1", fvk))

                    parts += [part1, part2]
                return parts

            pending_epi = []
            fready = {}

            def f_yg(bi):
                nt_ = BLOCKS[bi][1]
                xn_, xnk_, fy, fyk, fz, fzk = floaded.pop(bi)
                fyg, fygk = fyg_pool.next()
                for c in range(8):
                    eng = "pool" if c % 2 else "dve"
                    s.op(eng, lambda e: e.tensor_tensor(out=fyg[:, c, 0:nt_], in0=fy[:, c, 0:nt_], in1=fz[:, c, 0:nt_], op=ALU.mult),
                         reads=[fyk, fzk], writes=[(fygk, c)])
                fready[bi] = (xn_, xnk_, fyg, fygk)

            f_load(fblocks[0])
            for fi_, bi in enumerate(fblocks):
                t0, nt = BLOCKS[bi]
                is_ctx = bi == 0
                ntile = nt // 128
                gj = 1 if is_ctx else 0
                if fi_ == 0:
                    f_yg(bi)
                xn, xnk, fyg, fygk = fready.pop(bi)
                accT, accTk = accT_pool.next()
                epi_parts = pending_epi
                pending_epi = []
                for fc in range(8):
                    if fc > 0 and epi_parts:
                        epi_parts.pop(0)()
                        if fc == 7:
                            while epi_parts:
                                epi_parts.pop(0)()
                    if fc == 4 and fi_ + 1 < len(fblocks):
                        f_load(fblocks[fi_ + 1])
                    acc, acck = acc_pool.next()
                    for i in range(4):
                        psm, pkm = b.ps()
                        for kc in range(8):
                            s.op("pe", lambda e, psm=psm, kc=kc, i=i, fc=fc, xn=xn: e.matmul(psm[:, 0:nt], lhsT=wM[:, kc, i * 1024 + fc * 128:i * 1024 + (fc + 1) * 128],
                                                                                            rhs=xn[:, kc, 0:nt], start=(kc == 0), stop=(kc == 7)),
                                 reads=[("wM", kc), xnk], writes=[pkm])
                        sg, sgk = sg_pool.next()
                        s.op("act", lambda e, psm=psm, sg=sg: e.activation(out=sg[:, 0:nt], in_=psm[:, 0:nt], func=AF.Sigmoid), reads=[pkm], writes=[sgk])
                        psb_, pkb = b.ps()
                        for c2 in range(2):
                            s.op("pe", lambda e, psb_=psb_, c2=c2, i=i, fc=fc, fyg=fyg: e.matmul(psb_[:, 0:nt], lhsT=wB[:, i * 2 + c2, fc * 128:(fc + 1) * 128],
                                                                                                rhs=fyg[:, i * 2 + c2, 0:nt], start=(c2 == 0), stop=(c2 == 1)),
                                 reads=[("wB", i * 2 + c2), (fygk, i * 2 + c2)], writes=[pkb])
                        if i == 0:
                            s.op("dve", lambda e, psb_=psb_, sg=sg, acc=acc: e.tensor_tensor(out=acc[:, 0:nt], in0=psb_[:, 0:nt], in1=sg[:, 0:nt], op=ALU.mult),
                                 reads=[pkb, sgk], writes=[acck])
                        else:
                            tm, tmk = tmp_pool.next()
                            s.op("dve", lambda e, psb_=psb_, sg=sg, tm=tm: e.tensor_tensor(out=tm[:, 0:nt], in0=psb_[:, 0:nt], in1=sg[:, 0:nt], op=ALU.mult),
                                 reads=[pkb, sgk], writes=[tmk])
                            if i < 3:
                                s.op("pool", lambda e, tm=tm, acc=acc: e.tensor_tensor(out=acc[:, 0:nt], in0=acc[:, 0:nt], in1=tm[:, 0:nt], op=ALU.add),
                                     reads=[tmk, acck], writes=[acck])
                            else:
                                s.op("pool", lambda e, tm=tm, acc=acc, fc=fc, accT=accT: e.tensor_tensor(out=accT[:, fc, 0:nt], in0=acc[:, 0:nt], in1=tm[:, 0:nt], op=ALU.add),
                                     reads=[tmk, acck], writes=[(accTk, fc)])
                if fi_ + 1 < len(fblocks):
                    f_yg(fblocks[fi_ + 1])
                pending_epi = make_epilogue(bi, accT, accTk)
            for part in pending_epi:
                part()
        s.barrier()

    s.finish(final_groups=[g for g in ("outf:('fv', 0)", "outf:('fv', 1)")])
    return nc, s


def _prep_shared(inp):
    f = np.float32
    L = DEPTH
    p64, p32 = _perm(64, 16), _perm(32, 8)
    w_in = np.ascontiguousarray(inp["w_in"], dtype=f)
    dq = np.concatenate([1440 + h * 64 + p64 for h in range(4)])
    dk = np.concatenate([1696 + h * 64 + p64 for h in range(2)])
    kr = 1152 + p32
    w_insw = np.ascontiguousarray(w_in[:, :, np.concatenate([dq, dk, kr])])
    w_uq = np.ascontiguousarray(inp["mla_w_uq"], dtype=f)
    uq_idx = np.concatenate([np.concatenate([h * 96 + np.arange(64), h * 96 + 64 + p32]) for h in range(4)])
    w_uqsw = np.ascontiguousarray(w_uq[:, :, uq_idx])
    ukv = np.asarray(inp["mla_w_ukv"], dtype=f).reshape(L, 128, 4, 128)
    w_ukv = np.ascontiguousarray(np.concatenate([ukv[:, :, :, :64].reshape(L, 128, 256), ukv[:, :, :, 64:].reshape(L, 128, 256)], -1))
    w_br = np.ascontiguousarray(np.asarray(inp["w_branch"], dtype=f).reshape(L, 1024, 1024))
    lruw = np.zeros((L, 2, 2, 2, 128, 128), f)
    for gi, nm in enumerate(("lru_w_a", "lru_w_x")):
        w = np.asarray(inp[nm], dtype=f)
        for ch in range(2):
            for k2 in range(2):
                lruw[:, :, gi, ch, k2 * 64:(k2 + 1) * 64, k2 * 64:(k2 + 1) * 64] = w[:, :, ch * 2 + k2]
    cols = np.zeros((L, 128, NCOL), f)
    bm = np.asarray(inp["b_mod"], dtype=f)
    cols[:, :, 0:16] = bm[:, 0:2048].reshape(L, 16, 128).transpose(0, 2, 1)
    cols[:, :, 16:18] = np.asarray(inp["mla_q_norm"], f).reshape(L, 2, 128).transpose(0, 2, 1)
    cols[:, :, 18] = np.asarray(inp["mla_kv_norm"], f)
    gq = np.asarray(inp["gqa_q_norm"], f)
    gk = np.asarray(inp["gqa_k_norm"], f)
    cols[:, :, 19] = np.tile(gq, (1, 2))
    cols[:, :, 20] = np.tile(gq[:, p64], (1, 2))
    cols[:, :, 21] = np.tile(gk, (1, 2))
    cols[:, :, 22] = np.tile(gk[:, p64], (1, 2))
    cw = np.asarray(inp["lru_conv_w"], f)
    for ch in range(2):
        for j in range(4):
            cols[:, :, 23 + ch * 4 + j] = cw[:, j, ch * 128:(ch + 1) * 128]
        cols[:, :, 31 + ch] = np.asarray(inp["lru_conv_b"], f)[:, ch * 128:(ch + 1) * 128]
        for d in range(2):
            cols[:, :, 33 + d * 2 + ch] = np.asarray(inp["lru_b_a"], f)[:, d, ch * 128:(ch + 1) * 128]
            cols[:, :, 37 + d * 2 + ch] = np.asarray(inp["lru_b_x"], f)[:, d, ch * 128:(ch + 1) * 128]
            cols[:, :, 41 + d * 2 + ch] = np.asarray(inp["lru_lambda"], f)[:, d, ch * 128:(ch + 1) * 128]
    rep = np.zeros((L, 3, 128, 1024), f)
    rep[:, 0] = bm[:, None, 2048:3072]
    rep[:, 1] = np.asarray(inp["ln_g"], f)[:, None, :]
    rep[:, 2] = np.asarray(inp["ln_b"], f)[:, None, :]
    tabD, tabB, tabK = _rope_tables()
    nab = np.stack([_na_bias_tiles(np.asarray(inp["na_rel_bias"], f)[l]) for l in range(L)])
    cst = np.zeros((4, 128, 128), f)
    cst[0] = np.eye(128, dtype=f)
    cst[1] = 1.0 / 256
    cst[2] = 1.0 / 128
    cst[3, :64, :64] = 1.0 / 64
    cst[3, 64:, 64:] = 1.0 / 64
    return dict(w_mod=np.ascontiguousarray(inp["w_mod"], dtype=f), w_in=w_in, w_insw=w_insw, w_uq=w_uq, w_uqsw=w_uqsw,
                w_ukv=w_ukv, w_br=w_br, w_out=np.ascontiguousarray(inp["w_out"], dtype=f), lruw=lruw, cols=cols, rep=rep,
                tabD=tabD, tabB=tabB, tabK=tabK, nab=nab, cst=cst)


def make_in_maps(inp, n_cores=8):
    shared = _prep_shared(inp)
    x = np.asarray(inp["x"], np.float32)
    ctx = np.asarray(inp["ctx"], np.float32)
    c = np.asarray(inp["c"], np.float32)
    cc = np.asarray(inp["c_ctx"], np.float32)
    maps = []
    for bidx in range(n_cores):
        cv = np.stack([c[bidx].reshape(8, 128).T, cc.reshape(8, 128).T], -1).astype(np.float32)
        m = dict(shared)
        m["x"] = np.ascontiguousarray(x[bidx])
        m["ctx"] = np.ascontiguousarray(ctx[bidx])
        m["cv"] = np.ascontiguousarray(cv)
        maps.append(m)
    return maps


def kernel(**inputs):
    nc, _ = build_program(debug=False)
    maps = make_in_maps(inputs, 8)
    res = run_bass_kernel_spmd(nc, maps, core_ids=list(range(8)))
    return np.stack([np.asarray(r["out"], np.float32) for r in res.results], 0)
```

```python
from contextlib import ExitStack
import numpy as np
import ml_dtypes
import concourse.bass as bass
import concourse.mybir as mybir
from concourse.bass_utils import run_bass_kernel_spmd

F32 = mybir.dt.float32
BF16 = mybir.dt.bfloat16
AF = mybir.ActivationFunctionType
ALU = mybir.AluOpType
AX = mybir.AxisListType

D_MODEL = 1024
SEQ = 4096
CTX = 256
T = SEQ + CTX
DEPTH = 2
GRID_W = 64
WIN_H, WIN_W = 8, 16
EPS = 1e-6
THETA = 10000.0
NA_SCALE = 64 ** -0.5
MLA_SCALE = 96 ** -0.5
GQA_SCALE = 64 ** -0.5
ALPHA = (2 * DEPTH) ** 0.25
MIX_COLS = 1952
N_IN = 7072
NEG = -200.0
BLOCKS = [(0, 256)] + [(256 + 512 * j, 512) for j in range(8)]
NCOL = 45


STRICT_SAME_ENGINE = True


class _Op:
    __slots__ = ("eng", "fn", "reads", "writes", "group", "idx", "waits", "sig", "is_dma", "gval", "bar")


class _Rec:
    def __init__(self):
        self.call = None

    def __getattr__(self, name):
        def f(*a, **k):
            self.call = (name, a, k)
        return f


class Sched:
    def __init__(self, nc):
        self.nc = nc
        self.ops = []
        self.stack = ExitStack()

    def op(self, eng, fn, reads=(), writes=()):
        o = _Op()
        if fn is not None:
            rec = _Rec()
            fn(rec)
            assert rec.call is not None
            fn = rec.call
        o.eng, o.fn, o.reads, o.writes = eng, fn, tuple(reads), tuple(writes)
        o.group, o.is_dma, o.sig, o.waits, o.gval, o.bar = None, False, False, [], 0, False
        self.ops.append(o)
        return o

    def dma(self, eng, out, in_, reads=(), writes=(), group=None):
        o = self.op(eng, lambda e: e.dma_start(out=out, in_=in_), reads, writes)
        o.is_dma, o.group = True, group
        assert group is not None
        return o

    def barrier(self):
        o = self.op("*", None)
        o.bar = True

    def finish(self, final_groups):
        nc, ops = self.nc, self.ops
        ENGS = ["pe", "act", "dve", "pool", "sp"]
        last_writer, readers = {}, {}
        eng_ops = {e: [] for e in ENGS}
        last_comp = {e: 0 for e in ENGS}
        grp_cnt = {}
        known = {e: {} for e in ENGS}
        pending = {e: [] for e in ENGS}
        for gi, o in enumerate(ops):
            if o.bar:
                for e in ENGS:
                    w = []
                    for e2 in ENGS:
                        if (e2 != e or e != "pe") and last_comp[e2] > 0:
                            w.append((("e", e2), last_comp[e2]))
                    for g, c in grp_cnt.items():
                        w.append((("g", g), c))
                    pending[e] = w
                last_writer, readers = {}, {}
                continue
            lst = eng_ops[o.eng]
            lst.append(o)
            o.idx = len(lst)
            raw, deps = set(), set()
            for r in o.reads:
                if r in last_writer:
                    deps.add(last_writer[r]); raw.add(last_writer[r])
            for w in o.writes:
                if w in last_writer:
                    deps.add(last_writer[w])
                for rd in readers.get(w, ()):
                    deps.add(rd)
            deps.discard(gi)
            need = {}
            for key, val in pending[o.eng]:
                if val > need.get(key, 0):
                    need[key] = val
            pending[o.eng] = []
            for d in deps:
                p = ops[d]
                if p.is_dma:
                    key, val = ("g", p.group), grp_cnt[p.group]
                else:
                    if p.eng == o.eng and not o.is_dma:
                        if o.eng == "pe" or (d not in raw and not STRICT_SAME_ENGINE):
                            continue
                    key, val = ("e", p.eng), p.idx
                if val > need.get(key, 0):
                    need[key] = val
            kn = known[o.eng]
            for key, val in need.items():
                if kn.get(key, 0) >= val:
                    continue
                kn[key] = val
                o.waits.append((key, val))
                if key[0] == "e":
                    eng_ops[key[1]][val - 1].sig = True
            if o.is_dma:
                grp_cnt[o.group] = grp_cnt.get(o.group, 0) + 1
                o.gval = grp_cnt[o.group]
            else:
                last_comp[o.eng] = o.idx
            for r in o.reads:
                readers.setdefault(r, []).append(gi)
            for w in o.writes:
                last_writer[w] = gi
                readers[w] = []
        waited = {}
        for lst in eng_ops.values():
            for o in lst:
                for key, val in o.waits:
                    if key[0] == "g":
                        waited.setdefault(key[1], set()).add(val)
        for eng, lst in eng_ops.items():
            seen = {}
            for o in lst:
                for key, val in o.waits:
                    if key[0] == "g":
                        seen[key[1]] = max(seen.get(key[1], 0), val)
                if o.is_dma and o.gval > 1 and (o.gval - 1) in waited.get(o.group, ()) and seen.get(o.group, 0) < o.gval - 1:
                    o.waits.append((("g", o.group), o.gval - 1))
                    seen[o.group] = o.gval - 1
        sigcnt = {}
        for eng, lst in eng_ops.items():
            c, arr = 0, []
            for o in lst:
                if o.sig and not o.is_dma:
                    c += 1
                arr.append(c)
            sigcnt[eng] = arr
        st = self.stack
        esem = {eng: st.enter_context(nc.semaphore("s_" + eng)) for eng in ENGS}
        gsem = {g: st.enter_context(nc.semaphore("g_%d" % i)) for i, g in enumerate(grp_cnt)}
        self.n_sems = len(esem) + len(gsem)
        self.stats = {e: len(l) for e, l in eng_ops.items()}
        self.maxsem = {e: (a[-1] if a else 0) for e, a in sigcnt.items()}
        self.maxgrp = max(grp_cnt.values()) if grp_cnt else 0
        block = st.enter_context(nc.Block())
        attach = {"pe": block.tensor, "act": block.scalar, "dve": block.vector,
                  "pool": block.gpsimd, "sp": block.sync}

        def emit(eng, lst):
            def body(e):
                for o in lst:
                    for key, val in o.waits:
                        if key[0] == "e":
                            e.wait_ge(esem[key[1]], sigcnt[key[1]][val - 1])
                        else:
                            e.wait_ge(gsem[key[1]], 16 * val)
                    name, a, k = o.fn
                    ins = getattr(e, name)(*a, **k)
                    if o.is_dma:
                        ins.then_inc(gsem[o.group], 16)
                    elif o.sig:
                        ins.then_inc(esem[eng], 1)
                if eng == "sp":
                    for g in final_groups:
                        if g in gsem:
                            e.wait_ge(gsem[g], 16 * grp_cnt[g])
            attach[eng](body)

        for eng in ENGS:
            emit(eng, eng_ops[eng])
        st.close()


class Builder:
    def __init__(self, nc):
        self.nc = nc
        self.s = Sched(nc)
        self.lo = 16384
        self.p = self.lo
        self.hi = 229300
        self.cnt = 0
        self.psb = [nc.alloc_psum_tensor("psb%d" % i, [128, 512], F32) for i in range(8)]
        self.psi = 0

    def alloc(self, name, shape, dtype):
        n = 1
        for d in shape[1:]:
            n *= d
        nbytes = n * (2 if dtype == BF16 else 4)
        nbytes = (nbytes + 63) // 64 * 64
        self.cnt += 1
        t = self.nc.alloc_sbuf_tensor_at("%s_%d" % (name, self.cnt), list(shape), dtype, offset=self.p)
        self.p += nbytes
        assert self.p <= self.hi, ("SBUF overflow", name, self.p)
        return t

    def mark(self):
        return self.p

    def release(self, m):
        self.s.barrier()
        self.p = m

    def ps(self):
        i = self.psi % 8
        self.psi += 1
        return self.psb[i], ("psb", i)


class RPool:
    def __init__(self, b, name, shape, dtype, n):
        self.name = name
        self.bufs = [b.alloc(name, shape, dtype) for _ in range(n)]
        self.i = 0

    def next(self):
        k = self.i % len(self.bufs)
        self.i += 1
        return self.bufs[k], (self.name, k)


def _perm(width, half):
    idx = np.arange(width)
    return np.where((idx % (2 * half)) < half, idx + half, idx - half)


def _na_plan():
    variants, plan = {}, []
    n_rows = SEQ // GRID_W
    for r in range(n_rows):
        rs = int(np.clip(r - WIN_H // 2, 0, n_rows - WIN_H))
        g0 = rs & ~1
        tiles = []
        g = g0
        while g < rs + WIN_H:
            top = (g - r) if rs <= g < rs + WIN_H else None
            bot = (g + 1 - r) if rs <= g + 1 < rs + WIN_H else None
            key = (top, bot)
            if key not in variants:
                variants[key] = len(variants)
            tiles.append((g, variants[key]))
            g += 2
        plan.append(tiles)
    return variants, plan


NA_VARIANTS, NA_PLAN = _na_plan()
NV = len(NA_VARIANTS)


def _na_bias_tiles(rel_bias):
    H = rel_bias.shape[0]
    out = np.full((H, NV, 128, GRID_W), NEG, np.float32)
    qc = np.arange(GRID_W)
    cs = np.clip(qc - WIN_W // 2, 0, GRID_W - WIN_W)
    kc = np.arange(GRID_W)
    valid = (kc[:, None] >= cs[None, :]) & (kc[:, None] < cs[None, :] + WIN_W)
    dc = kc[:, None] - qc[None, :] + WIN_W - 1
    dc_c = np.clip(dc, 0, 2 * WIN_W - 2)
    for (top, bot), v in NA_VARIANTS.items():
        for half, d in ((0, top), (1, bot)):
            if d is None:
                continue
            g = rel_bias[:, d + WIN_H - 1, :][:, dc_c]
            out[:, v, half * 64:(half + 1) * 64, :] = np.where(valid[None], g, np.float32(NEG))
    return out


def _rope_tables():
    t = np.arange(SEQ)
    rows = (t // GRID_W).astype(np.float64)
    cols = (t % GRID_W).astype(np.float64)

    def tab(d_head_rot):
        half = d_head_rot // 2
        nf = half // 2
        inv = THETA ** (-np.arange(0, half, 2, dtype=np.float64) / half)
        C = np.ones((d_head_rot, T), np.float64)
        S = np.zeros((d_head_rot, T), np.float64)
        for ax, pos in enumerate((rows, cols)):
            ang = pos[None, :] * inv[:, None]
            c, s = np.cos(ang), np.sin(ang)
            b0 = ax * half
            C[b0:b0 + nf, CTX:] = c
            C[b0 + nf:b0 + half, CTX:] = c
            S[b0:b0 + nf, CTX:] = -s
            S[b0 + nf:b0 + half, CTX:] = s
        return C.astype(np.float32), S.astype(np.float32)

    C64, S64 = tab(64)
    C32, S32 = tab(32)
    tabD = np.stack([np.concatenate([C64, C64], 0), np.concatenate([S64, S64], 0)])
    CB = np.concatenate([np.ones((64, T), np.float32), C32], 0)
    SB = np.concatenate([np.zeros((64, T), np.float32), S32], 0)
    tabB = np.stack([CB, SB])
    tabK = np.stack([C32, S32])
    return tabD, tabB, tabK


def build_program(debug=False, PH="MPCABDF", n_layers=DEPTH):
    nc = bass.Bass("TRN2", target_bir_lowering=False)
    b = Builder(nc)
    s = b.s

    def din(name, shape, dt=F32):
        return nc.dram_tensor(name, list(shape), dt, kind="ExternalInput").ap()

    def dscr(name, shape, dt, out=False):
        return nc.dram_tensor(name, list(shape), dt, kind=("ExternalOutput" if out else "Internal")).ap()

    x_in = din("x", [SEQ, D_MODEL])
    ctx_in = din("ctx", [CTX, D_MODEL])
    cv_in = din("cv", [128, 8, 2])
    w_mod = din("w_mod", [DEPTH, 1024, 3072])
    w_in = din("w_in", [DEPTH, 1024, N_IN])
    w_insw = din("w_insw", [DEPTH, 1024, 416])
    w_uq = din("w_uq", [DEPTH, 256, 384])
    w_uqsw = din("w_uqsw", [DEPTH, 256, 384])
    w_ukv = din("w_ukv", [DEPTH, 128, 512])
    w_br = din("w_br", [DEPTH, 1024, 1024])
    w_out = din("w_out", [DEPTH, 1024, 1024])
    lruw = din("lruw", [DEPTH, 2, 2, 2, 128, 128])
    cols_in = din("cols", [DEPTH, 128, NCOL])
    rep_in = din("rep", [DEPTH, 3, 128, 1024])
    tabD = din("tabD", [2, 128, T])
    tabB = din("tabB", [2, 96, T])
    tabK = din("tabK", [2, 32, T])
    nab = din("nab", [DEPTH, 4, NV, 128, 64])
    cst = din("cst", [4, 128, 128])
    out_d = dscr("out", [SEQ, D_MODEL], F32, out=True)

    dbg = debug
    xn_d = dscr("xn_d", [1024, T], BF16, dbg)
    zs_d = dscr("zs_d", [1024, T], BF16, dbg)
    y_d = dscr("y_d", [1024, T], F32, dbg)
    qA_d = dscr("qA_d", [256, T], BF16, dbg)
    kA_d = dscr("kA_d", [256, T], BF16, dbg)
    vaug_d = dscr("vaug_d", [10, 128, 34, 128], BF16, dbg)
    qB_d = dscr("qB_d", [4, 96, T], BF16, dbg)
    kB_d = dscr("kB_d", [4, 96, T], BF16, dbg)
    qD_d = dscr("qD_d", [256, T], BF16, dbg)
    kD_d = dscr("kD_d", [128, T], BF16, dbg)
    x1_d = dscr("x1_d", [SEQ, D_MODEL], F32, dbg)
    ctx1_d = dscr("ctx1_d", [CTX, D_MODEL], F32, dbg)

    gcount = [0]

    def G(name, key=None):
        return name if key is None else "%s:%s" % (name, key)

    ident = b.alloc("ident", [128, 128], F32)
    ones256 = b.alloc("ones256", [128, 128], F32)
    ones128 = b.alloc("ones128", [128, 128], F32)
    bones64 = b.alloc("bones64", [128, 128], F32)
    ident_bf = b.alloc("ident_bf", [128, 128], BF16)
    for j, tt in enumerate((ident, ones256, ones128, bones64)):
        s.dma("sp", tt[:], cst[j], writes=[("cst", j)], group="cst")
    s.dma("pool", ident_bf[:], cst[0], writes=["ident_bf"], group="cstbf")
    cvs = b.alloc("cvs", [128, 8, 2], F32)
    scv = b.alloc("scv", [128, 8, 2], F32)
    s.dma("sp", cvs[:], cv_in, writes=["cvs"], group="cvs")
    s.op("act", lambda e: e.activation(out=scv[:], in_=cvs[:], func=AF.Silu), reads=["cvs"], writes=["scv"])
    CSTK = [("cst", j) for j in range(4)]
    epsc = b.alloc("epsc", [128, 1], F32)
    s.op("pool", lambda e: e.memset(epsc[:, :], EPS), writes=["epsc"])

    def rstd(out_ap, in_ap, reads, wkey):
        np_ = out_ap.shape[0]
        s.op("act", lambda e: e.activation(out=out_ap, in_=in_ap, func=AF.Sqrt, bias=epsc[0:np_, 0:1]), reads=list(reads) + ["epsc"], writes=[wkey])
        s.op("dve", lambda e: e.reciprocal(out=out_ap, in_=out_ap), reads=[wkey], writes=[wkey])

    base_mark = b.mark()
    onesb = b.alloc("onesb", [128, 34, 128], BF16)
    s.op("pool", lambda e: e.memset(onesb[:, :, :], 1.0), writes=["onesb"])
    for hh_ in range(10):
        s.dma("sp", vaug_d[hh_], onesb[:, :, :], reads=["onesb"], group="onesinit")

    for l in range(n_layers):
        need_ctx = l < DEPTH - 1
        x_src = x_in if l == 0 else x1_d
        c_src = ctx_in if l == 0 else ctx1_d
        x_dst = x1_d if l < DEPTH - 1 else out_d
        b.release(base_mark)
        cols = b.alloc("cols", [128, NCOL], F32)
        s.dma("sp", cols[:], cols_in[l], writes=["cols"], group=G("cols"))
        modfm = b.alloc("modfm", [128, 16, 2], F32)
        gate_rep = b.alloc("gate_rep", [128, 2, 1024], F32)
        lng = b.alloc("lng", [128, 1024], F32)
        lnb = b.alloc("lnb", [128, 1024], F32)
        s.dma("sp", lng[:], rep_in[l, 1], writes=["lng"], group=G("lng"))
        s.dma("sp", lnb[:], rep_in[l, 2], writes=["lnb"], group=G("lnb"))
        nsp = b.alloc("nsp", [128, 8], F32)
        layer_mark = b.mark()

        xrT = b.alloc("xrT", [128, 2, T], F32)
        pc_mark = b.mark()
        NP = 3392
        wP = b.alloc("wP", [128, 8, NP], BF16)
        for kc in range(8):
            for (c0, c1) in ((0, 1024), (1024, 2048), (2048, 2976)):
                s.dma("pool", wP[:, kc, c0:c1], w_in[l, kc * 128:(kc + 1) * 128, c0:c1], writes=[("wP", kc)], group=G("wP"))
            s.dma("pool", wP[:, kc, 2976:NP], w_insw[l, kc * 128:(kc + 1) * 128, :], writes=[("wP", kc)], group=G("wP"))
        WPK = [("wP", kc) for kc in range(8)]
        wuq = b.alloc("wuq", [128, 2, 384], BF16)
        wuqs = b.alloc("wuqs", [128, 2, 384], BF16)
        wukv = b.alloc("wukv", [128, 512], BF16)
        for ch in range(2):
            s.dma("pool", wuq[:, ch, :], w_uq[l, ch * 128:(ch + 1) * 128, :], writes=["wuq"], group=G("wuq"))
            s.dma("pool", wuqs[:, ch, :], w_uqsw[l, ch * 128:(ch + 1) * 128, :], writes=["wuqs"], group=G("wuq"))
        s.dma("pool", wukv[:, :], w_ukv[l], writes=["wukv"], group=G("wuq"))

        m_mark = b.mark()
        wm_pool = RPool(b, "wm", [128, 8, 512], F32, 4)
        screp = b.alloc("screp", [128, 2, 8, 128], F32)
        bgate = b.alloc("bgate", [128, 1024], F32)
        s.dma("sp", bgate[:], rep_in[l, 0], writes=["bgate"], group=G("bgate"))
        for j in range(2):
            for kc in range(8):
                s.op("act", lambda e, j=j, kc=kc: e.activation(out=screp[:, j, kc, :], in_=ones256[:, :], func=AF.Identity,
                                                                 scale=scv[:, kc, j:j + 1]),
                     reads=["scv", ("cst", 1)], writes=[("screp", j, kc)])
        for grp in range(6):
            wm, wmk = wm_pool.next()
            for kh in range(2):
                s.dma("sp", wm[:, kh * 4:(kh + 1) * 4, :],
                      w_mod[l].rearrange("(kc p) n -> p kc n", p=128)[:, kh * 4:(kh + 1) * 4, grp * 512:(grp + 1) * 512],
                      writes=[wmk], group=G("wm", wmk))
            if grp < 4:
                for q in range(4):
                    n = grp * 4 + q
                    ps, pk = b.ps()
                    for kc in range(8):
                        s.op("pe", lambda e, ps=ps, wm=wm, kc=kc, q=q: e.matmul(ps[:, 0:2], lhsT=wm[:, kc, q * 128:(q + 1) * 128],
                                                                                rhs=scv[:, kc, :], start=(kc == 0), stop=(kc == 7)),
                             reads=[wmk, "scv"], writes=[pk])
                    addc = 0.0 if n < 8 else 1.0
                    s.op("dve", lambda e, ps=ps, n=n, addc=addc: e.tensor_scalar(out=modfm[:, n, :], in0=ps[:, 0:2], scalar1=cols[:, n:n + 1],
                                                                                  scalar2=addc, op0=ALU.add, op1=ALU.add),
                         reads=[pk, "cols"], writes=[("modfm", n)])
            else:
                half = grp - 4
                for j in range(2):
                    if j == 1 and not need_ctx:
                        continue
                    ps, pk = b.ps()
                    for kc in range(8):
                        s.op("pe", lambda e, ps=ps, wm=wm, kc=kc, j=j: e.matmul(ps[:, :], lhsT=screp[:, j, kc, :], rhs=wm[:, kc, :],
                                                                                start=(kc == 0), stop=(kc == 7)),
                             reads=[wmk] + [("screp", j, kc)], writes=[pk])
                    s.op("dve", lambda e, ps=ps, j=j, half=half: e.scalar_tensor_tensor(
                        out=gate_rep[:, j, half * 512:(half + 1) * 512], in0=ps[:, :], scalar=256.0,
                        in1=bgate[:, half * 512:(half + 1) * 512], op0=ALU.mult, op1=ALU.add),
                         reads=[pk, "bgate"], writes=[("gate_rep", j, half)])
        MODK = [("modfm", n) for n in range(16)]
        GATEK = [("gate_rep", j, h) for j in range(2) for h in range(2)]
        sp = b.alloc("sp_tmp", [128, 8, 4], F32)
        lam = cols[:, 41:45]
        V = lambda i: sp[:, i, :]
        chain = []

        def dv(fn, rd, wr):
            s.op("dve", fn, reads=rd, writes=wr)
        s.op("act", lambda e: e.activation(out=V(0), in_=lam, func=AF.Abs), reads=["cols"], writes=[("sp", 0)])
        s.op("act", lambda e: e.activation(out=V(1), in_=V(0), func=AF.Exp, scale=-1.0), reads=[("sp", 0)], writes=[("sp", 1)])
        dv(lambda e: e.tensor_scalar(out=V(2), in0=V(1), scalar1=2.0, scalar2=None, op0=ALU.add), [("sp", 1)], [("sp", 2)])
        dv(lambda e: e.reciprocal(out=V(2), in_=V(2)), [("sp", 2)], [("sp", 2)])
        dv(lambda e: e.tensor_tensor(out=V(3), in0=V(1), in1=V(2), op=ALU.mult), [("sp", 1), ("sp", 2)], [("sp", 3)])
        dv(lambda e: e.tensor_tensor(out=V(4), in0=V(3), in1=V(3), op=ALU.mult), [("sp", 3)], [("sp", 4)])
        dv(lambda e: e.tensor_scalar(out=V(5), in0=V(4), scalar1=1.0 / 11, scalar2=1.0 / 9, op0=ALU.mult, op1=ALU.add), [("sp", 4)], [("sp", 5)])
        for cst_ in (1.0 / 7, 1.0 / 5, 1.0 / 3, 1.0):
            dv(lambda e: e.tensor_tensor(out=V(5), in0=V(5), in1=V(4), op=ALU.mult), [("sp", 5), ("sp", 4)], [("sp", 5)])
            dv(lambda e, c=cst_: e.tensor_scalar(out=V(5), in0=V(5), scalar1=c, scalar2=None, op0=ALU.add), [("sp", 5)], [("sp", 5)])
        dv(lambda e: e.tensor_tensor(out=V(5), in0=V(5), in1=V(3), op=ALU.mult), [("sp", 5), ("sp", 3)], [("sp", 5)])
        dv(lambda e: e.tensor_scalar(out=V(6), in0=lam, scalar1=-1.0, scalar2=0.0, op0=ALU.mult, op1=ALU.max), ["cols"], [("sp", 6)])
        dv(lambda e: e.scalar_tensor_tensor(out=V(7), in0=V(5), scalar=2.0, in1=V(6), op0=ALU.mult, op1=ALU.add),
           [("sp", 5), ("sp", 6)], [("sp", 7)])
        dv(lambda e: e.tensor_scalar(out=nsp[:, 0:4], in0=V(7), scalar1=-8.0, scalar2=None, op0=ALU.mult), [("sp", 7)], ["nsp"])
        dv(lambda e: e.tensor_scalar(out=nsp[:, 4:8], in0=V(7), scalar1=-16.0, scalar2=None, op0=ALU.mult), [("sp", 7)], ["nsp2"])

        b.release(m_mark)
        xt_pool = RPool(b, "xt", [128, 1024], F32, 4)
        st_pool = RPool(b, "st", [128, 16], F32, 4)
        xh_pool = RPool(b, "xh", [128, 4, 1024], F32, 1)
        xn_pool = RPool(b, "xn", [128, 8, 512], BF16, 2)
        f32_pool = RPool(b, "pf", [128, 512], F32, 8)
        bf_pool = RPool(b, "pb", [128, 512], BF16, 5)
        tab_pool = RPool(b, "tab", [128, 6, 512], F32, 1)
        cq_pool = RPool(b, "cqg", [128, 3, 512], BF16, 1)
        sq_pool = RPool(b, "sq", [128, 3, 512], F32, 1)
        vst_pool = RPool(b, "vst", [128, 640], BF16, 2)
        rc_pool = RPool(b, "rc", [128, 1], F32, 4)
        evi = [0]

        def evac_engine():
            evi[0] += 1
            return "act" if evi[0] % 2 else "dve"

        def pblock(bi):
            t0, nt = BLOCKS[bi]
            ntile = nt // 128
            is_ctx = bi == 0
            full = (not is_ctx) or need_ctx
            mc = 1 if is_ctx else 0
            src = c_src if is_ctx else x_src
            xts = []
            for i in range(ntile):
                xt, xtk = xt_pool.next()
                r0 = (t0 + i * 128) if is_ctx else (t0 - CTX + i * 128)
                s.dma("sp", xt[:, :], src[r0:r0 + 128, :], writes=[xtk], group=G("xt", xtk))
                xts.append((xt, xtk))
            yield "L"
            xh, xhk = xh_pool.next()
            for i in range(ntile):
                yield "a"
                xt, xtk = xts[i]
                st, stk = st_pool.next()
                s.op("dve", lambda e, st=st, xt=xt: e.bn_stats(out=st[:, 0:6], in_=xt[:, 0:512]), reads=[xtk], writes=[(stk, 0)])
                s.op("dve", lambda e, st=st, xt=xt: e.bn_stats(out=st[:, 6:12], in_=xt[:, 512:1024]), reads=[xtk], writes=[(stk, 1)])
                s.op("dve", lambda e, st=st: e.bn_aggr(out=st[:, 12:14], in_=st[:, 0:12].rearrange("p (a b) -> p a b", b=6)),
                     reads=[(stk, 0), (stk, 1)], writes=[(stk, 2)])
                rstd(st[:, 14:15], st[:, 13:14], [(stk, 2)], (stk, 3))
                s.op("dve", lambda e, st=st, xt=xt, xh=xh, i=i: e.tensor_scalar(out=xh[:, i, :], in0=xt[:, :], scalar1=st[:, 12:13],
                                                                                scalar2=st[:, 14:15], op0=ALU.subtract, op1=ALU.mult),
                     reads=[xtk, (stk, 2), (stk, 3)], writes=[(xhk, i)])
            yield "A"
            xn, xnk = xn_pool.next()
            for kc in range(8):
                yield "b"
                ps, pk = b.ps()
                for i in range(ntile):
                    s.op("pe", lambda e, ps=ps, xh=xh, i=i, kc=kc: e.transpose(ps[:, i * 128:(i + 1) * 128], xh[:, i, kc * 128:(kc + 1) * 128], ident[:, :]),
                         reads=[(xhk, i), ("cst", 0)], writes=[pk])
                eng = evac_engine()
                if eng == "act":
                    s.op("act", lambda e, ps=ps, xn=xn, kc=kc, mc=mc, nt=nt: e.activation(
                        out=xn[:, kc, 0:nt], in_=ps[:, 0:nt], func=AF.Identity, scale=modfm[:, 8 + kc, mc:mc + 1], bias=modfm[:, kc, mc:mc + 1]),
                         reads=[pk] + MODK, writes=[(xnk, kc)])
                else:
                    s.op("dve", lambda e, ps=ps, xn=xn, kc=kc, mc=mc, nt=nt: e.tensor_scalar(
                        out=xn[:, kc, 0:nt], in0=ps[:, 0:nt], scalar1=modfm[:, 8 + kc, mc:mc + 1], scalar2=modfm[:, kc, mc:mc + 1],
                        op0=ALU.mult, op1=ALU.add), reads=[pk] + MODK, writes=[(xnk, kc)])
            XNK = [(xnk, kc) for kc in range(8)]
            if full:
                s.dma("sp", xn_d.rearrange("(kc p) t -> p kc t", p=128)[:, :, t0:t0 + nt], xn[:, :, 0:nt], reads=XNK, group=G("xn_d", xnk))
            yield "B"
            tb, tbk = ptab[bi]

            def fm(c0, width):
                ps, pk = b.ps()
                for kc in range(8):
                    s.op("pe", lambda e, ps=ps, kc=kc: e.matmul(ps[0:width, 0:nt], lhsT=wP[:, kc, c0:c0 + width], rhs=xn[:, kc, 0:nt],
                                                                start=(kc == 0), stop=(kc == 7)),
                         reads=[("wP", kc), (xnk, kc)], writes=[pk])
                return ps, pk

            def store_fm(dst_ap, rows, src_t, src_k):
                s.dma("sp", dst_ap, src_t[0:rows, 0:nt], reads=[src_k], group=G("st", src_k))

            for ch in range(2):
                yield "c"
                if full:
                    ps, pk = fm(ch * 128, 128)
                    o, ok = bf_pool.next()
                    s.op("act", lambda e, ps=ps, o=o: e.activation(out=o[:, 0:nt], in_=ps[:, 0:nt], func=AF.Copy, scale=NA_SCALE),
                         reads=[pk], writes=[ok])
                    store_fm(qA_d[ch * 128:(ch + 1) * 128, t0:t0 + nt], 128, o, ok)
                ps, pk = fm(256 + ch * 128, 128)
                o, ok = bf_pool.next()
                s.op("dve", lambda e, ps=ps, o=o: e.tensor_copy(out=o[:, 0:nt], in_=ps[:, 0:nt]), reads=[pk], writes=[ok])
                store_fm(kA_d[ch * 128:(ch + 1) * 128, t0:t0 + nt], 128, o, ok)
            yield "C0"
            for ch in range(2):
                yield "c"
                ps, pk = fm(1184 + ch * 128, 128)
                s.op("act", lambda e, ps=ps, ch=ch: e.activation(out=xrT[:, ch, t0:t0 + nt], in_=ps[:, 0:nt], func=AF.Copy),
                     reads=[pk], writes=[("xrT", ch, bi)])
            if full:
                for j in range(8):
                    yield "c"
                    ps, pk = fm(1952 + j * 128, 128)
                    o, ok = bf_pool.next()
                    s.op("act", lambda e, ps=ps, o=o: e.activation(out=o[:, 0:nt], in_=ps[:, 0:nt], func=AF.Silu), reads=[pk], writes=[ok])
                    store_fm(zs_d[j * 128:(j + 1) * 128, t0:t0 + nt], 128, o, ok)
            yield "C1"
            dlist = []
            if full:
                dlist += [("q", 1440, 2976, 19, 20, qD_d, 0), ("q", 1568, 3104, 19, 20, qD_d, 128)]
            dlist += [("k", 1696, 3232, 21, 22, kD_d, 0)]
            for (kind, ca, cb_, gcol, gscol, dst, drow) in dlist:
                yield "c"
                psa, pka = fm(ca, 128)
                psb_, pkb = fm(cb_, 128)
                sqt, sqk = f32_pool.next()
                s.op("act", lambda e, psa=psa, sqt=sqt: e.activation(out=sqt[:, 0:nt], in_=psa[:, 0:nt], func=AF.Square), reads=[pka], writes=[sqk])
                psm, pkm = b.ps()
                s.op("pe", lambda e, psm=psm, sqt=sqt: e.matmul(psm[:, 0:nt], lhsT=bones64[:, :], rhs=sqt[:, 0:nt], start=True, stop=True),
                     reads=[sqk, ("cst", 3)], writes=[pkm])
                rs_, rsk = f32_pool.next()
                rstd(rs_[:, 0:nt], psm[:, 0:nt], [pkm], rsk)
                e1, e1k = f32_pool.next()
                e2, e2k = f32_pool.next()
                s.op("act", lambda e, psa=psa, e1=e1, gcol=gcol: e.activation(out=e1[:, 0:nt], in_=psa[:, 0:nt], func=AF.Identity,
                                                                              scale=cols[:, gcol:gcol + 1]), reads=[pka, "cols"], writes=[e1k])
                s.op("act", lambda e, psb_=psb_, e2=e2, gscol=gscol: e.activation(out=e2[:, 0:nt], in_=psb_[:, 0:nt], func=AF.Identity,
                                                                                  scale=cols[:, gscol:gscol + 1]), reads=[pkb, "cols"], writes=[e2k])
                s.op("pool", lambda e, e1=e1: e.tensor_tensor(out=e1[:, 0:nt], in0=e1[:, 0:nt], in1=tb[:, 0, 0:nt], op=ALU.mult),
                     reads=[e1k, (tbk, "D")], writes=[e1k])
                s.op("pool", lambda e, e2=e2: e.tensor_tensor(out=e2[:, 0:nt], in0=e2[:, 0:nt], in1=tb[:, 1, 0:nt], op=ALU.mult),
                     reads=[e2k, (tbk, "D")], writes=[e2k])
                s.op("pool", lambda e, e1=e1, e2=e2: e.tensor_tensor(out=e1[:, 0:nt], in0=e1[:, 0:nt], in1=e2[:, 0:nt], op=ALU.add),
                     reads=[e1k, e2k], writes=[e1k])
                o, ok = bf_pool.next()
                sc_ = GQA_SCALE if kind == "q" else 1.0
                s.op("dve", lambda e, e1=e1, rs_=rs_, o=o, sc_=sc_: e.scalar_tensor_tensor(out=o[:, 0:nt], in0=e1[:, 0:nt], scalar=sc_, in1=rs_[:, 0:nt],
                                                                                           op0=ALU.mult, op1=ALU.mult), reads=[e1k, rsk], writes=[ok])
                store_fm(dst[drow:drow + 128, t0:t0 + nt], 128, o, ok)
            cq, cqk = cq_pool.next()
            sq, sqk3 = sq_pool.next()
            chunks = ([(0, 768, 16), (1, 896, 17)] if full else []) + [(2, 1024, 18)]
            for (slot, c0, gcol) in chunks:
                yield "c"
                ps, pk = fm(c0, 128)
                s.op("act", lambda e, ps=ps, slot=slot, gcol=gcol: e.activation(out=cq[:, slot, 0:nt], in_=ps[:, 0:nt], func=AF.Identity,
                                                                                scale=cols[:, gcol:gcol + 1]), reads=[pk, "cols"], writes=[(cqk, slot)])
                s.op("act", lambda e, ps=ps, slot=slot: e.activation(out=sq[:, slot, 0:nt], in_=ps[:, 0:nt], func=AF.Square),
                     reads=[pk], writes=[(sqk3, slot)])
            if full:
                psm, pkm = b.ps()
                for ch in range(2):
                    s.op("pe", lambda e, psm=psm, ch=ch: e.matmul(psm[:, 0:nt], lhsT=ones256[:, :], rhs=sq[:, ch, 0:nt], start=(ch == 0), stop=(ch == 1)),
                         reads=[(sqk3, ch), ("cst", 1)], writes=[pkm])
                rq, rqk = f32_pool.next()
                rstd(rq[:, 0:nt], psm[:, 0:nt], [pkm], rqk)
                cp, cpk = f32_pool.next()
                sp_, spk = f32_pool.next()
                s.op("dve", lambda e, cp=cp, rq=rq: e.scalar_tensor_tensor(out=cp[0:96, 0:nt], in0=tb[0:96, 2, 0:nt], scalar=MLA_SCALE, in1=rq[0:96, 0:nt],
                                                                           op0=ALU.mult, op1=ALU.mult), reads=[(tbk, "B"), rqk], writes=[cpk])
                s.op("dve", lambda e, sp_=sp_, rq=rq: e.scalar_tensor_tensor(out=sp_[0:96, 0:nt], in0=tb[0:96, 3, 0:nt], scalar=MLA_SCALE, in1=rq[0:96, 0:nt],
                                                                             op0=ALU.mult, op1=ALU.mult), reads=[(tbk, "B"), rqk], writes=[spk])
                for h in range(4):
                    yield "c"
                    psa, pka = b.ps()
                    psb_, pkb = b.ps()
                    for ch in range(2):
                        s.op("pe", lambda e, psa=psa, ch=ch, h=h: e.matmul(psa[0:96, 0:nt], lhsT=wuq[:, ch, h * 96:(h + 1) * 96], rhs=cq[:, ch, 0:nt],
                                                                           start=(ch == 0), stop=(ch == 1)), reads=["wuq", (cqk, ch)], writes=[pka])
                    for ch in range(2):
                        s.op("pe", lambda e, psb_=psb_, ch=ch, h=h: e.matmul(psb_[0:96, 0:nt], lhsT=wuqs[:, ch, h * 96:(h + 1) * 96], rhs=cq[:, ch, 0:nt],
                                                                             start=(ch == 0), stop=(ch == 1)), reads=["wuqs", (cqk, ch)], writes=[pkb])
                    t1, t1k = f32_pool.next()
                    t2, t2k = f32_pool.next()
                    s.op("dve", lambda e, psa=psa, t1=t1, cp=cp: e.tensor_tensor(out=t1[0:96, 0:nt], in0=psa[0:96, 0:nt], in1=cp[0:96, 0:nt], op=ALU.mult),
                         reads=[pka, cpk], writes=[t1k])
                    s.op("dve", lambda e, psb_=psb_, t2=t2, sp_=sp_: e.tensor_tensor(out=t2[0:96, 0:nt], in0=psb_[0:96, 0:nt], in1=sp_[0:96, 0:nt], op=ALU.mult),
                         reads=[pkb, spk], writes=[t2k])
                    o, ok = bf_pool.next()
                    s.op("pool", lambda e, t1=t1, t2=t2, o=o: e.tensor_tensor(out=o[0:96, 0:nt], in0=t1[0:96, 0:nt], in1=t2[0:96, 0:nt], op=ALU.add),
                         reads=[t1k, t2k], writes=[ok])
                    store_fm(qB_d[h, :, t0:t0 + nt], 96, o, ok)
            psm, pkm = b.ps()
            s.op("pe", lambda e, psm=psm: e.matmul(psm[:, 0:nt], lhsT=ones128[:, :], rhs=sq[:, 2, 0:nt], start=True, stop=True),
                 reads=[(sqk3, 2), ("cst", 2)], writes=[pkm])
            rkv, rkvk = f32_pool.next()
            rstd(rkv[:, 0:nt], psm[:, 0:nt], [pkm], rkvk)
            for h in range(4):
                yield "c"
                ps, pk = b.ps()
                s.op("pe", lambda e, ps=ps, h=h: e.matmul(ps[0:64, 0:nt], lhsT=wukv[:, h * 64:(h + 1) * 64], rhs=cq[:, 2, 0:nt], start=True, stop=True),
                     reads=["wukv", (cqk, 2)], writes=[pk])
                o, ok = bf_pool.next()
                s.op("dve", lambda e, ps=ps, o=o, rkv=rkv: e.tensor_tensor(out=o[0:64, 0:nt], in0=ps[0:64, 0:nt], in1=rkv[0:64, 0:nt], op=ALU.mult),
                     reads=[pk, rkvk], writes=[ok])
                store_fm(kB_d[h, 0:64, t0:t0 + nt], 64, o, ok)
            psa, pka = fm(1152, 32)
            psb_, pkb = fm(3360, 32)
            t1, t1k = f32_pool.next()
            t2, t2k = f32_pool.next()
            s.op("dve", lambda e, psa=psa, t1=t1: e.tensor_tensor(out=t1[0:32, 0:nt], in0=psa[0:32, 0:nt], in1=tb[0:32, 4, 0:nt], op=ALU.mult),
                 reads=[pka, (tbk, "K")], writes=[t1k])
            s.op("dve", lambda e, psb_=psb_, t2=t2: e.tensor_tensor(out=t2[0:32, 0:nt], in0=psb_[0:32, 0:nt], in1=tb[0:32, 5, 0:nt], op=ALU.mult),
                 reads=[pkb, (tbk, "K")], writes=[t2k])
            o, ok = bf_pool.next()
            s.op("pool", lambda e, t1=t1, t2=t2, o=o: e.tensor_tensor(out=o[0:32, 0:nt], in0=t1[0:32, 0:nt], in1=t2[0:32, 0:nt], op=ALU.add),
                 reads=[t1k, t2k], writes=[ok])
            for h in range(4):
                store_fm(kB_d[h, 64:96, t0:t0 + nt], 32, o, ok)
            for i in range(ntile):
                yield "c"
                vs, vsk = vst_pool.next()
                tok = slice(i * 128, (i + 1) * 128)
                psa, pka = b.ps()
                for kc in range(8):
                    s.op("pe", lambda e, psa=psa, kc=kc, tok=tok: e.matmul(psa[:, 0:256], lhsT=xn[:, kc, tok], rhs=wP[:, kc, 512:768],
                                                                           start=(kc == 0), stop=(kc == 7)), reads=[("wP", kc), (xnk, kc)], writes=[pka])
                psd, pkd = b.ps()
                for kc in range(8):
                    s.op("pe", lambda e, psd=psd, kc=kc, tok=tok: e.matmul(psd[:, 0:128], lhsT=xn[:, kc, tok], rhs=wP[:, kc, 1824:1952],
                                                                           start=(kc == 0), stop=(kc == 7)), reads=[("wP", kc), (xnk, kc)], writes=[pkd])
                s.op("dve", lambda e, psa=psa, vs=vs: e.tensor_copy(out=vs[:, 0:256], in_=psa[:, 0:256]), reads=[pka], writes=[(vsk, 0)])
                s.op("act", lambda e, psd=psd, vs=vs: e.activation(out=vs[:, 256:384], in_=psd[:, 0:128], func=AF.Copy), reads=[pkd], writes=[(vsk, 1)])
                psc, pkc = b.ps()
                s.op("pe", lambda e, psc=psc, tok=tok: e.matmul(psc[:, 0:1], lhsT=sq[:, 2, tok], rhs=ones128[:, 0:1], start=True, stop=True),
                     reads=[(sqk3, 2), ("cst", 2)], writes=[pkc])
                rc, rck = rc_pool.next()
                rstd(rc[:, 0:1], psc[:, 0:1], [pkc], rck)
                psv, pkv = b.ps()
                s.op("pe", lambda e, psv=psv, tok=tok: e.matmul(psv[:, 0:256], lhsT=cq[:, 2, tok], rhs=wukv[:, 256:512], start=True, stop=True),
                     reads=["wukv", (cqk, 2)], writes=[pkv])
                s.op("act", lambda e, psv=psv, vs=vs, rc=rc: e.activation(out=vs[:, 384:640], in_=psv[:, 0:256], func=AF.Identity, scale=rc[:, 0:1]),
                     reads=[pkv, rck], writes=[(vsk, 2)])
                gt = (t0 + i * 128) // 128
                s.dma("sp", vaug_d[0:4, :, gt, 0:64].rearrange("h p c -> p h c"), vs[:, 0:256].rearrange("p (h c) -> p h c", c=64),
                      reads=[(vsk, 0)], group=G("st", vsk))
                s.dma("sp", vaug_d[8:10, :, gt, 0:64].rearrange("h p c -> p h c"), vs[:, 256:384].rearrange("p (h c) -> p h c", c=64),
                      reads=[(vsk, 1)], group=G("st", vsk))
                s.dma("sp", vaug_d[4:8, :, gt, 0:64].rearrange("h p c -> p h c"), vs[:, 384:640].rearrange("p (h c) -> p h c", c=64),
                      reads=[(vsk, 2)], group=G("st", vsk))

        ptab = {}

        def load_tables(bi):
            t0, nt = BLOCKS[bi]
            tb, tbk = tab_pool.next()
            s.dma("sp", tb[:, 0:2, 0:nt], tabD[:, :, t0:t0 + nt].rearrange("a p t -> p a t"), writes=[(tbk, "D")], group=G("tab"))
            s.dma("sp", tb[0:96, 2:4, 0:nt], tabB[:, :, t0:t0 + nt].rearrange("a p t -> p a t"), writes=[(tbk, "B")], group=G("tab"))
            s.dma("sp", tb[0:32, 4:6, 0:nt], tabK[:, :, t0:t0 + nt].rearrange("a p t -> p a t"), writes=[(tbk, "K")], group=G("tab"))
            ptab[bi] = (tb, tbk)

        gens = [pblock(bi) for bi in range(len(BLOCKS))]

        def advance(bi_, stops):
            for tag in gens[bi_]:
                if tag in stops:
                    return tag
            return None

        load_tables(0)
        advance(0, ("B",))
        for j in range(len(BLOCKS)):
            has_next = j + 1 < len(BLOCKS)
            if has_next:
                advance(j + 1, ("L",))
            advance(j, ("C0",))
            if has_next:
                advance(j + 1, ("A",))
            advance(j, ("C1",))
            if has_next:
                advance(j + 1, ("B",))
            advance(j, ())
            if has_next:
                load_tables(j + 1)

        b.release(pc_mark)
        if "C" in PH:
            wl = b.alloc("wl", [128, 2, 2, 2, 128], F32)
            for d in range(2):
                for g_ in range(2):
                    for ch in range(2):
                        s.dma("sp", wl[:, d, g_, ch, :], lruw[l, d, g_, ch], writes=["wl"], group=G("wl"))
            xcs = [b.alloc("xc%d" % ch_, [128, T], F32) for ch_ in range(2)]
            Abuf = b.alloc("Abuf", [128, T], F32)
            Ubuf = b.alloc("Ubuf", [128, T], F32)
            H0 = b.alloc("H0", [128, T], F32)
            H1 = b.alloc("H1", [128, T], F32)
            Hs = [H0, H1]
            NBK = len(BLOCKS)
            XRK = lambda ch: [("xrT", ch, bi) for bi in range(NBK)]
            AK = [("A", bi) for bi in range(NBK)]
            UK = ["U"]
            HK = lambda d: [("H", d, bi) for bi in range(NBK)]
            for ch in range(2):
                xr = xrT[:, ch, :]
                xc = xcs[ch]
                s.op("dve", lambda e: e.tensor_scalar(out=xc[:, :], in0=xr, scalar1=cols[:, 23 + ch * 4 + 2:24 + ch * 4 + 2],
                                                      scalar2=cols[:, 31 + ch:32 + ch], op0=ALU.mult, op1=ALU.add),
                     reads=XRK(ch) + ["cols"], writes=[("xc", ch)])
                for (s0, s1) in ((0, CTX), (CTX, T)):
                    for j, off in ((0, -2), (1, -1), (3, 1)):
                        lo = max(s0, s0 - off)
                        hi_ = min(s1, s1 - off)
                        s.op("dve", lambda e: e.scalar_tensor_tensor(
                            out=xc[:, lo:hi_], in0=xr[:, lo + off:hi_ + off], scalar=cols[:, 23 + ch * 4 + j:24 + ch * 4 + j],
                            in1=xc[:, lo:hi_], op0=ALU.mult, op1=ALU.add), reads=XRK(ch) + ["cols", ("xc", ch)], writes=[("xc", ch)])
            for ch in range(2):
                xc = xcs[ch]
                for d in range(2):
                    Hd = Hs[d]
                    ca = 33 + d * 2 + ch
                    cx = 37 + d * 2 + ch
                    cn = d * 2 + ch
                    for bi, (t0, nt) in enumerate(BLOCKS):
                        sl = slice(t0, t0 + nt)
                        psa, pka = b.ps()
                        s.op("pe", lambda e: e.matmul(psa[:, 0:nt], lhsT=wl[:, d, 0, ch, :], rhs=xc[:, sl], start=True, stop=True),
                             reads=["wl", ("xc", ch)], writes=[pka])
                        psx, pkx = b.ps()
                        s.op("pe", lambda e: e.matmul(psx[:, 0:nt], lhsT=wl[:, d, 1, ch, :], rhs=xc[:, sl], start=True, stop=True),
                             reads=["wl", ("xc", ch)], writes=[pkx])
                        s.op("act", lambda e: e.activation(out=Abuf[:, sl], in_=psa[:, 0:nt], func=AF.Sigmoid, bias=cols[:, ca:ca + 1]),
                             reads=[pka, "cols"], writes=[("A", bi)])
                        s.op("act", lambda e: e.activation(out=Hd[:, sl], in_=psx[:, 0:nt], func=AF.Sigmoid, bias=cols[:, cx:cx + 1]),
                             reads=[pkx, "cols"], writes=[("H", d, bi)])
                    s.op("pool", lambda e: e.tensor_tensor(out=Hd[:, :], in0=Hd[:, :], in1=xc[:, :], op=ALU.mult),
                         reads=HK(d) + [("xc", ch)], writes=HK(d))
                    s.op("act", lambda e: e.activation(out=Ubuf[:, :], in_=Abuf[:, :], func=AF.Exp, scale=nsp[:, 4 + cn:5 + cn]),
                         reads=AK + ["nsp2"], writes=UK)
                    s.op("act", lambda e: e.activation(out=Abuf[:, :], in_=Abuf[:, :], func=AF.Exp, scale=nsp[:, cn:cn + 1]),
                         reads=AK + ["nsp"], writes=AK)
                    s.op("dve", lambda e: e.tensor_scalar(out=Ubuf[:, :], in0=Ubuf[:, :], scalar1=-1.0, scalar2=1.0, op0=ALU.mult, op1=ALU.add),
                         reads=UK, writes=UK)
                    s.op("act", lambda e: e.activation(out=Ubuf[:, :], in_=Ubuf[:, :], func=AF.Sqrt), reads=UK, writes=UK)
                    s.op("dve", lambda e: e.tensor_tensor(out=Ubuf[:, :], in0=Ubuf[:, :], in1=Hd[:, :], op=ALU.mult),
                         reads=UK + HK(d), writes=UK)
                    if d == 0:
                        s.op("dve", lambda e: e.tensor_tensor_scan(out=Hd[:, :], data0=Abuf[:, :], data1=Ubuf[:, :], initial=0.0, op0=ALU.mult, op1=ALU.add),
                             reads=AK + UK + HK(d), writes=HK(d))
                    else:
                        s.op("dve", lambda e: e.tensor_tensor_scan(out=Hd[:, 0:CTX][:, ::-1], data0=Abuf[:, 0:CTX][:, ::-1], data1=Ubuf[:, 0:CTX][:, ::-1],
                                                                   initial=0.0, op0=ALU.mult, op1=ALU.add), reads=AK + UK + HK(d), writes=[("H", d, 0)])
                        s.op("dve", lambda e: e.tensor_tensor_scan(out=Hd[:, CTX:T][:, ::-1], data0=Abuf[:, CTX:T][:, ::-1], data1=Ubuf[:, CTX:T][:, ::-1],
                                                                   initial=Hd[:, 0:1], op0=ALU.mult, op1=ALU.add),
                             reads=AK + UK + HK(d), writes=HK(d)[1:])
                s.op("pool", lambda e: e.tensor_tensor(out=H0[:, :], in0=H0[:, :], in1=H1[:, :], op=ALU.add),
                     reads=HK(0) + HK(1), writes=HK(0))
                s.dma("sp", y_d[512 + ch * 128:512 + (ch + 1) * 128, :], H0[:, :], reads=HK(0), group=G("yC"))

        b.release(layer_mark)
        wM = b.alloc("wM", [128, 8, 4096], BF16)
        wB = b.alloc("wB", [128, 8, 1024], BF16)
        wO = b.alloc("wO", [128, 8, 1024], BF16)
        f_mark = b.mark()
        ka_pool = RPool(b, "ka", [128, T], BF16, 2)
        va_pool = RPool(b, "va", [128, 34, 128], BF16, 2)
        q_pool = RPool(b, "qsb", [128, 512], BF16, 3)
        qz_pools = [RPool(b, "qz%d" % g_, [128, 512], BF16, 3) for g_ in range(2)]
        for g_ in range(2):
            for i_, t_ in enumerate(qz_pools[g_].bufs):
                s.op("pool", lambda e: e.memset(t_[:, :], 0.0), writes=[(("qz%d" % g_, i_), "z")])
        qaz = [b.alloc("qaz%d" % hh_, [128, T], BF16) for hh_ in range(2)]
        for hh_ in range(2):
            s.op("pool", lambda e: e.memset(qaz[hh_][:, :], 0.0), writes=[("qaz", hh_, "z")])
        pT_pool = RPool(b, "pT", [128, 512], BF16, 4)
        rd_pool = RPool(b, "rden", [64, 512], F32, 2)
        yo_pool = RPool(b, "yo", [64, 512], F32, 2)
        yoA_pool = RPool(b, "yoA", [64, 512], F32, 2)
        nab_sb = b.alloc("nab_sb", [128, 4, NV, 64], BF16)
        for h in range(4):
            s.dma("pool", nab_sb[:, h, :, :], nab[l, h].rearrange("v p q -> p v q"), writes=["nab_sb"], group=G("nab"))

        if "F" in PH:
            for kc in range(8):
                for c4 in range(4):
                    s.dma("pool", wM[:, kc, c4 * 1024:(c4 + 1) * 1024], w_in[l, kc * 128:(kc + 1) * 128, 2976 + c4 * 1024:2976 + (c4 + 1) * 1024],
                          writes=[("wM", kc)], group=G("wM"))
                s.dma("pool", wB[:, kc, :], w_br[l, kc * 128:(kc + 1) * 128, :], writes=[("wB", kc)], group=G("wB"))
                s.dma("pool", wO[:, kc, :], w_out[l, kc * 128:(kc + 1) * 128, :], writes=[("wO", kc)], group=G("wO"))
        ALLK = list(range(34))
        lat_blocks = [(256 + 512 * j, 512, ALLK) for j in range(8)]
        ctx_blocks = [(0, 256, [0, 1])] if need_ctx else []
        QB = ctx_blocks + lat_blocks

        jobs = []
        if "B" in PH:
            for h in range(4):
                jobs.append(dict(kind="full", k_src=kB_d[h], krows=96, v=4 + h,
                                 runs=[(qB_d[h], 96, 0, 256 + h * 64, None)]))
        if "D" in PH:
            for g_ in range(2):
                jobs.append(dict(kind="full", k_src=kD_d[:, :], krows=128, v=8 + g_,
                                 runs=[(qD_d[h * 64:(h + 1) * 64, :], 64, g_ * 64, 768 + h * 64, qz_pools[g_]) for h in (2 * g_, 2 * g_ + 1)]))
        if "A" in PH:
            for h in range(4):
                jobs.append(dict(kind="na", k_src=kA_d[(h // 2) * 128:(h // 2 + 1) * 128, :], krows=128, v=h, h=h))

        def issue_loads(job):
            ka, kak = ka_pool.next()
            va, vak = va_pool.next()
            s.dma("sp", ka[0:job["krows"], :], job["k_src"], writes=[kak], group=G("ka", kak))
            s.dma("sp", va[:, :, :], vaug_d[job["v"]], writes=[vak], group=G("va", vak))
            job["ka"], job["kak"], job["va"], job["vak"] = ka, kak, va, vak
            if job["kind"] == "na":
                h = job["h"]
                hh = h % 2
                s.dma("sp", qaz[hh][hh * 64:(hh + 1) * 64, :], qA_d[h * 64:(h + 1) * 64, :], reads=[("qaz", hh, "z")], writes=[("qaz", hh)], group=G("qa", hh))

        nrm_pool = RPool(b, "nrm", [128, 512], F32, 2)

        def full_stream(fjobs):
            runs = []
            for job in fjobs:
                for ri_, run in enumerate(job["runs"]):
                    runs.append((job, ri_ == 0) + tuple(run))
            items = [(ri, bi_, n_) for ri in range(len(runs)) for bi_, (_, _, kts) in enumerate(QB) for n_ in range(len(kts))]
            qsl, psos, infl = {}, {}, {}
            LOOK = 3

            def load_q(ri, bi_):
                job, first, q_src, dk, kb, yrow0, qp = runs[ri]
                tq0, nq, _ = QB[bi_]
                qs, qk = (qp or q_pool).next()
                s.dma("sp", qs[kb:kb + dk, 0:nq], q_src[:, tq0:tq0 + nq], reads=[(qk, "z")], writes=[qk], group=G("q", qk))
                qsl[(ri, bi_)] = (qs, qk)

            load_q(0, 0)
            for it in range(len(items) + LOOK):
                if it < len(items):
                    ri, bi_, n_ = items[it]
                    job, first, q_src, dk, kb, yrow0, qp = runs[ri]
                    ka, kak, va, vak = job["ka"], job["kak"], job["va"], job["vak"]
                    k0_, k1_ = (0, 128) if qp is not None else (kb, kb + dk)
                    tq0, nq, kts = QB[bi_]
                    if n_ == 0:
                        psos[(ri, bi_)] = b.ps()
                        if bi_ + 1 < len(QB):
                            load_q(ri, bi_ + 1)
                        elif ri + 1 < len(runs):
                            load_q(ri + 1, 0)
                    qs, qk = qsl[(ri, bi_)]
                    kt = kts[n_]
                    pss, pks = b.ps()
                    while any(pks == pk_ for (_, pk_) in psos.values()):
                        pss, pks = b.ps()
                    s.op("pe", lambda e: e.matmul(pss[:, 0:nq], lhsT=ka[k0_:k1_, kt * 128:(kt + 1) * 128], rhs=qs[k0_:k1_, 0:nq],
                                                  start=True, stop=True), reads=[kak, qk, (qk, "z")], writes=[pks])
                    infl[it] = (pss, pks)
                m_ = it - LOOK
                if m_ >= 0:
                    ri, bi_, n_ = items[m_]
                    job, first, q_src, dk, kb, yrow0, qp = runs[ri]
                    va, vak = job["va"], job["vak"]
                    tq0, nq, kts = QB[bi_]
                    kt = kts[n_]
                    if n_ == 0 and bi_ == 0 and first and job["next"] is not None:
                        issue_loads(job["next"])
                    pss, pks = infl.pop(m_)
                    pso, pko = psos[(ri, bi_)]
                    pT, pTk = pT_pool.next()
                    s.op("act", lambda e: e.activation(out=pT[:, 0:nq], in_=pss[:, 0:nq], func=AF.Exp), reads=[pks], writes=[pTk])
                    s.op("pe", lambda e: e.matmul(pso[:, 0:nq], lhsT=va[:, kt, :], rhs=pT[:, 0:nq],
                                                  start=(n_ == 0), stop=(n_ == len(kts) - 1)), reads=[vak, pTk], writes=[pko])
                    if n_ == len(kts) - 1:
                        nr, nrk = nrm_pool.next()
                        s.op("dve", lambda e: e.tensor_copy(out=nr[:, 0:nq], in_=pso[:, 0:nq]), reads=[pko], writes=[nrk])
                        rd, rdk = rd_pool.next()
                        s.op("dve", lambda e: e.reciprocal(out=rd[0:64, 0:nq], in_=nr[64:128, 0:nq]), reads=[nrk], writes=[rdk])
                        yo, yok = yo_pool.next()
                        s.op("dve", lambda e: e.tensor_tensor(out=yo[0:64, 0:nq], in0=nr[0:64, 0:nq], in1=rd[0:64, 0:nq], op=ALU.mult),
                             reads=[nrk, rdk], writes=[yok])
                        s.dma("sp", y_d[yrow0:yrow0 + 64, tq0:tq0 + nq], yo[0:64, 0:nq], reads=[yok], group=G("yst", yok))
                        del psos[(ri, bi_)]

        def na_attention(job):
            ka, kak, va, vak = job["ka"], job["kak"], job["va"], job["vak"]
            h = job["h"]
            hh = h % 2
            groups = []
            if need_ctx:
                groups.append([("c", j) for j in range(4)])
            for g8 in range(8):
                groups.append([("l", g8 * 8 + j) for j in range(8)])
            flat = []
            for grp_rows in groups:
                for gi_, (kind, r) in enumerate(grp_rows):
                    flat.append((kind, r, gi_, len(grp_rows), grp_rows))
            st1, cur = {}, {}

            def stage1(i):
                kind, r, gi_, ng, grp_rows = flat[i]
                if kind == "c":
                    tq0 = r * 64
                    tiles = [(0, None), (1, None)]
                else:
                    tq0 = CTX + r * 64
                    tiles = [(0, None), (1, None)] + [(2 + g // 2, v) for (g, v) in NA_PLAN[r]]
                pss, pks = b.ps()
                for j, (kt, v) in enumerate(tiles):
                    s.op("pe", lambda e: e.matmul(pss[:, j * 64:(j + 1) * 64], lhsT=ka[:, kt * 128:(kt + 1) * 128],
                                                  rhs=qaz[hh][:, tq0:tq0 + 64], start=True, stop=(v is None)),
                         reads=[kak, ("qaz", hh), ("qaz", hh, "z")], writes=[pks])
                    if v is not None:
                        s.op("pe", lambda e: e.matmul(pss[:, j * 64:(j + 1) * 64], lhsT=ident_bf[:, :], rhs=nab_sb[:, h, v, :],
                                                      start=False, stop=True), reads=["ident_bf", "nab_sb"], writes=[pks])
                st1[i] = (pss, pks, tiles, tq0)

            def stage2(i):
                kind, r, gi_, ng, grp_rows = flat[i]
                pss, pks, tiles, tq0 = st1.pop(i)
                nk = len(tiles)
                if gi_ == 0:
                    cur["yo"] = yoA_pool.next()
                yo, yok = cur["yo"]
                pT, pTk = pT_pool.next()
                s.op("act", lambda e: e.activation(out=pT[:, 0:nk * 64], in_=pss[:, 0:nk * 64], func=AF.Exp), reads=[pks], writes=[pTk])
                pso, pko = b.ps()
                for j, (kt, v) in enumerate(tiles):
                    s.op("pe", lambda e: e.matmul(pso[:, 0:64], lhsT=va[:, kt, :], rhs=pT[:, j * 64:(j + 1) * 64],
                                                  start=(j == 0), stop=(j == nk - 1)), reads=[vak, pTk], writes=[pko])
                rd, rdk = rd_pool.next()
                s.op("dve", lambda e: e.reciprocal(out=rd[0:64, 0:64], in_=pso[64:128, 0:64]), reads=[pko], writes=[rdk])
                c0 = gi_ * 64
                s.op("dve", lambda e: e.tensor_tensor(out=yo[0:64, c0:c0 + 64], in0=pso[0:64, 0:64], in1=rd[0:64, 0:64], op=ALU.mult),
                     reads=[pko, rdk], writes=[(yok, gi_)])
                if gi_ == ng - 1:
                    k0, r0_ = grp_rows[0]
                    grp_t0 = r0_ * 64 if k0 == "c" else CTX + r0_ * 64
                    s.dma("sp", y_d[h * 64:(h + 1) * 64, grp_t0:grp_t0 + ng * 64], yo[0:64, 0:ng * 64],
                          reads=[(yok, n_) for n_ in range(ng)], group=G("yst", yok))

            NLOOK = 3
            for i in range(len(flat) + NLOOK):
                if i < len(flat):
                    stage1(i)
                if i - NLOOK >= 0:
                    stage2(i - NLOOK)

        for ji, job in enumerate(jobs):
            job["next"] = jobs[ji + 1] if ji + 1 < len(jobs) else None
        fjobs = [j_ for j_ in jobs if j_["kind"] == "full"]
        njobs = [j_ for j_ in jobs if j_["kind"] == "na"]
        if jobs:
            issue_loads(jobs[0])
        if fjobs:
            full_stream(fjobs)
        for job in njobs:
            if job["next"] is not None:
                issue_loads(job["next"])
            na_attention(job)

        b.release(f_mark)
        if "F" in PH:
            fxn_pool = RPool(b, "fxn", [128, 8, 512], BF16, 2)
            fy_pool = RPool(b, "fy", [128, 8, 512], F32, 1)
            fz_pool = RPool(b, "fz", [128, 8, 512], BF16, 1)
            fyg_pool = RPool(b, "fyg", [128, 8, 512], BF16, 1)
            acc_pool = RPool(b, "acc", [128, 512], F32, 2)
            sg_pool = RPool(b, "sg", [128, 512], F32, 2)
            tmp_pool = RPool(b, "ftmp", [128, 512], F32, 2)
            accT_pool = RPool(b, "accT", [128, 8, 512], BF16, 2)
            fx_pool = RPool(b, "fx", [128, 1024], F32, 1)
            fv_pool = RPool(b, "fv", [128, 1024], F32, 2)
            fst_pool = RPool(b, "fst", [128, 16], F32, 4)
            fblocks = [bi for bi in range(len(BLOCKS)) if not (bi == 0 and not need_ctx)]
            floaded = {}

            def f_load(bi):
                t0, nt = BLOCKS[bi]
                xn, xnk = fxn_pool.next()
                fy, fyk = fy_pool.next()
                fz, fzk = fz_pool.next()
                s.dma("sp", xn[:, :, 0:nt], xn_d.rearrange("(kc p) t -> p kc t", p=128)[:, :, t0:t0 + nt], writes=[xnk], group=G("fxn"))
                for half in range(2):
                    s.dma("sp", fy[:, half * 4:(half + 1) * 4, 0:nt], y_d.rearrange("(kc p) t -> p kc t", p=128)[:, half * 4:(half + 1) * 4, t0:t0 + nt],
                          writes=[fyk], group=G("fy"))
                s.dma("sp", fz[:, :, 0:nt], zs_d.rearrange("(kc p) t -> p kc t", p=128)[:, :, t0:t0 + nt], writes=[fzk], group=G("fz"))
                floaded[bi] = (xn, xnk, fy, fyk, fz, fzk)

            def make_epilogue(bi, accT, accTk):
                t0, nt = BLOCKS[bi]
                is_ctx = bi == 0
                gj = 1 if is_ctx else 0
                parts = []
                for i in range(nt // 128):
                    r0 = (t0 + i * 128) if is_ctx else (t0 - CTX + i * 128)
                    src = c_src if is_ctx else x_src
                    dst = ctx1_d if is_ctx else x_dst
                    box = {}

                    def part1(i=i, r0=r0, src=src, box=box):
                        fx, fxk = fx_pool.next()
                        s.dma("sp", fx[:, :], src[r0:r0 + 128, :], writes=[fxk], group=G("fx", fxk))
                        fv, fvk = fv_pool.next()
                        for half in range(2):
                            pso, pko = b.ps()
                            hs = slice(half * 512, (half + 1) * 512)
                            for fc in range(8):
                                s.op("pe", lambda e: e.matmul(pso[:, :], lhsT=accT[:, fc, i * 128:(i + 1) * 128], rhs=wO[:, fc, hs],
                                                              start=(fc == 0), stop=(fc == 7)), reads=[("wO", fc), (accTk, fc)], writes=[pko])
                            tm, tmk = tmp_pool.next()
                            s.op("dve", lambda e: e.tensor_tensor(out=tm[:, :], in0=pso[:, :], in1=gate_rep[:, gj, hs], op=ALU.mult),
                                 reads=[pko] + GATEK, writes=[tmk])
                            s.op("dve", lambda e: e.scalar_tensor_tensor(out=fv[:, hs], in0=fx[:, hs], scalar=ALPHA, in1=tm[:, :],
                                                                         op0=ALU.mult, op1=ALU.add), reads=[tmk, fxk], writes=[(fvk, half)])
                        st, stk = fst_pool.next()
                        s.op("dve", lambda e: e.bn_stats(out=st[:, 0:6], in_=fv[:, 0:512]), reads=[(fvk, 0)], writes=[(stk, 0)])
                        s.op("dve", lambda e: e.bn_stats(out=st[:, 6:12], in_=fv[:, 512:1024]), reads=[(fvk, 1)], writes=[(stk, 1)])
                        s.op("dve", lambda e: e.bn_aggr(out=st[:, 12:14], in_=st[:, 0:12].rearrange("p (a b) -> p a b", b=6)),
                             reads=[(stk, 0), (stk, 1)], writes=[(stk, 2)])
                        rstd(st[:, 14:15], st[:, 13:14], [(stk, 2)], (stk, 3))
                        s.op("dve", lambda e: e.tensor_scalar(out=fv[:, :], in0=fv[:, :], scalar1=st[:, 12:13], scalar2=st[:, 14:15],
                                                              op0=ALU.subtract, op1=ALU.mult), reads=[(fvk, 0), (fvk, 1), (stk, 2), (stk, 3)], writes=[(fvk, 0), (fvk, 1)])
                        box["fv"] = (fv, fvk)

                    def part2(r0=r0, dst=dst, box=box):
                        fv, fvk = box["fv"]
                        s.op("dve", lambda e: e.tensor_tensor(out=fv[:, :], in0=fv[:, :], in1=lng[:, :], op=ALU.mult), reads=[(fvk, 0), (fvk, 1), "lng"], writes=[(fvk, 0), (fvk, 1)])
                        s.op("dve", lambda e: e.tensor_tensor(out=fv[:, :], in0=fv[:, :], in1=lnb[:, :], op=ALU.add), reads=[(fvk, 0), (fvk, 1), "lnb"], writes=[(fvk, 0), (fvk, 1)])
                        s.dma("sp", dst[r0:r0 + 128, :], fv[:, :], reads=[(fvk, 0), (fvk, 1)], group=G("outf" if (l == n_layers - 1) else "x1", fvk))

                    parts += [part1, part2]
                return parts

            pending_epi = []
            fready = {}

            def f_yg(bi):
                nt_ = BLOCKS[bi][1]
                xn_, xnk_, fy, fyk, fz, fzk = floaded.pop(bi)
                fyg, fygk = fyg_pool.next()
                for c in range(8):
                    eng = "pool" if c % 2 else "dve"
                    s.op(eng, lambda e: e.tensor_tensor(out=fyg[:, c, 0:nt_], in0=fy[:, c, 0:nt_], in1=fz[:, c, 0:nt_], op=ALU.mult),
                         reads=[fyk, fzk], writes=[(fygk, c)])
                fready[bi] = (xn_, xnk_, fyg, fygk)

            f_load(fblocks[0])
            for fi_, bi in enumerate(fblocks):
                t0, nt = BLOCKS[bi]
                is_ctx = bi == 0
                ntile = nt // 128
                gj = 1 if is_ctx else 0
                if fi_ == 0:
                    f_yg(bi)
                xn, xnk, fyg, fygk = fready.pop(bi)
                accT, accTk = accT_pool.next()
                epi_parts = pending_epi
                pending_epi = []
                for fc in range(8):
                    if fc > 0 and epi_parts:
                        epi_parts.pop(0)()
                        if fc == 7:
                            while epi_parts:
                                epi_parts.pop(0)()
                    if fc == 4 and fi_ + 1 < len(fblocks):
                        f_load(fblocks[fi_ + 1])
                    acc, acck = acc_pool.next()
                    for i in range(4):
                        psm, pkm = b.ps()
                        for kc in range(8):
                            s.op("pe", lambda e, psm=psm, kc=kc, i=i, fc=fc, xn=xn: e.matmul(psm[:, 0:nt], lhsT=wM[:, kc, i * 1024 + fc * 128:i * 1024 + (fc + 1) * 128],
                                                                                            rhs=xn[:, kc, 0:nt], start=(kc == 0), stop=(kc == 7)),
                                 reads=[("wM", kc), xnk], writes=[pkm])
                        sg, sgk = sg_pool.next()
                        s.op("act", lambda e, psm=psm, sg=sg: e.activation(out=sg[:, 0:nt], in_=psm[:, 0:nt], func=AF.Sigmoid), reads=[pkm], writes=[sgk])
                        psb_, pkb = b.ps()
                        for c2 in range(2):
                            s.op("pe", lambda e, psb_=psb_, c2=c2, i=i, fc=fc, fyg=fyg: e.matmul(psb_[:, 0:nt], lhsT=wB[:, i * 2 + c2, fc * 128:(fc + 1) * 128],
                                                                                                rhs=fyg[:, i * 2 + c2, 0:nt], start=(c2 == 0), stop=(c2 == 1)),
                                 reads=[("wB", i * 2 + c2), (fygk, i * 2 + c2)], writes=[pkb])
                        if i == 0:
                            s.op("dve", lambda e, psb_=psb_, sg=sg, acc=acc: e.tensor_tensor(out=acc[:, 0:nt], in0=psb_[:, 0:nt], in1=sg[:, 0:nt], op=ALU.mult),
                                 reads=[pkb, sgk], writes=[acck])
                        else:
                            tm, tmk = tmp_pool.next()
                            s.op("dve", lambda e, psb_=psb_, sg=sg, tm=tm: e.tensor_tensor(out=tm[:, 0:nt], in0=psb_[:, 0:nt], in1=sg[:, 0:nt], op=ALU.mult),
                                 reads=[pkb, sgk], writes=[tmk])
                            if i < 3:
                                s.op("pool", lambda e, tm=tm, acc=acc: e.tensor_tensor(out=acc[:, 0:nt], in0=acc[:, 0:nt], in1=tm[:, 0:nt], op=ALU.add),
                                     reads=[tmk, acck], writes=[acck])
                            else:
                                s.op("pool", lambda e, tm=tm, acc=acc, fc=fc, accT=accT: e.tensor_tensor(out=accT[:, fc, 0:nt], in0=acc[:, 0:nt], in1=tm[:, 0:nt], op=ALU.add),
                                     reads=[tmk, acck], writes=[(accTk, fc)])
                if fi_ + 1 < len(fblocks):
                    f_yg(fblocks[fi_ + 1])
                pending_epi = make_epilogue(bi, accT, accTk)
            for part in pending_epi:
                part()
        s.barrier()

    s.finish(final_groups=[g for g in ("outf:('fv', 0)", "outf:('fv', 1)")])
    return nc, s


def _prep_shared(inp):
    f = np.float32
    L = DEPTH
    p64, p32 = _perm(64, 16), _perm(32, 8)
    w_in = np.ascontiguousarray(inp["w_in"], dtype=f)
    dq = np.concatenate([1440 + h * 64 + p64 for h in range(4)])
    dk = np.concatenate([1696 + h * 64 + p64 for h in range(2)])
    kr = 1152 + p32
    w_insw = np.ascontiguousarray(w_in[:, :, np.concatenate([dq, dk, kr])])
    w_uq = np.ascontiguousarray(inp["mla_w_uq"], dtype=f)
    uq_idx = np.concatenate([np.concatenate([h * 96 + np.arange(64), h * 96 + 64 + p32]) for h in range(4)])
    w_uqsw = np.ascontiguousarray(w_uq[:, :, uq_idx])
    ukv = np.asarray(inp["mla_w_ukv"], dtype=f).reshape(L, 128, 4, 128)
    w_ukv = np.ascontiguousarray(np.concatenate([ukv[:, :, :, :64].reshape(L, 128, 256), ukv[:, :, :, 64:].reshape(L, 128, 256)], -1))
    w_br = np.ascontiguousarray(np.asarray(inp["w_branch"], dtype=f).reshape(L, 1024, 1024))
    lruw = np.zeros((L, 2, 2, 2, 128, 128), f)
    for gi, nm in enumerate(("lru_w_a", "lru_w_x")):
        w = np.asarray(inp[nm], dtype=f)
        for ch in range(2):
            for k2 in range(2):
                lruw[:, :, gi, ch, k2 * 64:(k2 + 1) * 64, k2 * 64:(k2 + 1) * 64] = w[:, :, ch * 2 + k2]
    cols = np.zeros((L, 128, NCOL), f)
    bm = np.asarray(inp["b_mod"], dtype=f)
    cols[:, :, 0:16] = bm[:, 0:2048].reshape(L, 16, 128).transpose(0, 2, 1)
    cols[:, :, 16:18] = np.asarray(inp["mla_q_norm"], f).reshape(L, 2, 128).transpose(0, 2, 1)
    cols[:, :, 18] = np.asarray(inp["mla_kv_norm"], f)
    gq = np.asarray(inp["gqa_q_norm"], f)
    gk = np.asarray(inp["gqa_k_norm"], f)
    cols[:, :, 19] = np.tile(gq, (1, 2))
    cols[:, :, 20] = np.tile(gq[:, p64], (1, 2))
    cols[:, :, 21] = np.tile(gk, (1, 2))
    cols[:, :, 22] = np.tile(gk[:, p64], (1, 2))
    cw = np.asarray(inp["lru_conv_w"], f)
    for ch in range(2):
        for j in range(4):
            cols[:, :, 23 + ch * 4 + j] = cw[:, j, ch * 128:(ch + 1) * 128]
        cols[:, :, 31 + ch] = np.asarray(inp["lru_conv_b"], f)[:, ch * 128:(ch + 1) * 128]
        for d in range(2):
            cols[:, :, 33 + d * 2 + ch] = np.asarray(inp["lru_b_a"], f)[:, d, ch * 128:(ch + 1) * 128]
            cols[:, :, 37 + d * 2 + ch] = np.asarray(inp["lru_b_x"], f)[:, d, ch * 128:(ch + 1) * 128]
            cols[:, :, 41 + d * 2 + ch] = np.asarray(inp["lru_lambda"], f)[:, d, ch * 128:(ch + 1) * 128]
    rep = np.zeros((L, 3, 128, 1024), f)
    rep[:, 0] = bm[:, None, 2048:3072]
    rep[:, 1] = np.asarray(inp["ln_g"], f)[:, None, :]
    rep[:, 2] = np.asarray(inp["ln_b"], f)[:, None, :]
    tabD, tabB, tabK = _rope_tables()
    nab = np.stack([_na_bias_tiles(np.asarray(inp["na_rel_bias"], f)[l]) for l in range(L)])
    cst = np.zeros((4, 128, 128), f)
    cst[0] = np.eye(128, dtype=f)
    cst[1] = 1.0 / 256
    cst[2] = 1.0 / 128
    cst[3, :64, :64] = 1.0 / 64
    cst[3, 64:, 64:] = 1.0 / 64
    return dict(w_mod=np.ascontiguousarray(inp["w_mod"], dtype=f), w_in=w_in, w_insw=w_insw, w_uq=w_uq, w_uqsw=w_uqsw,
                w_ukv=w_ukv, w_br=w_br, w_out=np.ascontiguousarray(inp["w_out"], dtype=f), lruw=lruw, cols=cols, rep=rep,
                tabD=tabD, tabB=tabB, tabK=tabK, nab=nab, cst=cst)


def make_in_maps(inp, n_cores=8):
    shared = _prep_shared(inp)
    x = np.asarray(inp["x"], np.float32)
    ctx = np.asarray(inp["ctx"], np.float32)
    c = np.asarray(inp["c"], np.float32)
    cc = np.asarray(inp["c_ctx"], np.float32)
    maps = []
    for bidx in range(n_cores):
        cv = np.stack([c[bidx].reshape(8, 128).T, cc.reshape(8, 128).T], -1).astype(np.float32)
        m = dict(shared)
        m["x"] = np.ascontiguousarray(x[bidx])
        m["ctx"] = np.ascontiguousarray(ctx[bidx])
        m["cv"] = np.ascontiguousarray(cv)
        maps.append(m)
    return maps


def kernel(**inputs):
    nc, _ = build_program(debug=False)
    maps = make_in_maps(inputs, 8)
    res = run_bass_kernel_spmd(nc, maps, core_ids=list(range(8)))
    return np.stack([np.asarray(r["out"], np.float32) for r in res.results], 0)
```
